# Optimizing a Trainium2 kernel written in Bass

```python
import jax, jax.numpy as jnp
from jax import lax
import numpy as np

D_MODEL = 1024
BATCH = 8
SEQ = 4096
DEPTH = 1

D_MIX = D_MODEL
D_A = D_MIX // 2
D_B = D_MIX - D_A
N_HEADS_A = 4
HEAD_DIM_A = D_A // N_HEADS_A
CHUNK = 128
POOL_WINDOWS = (2, 4, 8, 16)
N_POOL_GROUPS = len(POOL_WINDOWS)
POOL_GROUP_DIM = D_B // N_POOL_GROUPS
D_FF = 4 * D_MODEL
N_MOD = 6
EPS = 1e-6

kernel_name = "hybrid_gmlp_pool_sqrelu_block"


def rms_norm(x, g):
    xf = x.astype(jnp.float32)
    y = xf * lax.rsqrt(jnp.mean(xf * xf, axis=-1, keepdims=True) + EPS)
    return (y * g.astype(jnp.float32)).astype(x.dtype)


def layer_norm(x, g, b):
    xf = x.astype(jnp.float32)
    mu = jnp.mean(xf, axis=-1, keepdims=True)
    var = jnp.mean(jnp.square(xf - mu), axis=-1, keepdims=True)
    y = (xf - mu) * lax.rsqrt(var + EPS)
    return (y * g.astype(jnp.float32) + b.astype(jnp.float32)).astype(x.dtype)


def spatial_gating(z_a, w_spatial, b_spatial, ln_v_gain, ln_v_bias):
    b, s, _ = z_a.shape
    z_a = jax.nn.gelu(z_a)
    u, v = z_a[..., :D_A], z_a[..., D_A:]
    v = layer_norm(v, ln_v_gain, ln_v_bias)
    v = v.reshape(b, s // CHUNK, CHUNK, N_HEADS_A, HEAD_DIM_A)
    mask = jnp.tril(jnp.ones((CHUNK, CHUNK), dtype=w_spatial.dtype))
    w_causal = w_spatial * mask[None]
    mixed = jnp.einsum("hts,bnshd->bnthd", w_causal, v)
    mixed = mixed + b_spatial.T[:, :, None]
    return u * mixed.reshape(b, s, D_A)


def multiscale_pool(z_b, w_pool, b_pool, pool_scale):
    b, s, _ = z_b.shape
    zg = z_b.reshape(b, s, N_POOL_GROUPS, POOL_GROUP_DIM)
    zf = zg.astype(jnp.float32)
    cs = jnp.concatenate(
        [jnp.zeros((b, 1, N_POOL_GROUPS, POOL_GROUP_DIM), jnp.float32), jnp.cumsum(zf, axis=1)],
        axis=1)
    pos = jnp.arange(s, dtype=jnp.float32)
    pooled = []
    for g, w in enumerate(POOL_WINDOWS):
        csg = cs[:, :, g]
        lower = jnp.concatenate(
            [jnp.zeros((b, w - 1, POOL_GROUP_DIM), jnp.float32), csg[:, : s + 1 - w]], axis=1)
        count = jnp.minimum(pos + 1.0, float(w))[None, :, None]
        pooled.append((csg[:, 1:] - lower) / count)
    pooled = jnp.stack(pooled, axis=2)
    diff = (pooled - zf).astype(z_b.dtype)
    y = jnp.einsum("bsgc,gcd->bsgd", diff, w_pool) + b_pool
    return y.reshape(b, s, D_B) * pool_scale


def setup_inputs(seed: int = 0) -> dict:
    key = jax.random.key(seed)
    ks = jax.random.split(key, 20)
    f32 = jnp.float32
    nrm = lambda k, shape, scale: jax.random.normal(k, shape, f32) * scale
    return {
        "x": nrm(ks[0], (BATCH, SEQ, D_MODEL), 1.0),
        "c": nrm(ks[1], (BATCH, D_MODEL), 1.0),
        "w_ada": nrm(ks[2], (D_MODEL, N_MOD * D_MODEL), 0.5 * D_MODEL ** -0.5),
        "b_ada": nrm(ks[3], (N_MOD * D_MODEL,), 0.01),
        "norm1_pre": 1.0 + nrm(ks[4], (D_MODEL,), 0.02),
        "norm1_post": 1.0 + nrm(ks[5], (D_MODEL,), 0.02),
        "w_in": nrm(ks[6], (D_MODEL, 2 * D_A + D_B), D_MODEL ** -0.5),
        "w_spatial": nrm(ks[7], (N_HEADS_A, CHUNK, CHUNK), 0.5 * CHUNK ** -0.5),
        "b_spatial": 1.0 + nrm(ks[8], (N_HEADS_A, CHUNK), 0.02),
        "ln_v_gain": 1.0 + nrm(ks[9], (D_A,), 0.02),
        "ln_v_bias": nrm(ks[10], (D_A,), 0.02),
        "w_pool": nrm(ks[11], (N_POOL_GROUPS, POOL_GROUP_DIM, POOL_GROUP_DIM), POOL_GROUP_DIM ** -0.5),
        "b_pool": nrm(ks[12], (N_POOL_GROUPS, POOL_GROUP_DIM), 0.02),
        "pool_scale": 1.0 + nrm(ks[13], (D_B,), 0.02),
        "w_out": nrm(ks[14], (D_MIX, D_MODEL), D_MIX ** -0.5),
        "norm2_pre": 1.0 + nrm(ks[15], (D_MODEL,), 0.02),
        "norm2_post": 1.0 + nrm(ks[16], (D_MODEL,), 0.02),
        "w_fc1": nrm(ks[17], (DEPTH, D_MODEL, D_FF), D_MODEL ** -0.5)[0],
        "w_fc2": nrm(ks[18], (D_FF, D_MODEL), D_FF ** -0.5),
    }


def reference(x, c, w_ada, b_ada, norm1_pre, norm1_post, w_in, w_spatial, b_spatial,
              ln_v_gain, ln_v_bias, w_pool, b_pool, pool_scale, w_out,
              norm2_pre, norm2_post, w_fc1, w_fc2):
    mod = jax.nn.silu(c) @ w_ada + b_ada
    shift1, scale1, gate1, shift2, scale2, gate2 = [
        m[:, None, :] for m in jnp.split(mod, N_MOD, axis=-1)]

    for _ in range(DEPTH):
        h = rms_norm(x, norm1_pre) * (1.0 + scale1) + shift1
        z = h @ w_in
        y_a = spatial_gating(z[..., : 2 * D_A], w_spatial, b_spatial, ln_v_gain, ln_v_bias)
        y_b = multiscale_pool(z[..., 2 * D_A:], w_pool, b_pool, pool_scale)
        mix = jnp.concatenate([y_a, y_b], axis=-1) @ w_out
        x = x + gate1 * rms_norm(mix, norm1_post)

        h = rms_norm(x, norm2_pre) * (1.0 + scale2) + shift2
        f = jnp.square(jax.nn.relu(h @ w_fc1)) @ w_fc2
        x = x + gate2 * rms_norm(f, norm2_post)
    return x
```

```python
import numpy as np
import concourse.bass as bass
import concourse.mybir as mybir
from concourse.bass_utils import run_bass_kernel_spmd

F32 = mybir.dt.float32
BF16 = mybir.dt.bfloat16
I32 = mybir.dt.int32
AF = mybir.ActivationFunctionType
ALU = mybir.AluOpType

D = 1024
SEQ = 4096
T = 256
NJ = 32
EPS = 1e-6
PE, ACT, DVE, POOL, SP = "pe", "act", "dve", "pool", "sp"
ENGS = (PE, ACT, DVE, POOL, SP)

C_C, C_N1, C_N2, C_LG, C_LB, C_BP, C_PS, C_RC = 0, 8, 16, 24, 28, 32, 36, 40
NCV = 40 + 64


class Buf:
    __slots__ = ("name", "w", "r")

    def __init__(self, name):
        self.name = name
        self.w = None
        self.r = []


class Sched:
    def __init__(self):
        self.ops = {e: [] for e in ENGS}
        self.count = {}

    def _deps(self, eng, reads, writes):
        deps = []
        for b in reads:
            if b.w is not None:
                deps.append((b.w, "raw"))
        for b in writes:
            if b.w is not None:
                deps.append((b.w, "waw"))
            for r in b.r:
                deps.append((r, "war"))
        out = []
        for (ev, kind) in deps:
            if ev[0] == eng:
                if eng == PE:
                    continue
            out.append(ev)
        return out

    def _record(self, eng, fn, key, amt, reads, writes):
        deps = self._deps(eng, reads, writes)
        self.count[key] = self.count.get(key, 0) + amt
        ev = (key, self.count[key])
        self.ops[eng].append((deps, fn, key, amt))
        for b in reads:
            b.r.append(ev)
        for b in writes:
            b.w = ev
            b.r = []
        return ev

    def op(self, eng, fn, reads=(), writes=()):
        return self._record(eng, fn, eng, 1, reads, writes)

    def dma(self, queue, fn, key, reads=(), writes=()):
        return self._record(queue, fn, key, 16, reads, writes)

    def wait(self, eng, events):
        self.ops[eng].append((list(events), None, None, 0))


def build(NT=16):
    S = NT * T
    nc = bass.Bass("TRN2", target_bir_lowering=False)
    dt_in = lambda name, shape: nc.dram_tensor(name, shape, F32, kind="ExternalInput").ap()
    x_d = dt_in("x", [S, D])
    cvec_d = dt_in("cvec", [128, NCV])
    n1pb_d = dt_in("n1pb", [128, D])
    n2pb_d = dt_in("n2pb", [128, D])
    bspb_d = dt_in("bspb", [128, 512])
    mask_d = dt_in("mask", [128, 512])
    wspT_d = dt_in("wspT", [128, 512])
    ident_d = dt_in("ident", [128, 128])
    wpool_d = dt_in("wpool", [128, 512])
    bada_d = dt_in("bada", [1, 6144])
    wada_d = dt_in("wada", [128, 8 * 6144])
    win_d = dt_in("w_in", [128, 8 * 1536])
    wout_d = dt_in("w_out", [128, 8 * 1024])
    fc1_d = dt_in("w_fc1", [128, 8 * 4096])
    fc2_d = dt_in("w_fc2", [4096, D])
    out_d = nc.dram_tensor("out", [S, D], F32, kind="ExternalOutput").ap()
    fc2s_d = nc.dram_tensor("fc2s", [4096, D], BF16, kind="Internal").ap()

    sc = Sched()
    sems = {}

    import contextlib
    with contextlib.ExitStack() as _st:
        w_in_sb = _st.enter_context(nc.sbuf_tensor("w_in_sb", [128, 8, 1536], BF16))
        w_out_sb = _st.enter_context(nc.sbuf_tensor("w_out_sb", [128, 8, 1024], BF16))
        fc1_sb = _st.enter_context(nc.sbuf_tensor("fc1_sb", [128, 8, 4096], BF16))
        fc2buf = _st.enter_context(nc.sbuf_tensor("fc2buf", [128, 3, 2, 1024], BF16))
        wsp_sb = _st.enter_context(nc.sbuf_tensor("wsp_sb", [128, 4, 128], BF16))
        wpool_sb = _st.enter_context(nc.sbuf_tensor("wpool_sb", [128, 4, 128], BF16))
        ident = _st.enter_context(nc.sbuf_tensor("ident_sb", [128, 128], BF16))
        ones = _st.enter_context(nc.sbuf_tensor("ones_sb", [128, 128], F32))
        gp1 = _st.enter_context(nc.sbuf_tensor("gp1", [128, D], F32))
        gp2 = _st.enter_context(nc.sbuf_tensor("gp2", [128, D], F32))
        Bt = _st.enter_context(nc.sbuf_tensor("Bt", [128, 4, 128], F32))
        cv = _st.enter_context(nc.sbuf_tensor("cv", [128, NCV], F32))
        mc = _st.enter_context(nc.sbuf_tensor("mc", [128, 64], F32))
        sm = _st.enter_context(nc.sbuf_tensor("sm", [128, 96], F32))
        xb = _st.enter_context(nc.sbuf_tensor("xb", [128, 3, 2048], F32))
        hbf = _st.enter_context(nc.sbuf_tensor("hbf", [128, 2, 1024], BF16))
        hP = _st.enter_context(nc.sbuf_tensor("hP", [128, 2, 8, T], BF16))
        junk = _st.enter_context(nc.sbuf_tensor("junk", [128, 1024], BF16))
        ubf = _st.enter_context(nc.sbuf_tensor("ubf", [128, 4, T], BF16))
        vg = _st.enter_context(nc.sbuf_tensor("vg", [128, 2, 512], F32))
        vn = _st.enter_context(nc.sbuf_tensor("vn", [128, 2, 512], BF16))
        Z = _st.enter_context(nc.sbuf_tensor("Z", [128, 4, 272], F32))
        pt = _st.enter_context(nc.sbuf_tensor("pt", [128, 2, 272], F32))
        diff = _st.enter_context(nc.sbuf_tensor("diff", [128, 4, T], BF16))
        tmpS = _st.enter_context(nc.sbuf_tensor("tmpS", [128, 4, 128], F32))
        yT = _st.enter_context(nc.sbuf_tensor("yT", [128, 8, T], BF16))
        tmp = _st.enter_context(nc.sbuf_tensor("tmp", [128, 2, 512], F32))
        rl = _st.enter_context(nc.sbuf_tensor("rl", [128, 2, 512], F32))
        hid = _st.enter_context(nc.sbuf_tensor("hid", [128, 3, 512], BF16))
        ps = _st.enter_context(nc.psum_tensor("ps", [128, 8, 512], F32))
        B = {}

        def buf(name):
            if name not in B:
                B[name] = Buf(name)
            return B[name]

        def bank(b):
            return ps[:, b, :]

        def bank_bf(b):
            return ps[:, b, :].bitcast(BF16)

        def Bps(b):
            return buf("ps%d" % b)

        def stg4(ti):
            return tmp[:, ti, :] if ti < 2 else vg[:, ti - 2, :]

        def stg4_buf(ti):
            return buf("tmp%d" % ti) if ti < 2 else buf("vg%d" % (ti - 2))

        def xs(slot, s):
            return xb[:, slot, s * 1024:(s + 1) * 1024]

        sm_next = [0]

        def smcol(n):
            a = sm_next[0]
            sm_next[0] += n
            assert sm_next[0] <= 96
            return a

        def rstd_chain(src_ap, src_bufs, n, inv_d, tag):
            c0 = smcol(4 * n)
            vv = sm[:, c0:c0 + n]
            r = sm[:, c0 + n:c0 + 2 * n]
            t = sm[:, c0 + 2 * n:c0 + 3 * n]
            u = sm[:, c0 + 3 * n:c0 + 4 * n]
            bv, br, bt_, bu = (buf(tag + "_vv"), buf(tag + "_r"), buf(tag + "_t"), buf(tag + "_u"))

            def chain():
                sc.op(DVE, lambda e: e.tensor_scalar(out=vv, in0=src_ap, scalar1=inv_d, scalar2=EPS,
                                                     op0=ALU.mult, op1=ALU.add),
                      reads=src_bufs, writes=[bv])
                sc.op(DVE, lambda e: e.tensor_scalar(out=r.bitcast(I32), in0=vv.bitcast(I32), scalar1=-0.5,
                                                     scalar2=1597463007.0, op0=ALU.mult, op1=ALU.add),
                      reads=[bv], writes=[br])
                for _ in range(3):
                    sc.op(DVE, lambda e: e.tensor_tensor(out=t, in0=r, in1=r, op=ALU.mult),
                          reads=[br], writes=[bt_])
                    sc.op(DVE, lambda e: e.scalar_tensor_tensor(out=u, in0=t, scalar=-0.5, in1=vv,
                                                                op0=ALU.mult, op1=ALU.mult),
                          reads=[bt_, bv], writes=[bu])
                    sc.op(DVE, lambda e: e.scalar_tensor_tensor(out=r, in0=u, scalar=1.5, in1=r,
                                                                op0=ALU.add, op1=ALU.mult),
                          reads=[bu, br], writes=[br])
            return chain, r, br

        sc.op(POOL, lambda e: e.memset(ones[:], 1.0), writes=[buf("ones")])
        sc.op(POOL, lambda e: e.memset(Z[:, :, 0:16], 0.0), writes=[buf("Z")])

        sc.dma(SP, lambda e: e.dma_start(out=cv[:], in_=cvec_d[:, :]), "c_cv", writes=[buf("cv")])
        sc.dma(SP, lambda e: e.dma_start(out=gp1[:], in_=n1pb_d[:, :]), "c_g1", writes=[buf("gp1")])
        sc.dma(SP, lambda e: e.dma_start(out=gp2[:], in_=n2pb_d[:, :]), "c_g2", writes=[buf("gp2")])
        sc.dma(SP, lambda e: e.dma_start(out=Bt[:].rearrange("p h t -> p (h t)"), in_=bspb_d[:, :]), "c_bt",
               writes=[buf("Bt")])
        sc.dma(SP, lambda e: e.dma_start(out=vg[:, 0, :], in_=wspT_d[:, :]), "c_ws", writes=[buf("vg0")])
        sc.dma(SP, lambda e: e.dma_start(out=vg[:, 1, :], in_=mask_d[:, :]), "c_mk", writes=[buf("vg1")])
        sc.dma(POOL, lambda e: e.dma_start(out=ident[:], in_=ident_d[:, :]), "c_id", writes=[buf("ident")])
        sc.dma(POOL, lambda e: e.dma_start(out=wpool_sb[:].rearrange("p g d -> p (g d)"), in_=wpool_d[:, :]),
               "c_wp", writes=[buf("wpool")])
        def x_load(i):
            slot = i % 3
            src = x_d[i * T:(i + 1) * T, :].rearrange("(s p) d -> p s d", p=128)
            dst = xb[:, slot, :].rearrange("p (s d) -> p s d", s=2)
            sc.dma(SP, lambda e: e.dma_start(out=dst, in_=src), "xl%d" % slot,
                   writes=[buf("xb%d_0" % slot), buf("xb%d_1" % slot)])
        x_load(0)
        win_v = win_d.rearrange("p (k n) -> p k n", k=8)
        for hh in range(2):
            sc.dma(POOL, lambda e, hh=hh: e.dma_start(out=w_in_sb[:, hh * 4:(hh + 1) * 4, :],
                                                      in_=win_v[:, hh * 4:(hh + 1) * 4, :]),
                   "w_in%d" % hh, writes=[buf("w_in%d" % hh)])
        W_IN = [buf("w_in0"), buf("w_in1")]
        sc.dma(POOL, lambda e: e.dma_start(out=w_out_sb[:], in_=wout_d.rearrange("p (k n) -> p k n", k=8)),
               "w_out", writes=[buf("w_out")])
        fc1_v = fc1_d.rearrange("p (k n) -> p k n", k=8)
        for q in range(4):
            sc.dma(POOL, lambda e, q=q: e.dma_start(out=fc1_sb[:, 2 * q:2 * q + 2, :], in_=fc1_v[:, 2 * q:2 * q + 2, :]),
                   "fc1_%d" % q, writes=[buf("fc1_%d" % q)])
        FC1 = [buf("fc1_%d" % q) for q in range(4)]

        c_th = smcol(8); c_hf = smcol(8); c_sc = smcol(8)
        sc.op(ACT, lambda e: e.activation(out=sm[:, c_th:c_th + 8], in_=cv[:, C_C:C_C + 8], func=AF.Tanh, scale=0.5),
              reads=[buf("cv")], writes=[buf("s_th")])
        sc.op(DVE, lambda e: e.tensor_scalar(out=sm[:, c_hf:c_hf + 8], in0=sm[:, c_th:c_th + 8], scalar1=1.0,
                                             scalar2=0.5, op0=ALU.add, op1=ALU.mult),
              reads=[buf("s_th")], writes=[buf("s_hf")])
        sc.op(DVE, lambda e: e.tensor_tensor(out=sm[:, c_sc:c_sc + 8], in0=sm[:, c_hf:c_hf + 8],
                                             in1=cv[:, C_C:C_C + 8], op=ALU.mult),
              reads=[buf("s_hf"), buf("cv")], writes=[buf("s_sc")])
        scv = sm[:, c_sc:c_sc + 8]

        wada_v = wada_d.rearrange("p (k n) -> p k n", k=8)
        CH_COL = {0: 0, 1: 8, 3: 16, 4: 24}
        for b in range(24):
            st = 1 + b % 2
            stg = xb[:, st, :].rearrange("p (k c) -> p k c", c=256)
            stg_bufs = [buf("xb%d_0" % st), buf("xb%d_1" % st)]
            sc.dma(SP, lambda e, b=b, stg=stg: e.dma_start(out=stg, in_=wada_v[:, :, b * 256:(b + 1) * 256]),
                   "wa%d" % (b % 2), writes=stg_bufs)
            sc.dma(SP, lambda e, b=b: e.dma_start(out=rl[0:1, b % 2, 0:256], in_=bada_d[0:1, b * 256:(b + 1) * 256]),
                   "ba%d" % (b % 2), writes=[buf("rl%d" % (b % 2))])
            pr = 4 + b % 2

            def mm_mod(e, b=b, stg=stg, pr=pr):
                for k in range(8):
                    e.matmul(ps[0:1, pr, 0:256], lhsT=scv[:, k:k + 1], rhs=stg[:, k, :], start=(k == 0), stop=False)
                return e.matmul(ps[0:1, pr, 0:256], lhsT=ones[0:1, 0:1], rhs=rl[0:1, b % 2, 0:256], start=False, stop=True)
            sc.op(PE, mm_mod, reads=stg_bufs + [buf("s_sc"), buf("ones"), buf("rl%d" % (b % 2))], writes=[Bps(pr)])
            sc.op(DVE, lambda e, b=b, pr=pr: e.tensor_copy(out=tmp[0:1, b % 2, 0:256], in_=ps[0:1, pr, 0:256]),
                  reads=[Bps(pr)], writes=[buf("tmp%d" % (b % 2))])
            chunk = b // 4
            if chunk in (2, 5):
                gp = gp1 if chunk == 2 else gp2
                gpb = buf("gp1") if chunk == 2 else buf("gp2")
                pb = 6 + b % 2
                cols = slice((b % 4) * 256, (b % 4) * 256 + 256)
                sc.op(PE, lambda e, b=b, pb=pb: e.matmul(ps[:, pb, 0:256], lhsT=ones[0:1, :], rhs=tmp[0:1, b % 2, 0:256],
                                                        start=True, stop=True),
                      reads=[buf("tmp%d" % (b % 2)), buf("ones")], writes=[Bps(pb)])
                sc.op(DVE, lambda e, gp=gp, pb=pb, cols=cols: e.tensor_tensor(out=gp[:, cols], in0=ps[:, pb, 0:256],
                                                                          in1=gp[:, cols], op=ALU.mult),
                      reads=[Bps(pb), gpb], writes=[gpb])
            else:
                col0 = CH_COL[chunk] + (b % 4) * 2

                def mm_col(e, b=b, col0=col0):
                    ins = None
                    for q in range(2):
                        ins = e.matmul(ps[:, 0, col0 + q:col0 + q + 1], lhsT=tmp[0:1, b % 2, q * 128:(q + 1) * 128],
                                       rhs=ones[0:1, 0:1], start=True, stop=True)
                    return ins
                sc.op(PE, mm_col, reads=[buf("tmp%d" % (b % 2)), buf("ones")], writes=[Bps(0)])
        sc.op(DVE, lambda e: e.tensor_copy(out=mc[:, 0:32], in_=ps[:, 0, 0:32]), reads=[Bps(0)], writes=[buf("mc_raw")])
        sc.op(DVE, lambda e: e.scalar_tensor_tensor(out=mc[:, 32:40], in0=mc[:, 8:16], scalar=1.0,
                                                    in1=cv[:, C_N1:C_N1 + 8], op0=ALU.add, op1=ALU.mult),
              reads=[buf("mc_raw"), buf("cv")], writes=[buf("mc_g1")])
        sc.op(DVE, lambda e: e.scalar_tensor_tensor(out=mc[:, 40:48], in0=mc[:, 24:32], scalar=1.0,
                                                    in1=cv[:, C_N2:C_N2 + 8], op0=ALU.add, op1=ALU.mult),
              reads=[buf("mc_raw"), buf("cv")], writes=[buf("mc_g2")])
        MODB = [buf("mc_raw"), buf("mc_g1"), buf("mc_g2")]
        G1, SH1, G2, SH2 = 32, 0, 40, 16

        sc.op(DVE, lambda e: e.tensor_tensor(out=vg[:, 0, :], in0=vg[:, 0, :], in1=vg[:, 1, :], op=ALU.mult),
              reads=[buf("vg0"), buf("vg1")], writes=[buf("vg0")])
        sc.op(ACT, lambda e: e.activation(out=wsp_sb[:].rearrange("p h t -> p (h t)"), in_=vg[:, 0, :], func=AF.Identity),
              reads=[buf("vg0")], writes=[buf("wsp")])
        sc.op(PE, lambda e: e.matmul(ps[:, 1, :], lhsT=ones[:, :], rhs=vg[:, 0, :], start=True, stop=True),
              reads=[buf("vg0"), buf("ones")], writes=[Bps(1)])

        def bt_fix(e):
            ins = None
            for h in range(4):
                ins = e.scalar_tensor_tensor(out=Bt[:, h, :], in0=ps[:, 1, h * 128:(h + 1) * 128],
                                             scalar=cv[:, C_LB + h:C_LB + h + 1], in1=Bt[:, h, :],
                                             op0=ALU.mult, op1=ALU.add)
            return ins
        sc.op(DVE, bt_fix, reads=[Bps(1), buf("cv"), buf("Bt")], writes=[buf("Bt")])

        c_ss1 = smcol(2)
        ch1, r1, br1 = rstd_chain(sm[:, c_ss1:c_ss1 + 2], [buf("ss1_0"), buf("ss1_1")], 2, 1.0 / D, "r1")
        c_vs = smcol(2); c_vq = smcol(2); c_mean = smcol(2); c_msq = smcol(2); c_var = smcol(2)
        chv, rv, brv = rstd_chain(sm[:, c_var:c_var + 2], [buf("var")], 2, 1.0, "rv")
        c_ssm = smcol(2); c_ssms = smcol(1)
        chm, rm, brm = rstd_chain(sm[:, c_ssms:c_ssms + 1], [buf("ssms")], 1, 1.0 / D, "rm")
        c_ss2 = smcol(2)
        ch2, r2, br2 = rstd_chain(sm[:, c_ss2:c_ss2 + 2], [buf("ss2_0"), buf("ss2_1")], 2, 1.0 / D, "r2")
        c_ssf = smcol(4); c_ssfs = smcol(2)
        chf, rf, brf = rstd_chain(sm[:, c_ssfs:c_ssfs + 2], [buf("ssfs")], 2, 1.0 / D, "rf")

        MB = (6, 7)
        FB = (4, 5)

        def norm_A(slot, ss_col, ss_name, chain, r_ap, r_buf):
            for s in range(2):
                sc.op(ACT, lambda e, s=s: e.activation(out=junk[:], in_=xs(slot, s), func=AF.Square,
                                                       accum_out=sm[:, ss_col + s:ss_col + s + 1]),
                      reads=[buf("xb%d_%d" % (slot, s))], writes=[buf("junk"), buf("%s_%d" % (ss_name, s))])
            chain()
            for s in range(2):
                sc.op(ACT, lambda e, s=s: e.activation(out=hbf[:, s, :], in_=xs(slot, s), func=AF.Identity,
                                                       scale=r_ap[:, s:s + 1]),
                      reads=[buf("xb%d_%d" % (slot, s)), r_buf], writes=[buf("hbf_%d" % s)])

        def norm_B(dstT, dstT_buf, gcol, shcol):
            for s in range(2):
                mb = MB[s]

                def tr(e, s=s, mb=mb):
                    ins = None
                    for k in range(8):
                        ins = e.transpose(out=bank_bf(mb)[:, k * 128:(k + 1) * 128],
                                          in_=hbf[:, s, k * 128:(k + 1) * 128], identity=ident[:])
                    return ins
                sc.op(PE, tr, reads=[buf("hbf_%d" % s), buf("ident")], writes=[Bps(mb)])

                if s == 0:
                    def ev(e, s=s, mb=mb):
                        ins = None
                        for k in range(8):
                            ins = e.activation(out=dstT[:, k, s * 128:(s + 1) * 128],
                                               in_=bank_bf(mb)[:, k * 128:(k + 1) * 128], func=AF.Identity,
                                               scale=mc[:, gcol + k:gcol + k + 1], bias=mc[:, shcol + k:shcol + k + 1])
                        return ins
                    sc.op(ACT, ev, reads=[Bps(mb)] + MODB, writes=[buf(dstT_buf + "_%d" % s)])
                else:
                    def ev(e, s=s, mb=mb):
                        ins = None
                        for k in range(8):
                            ins = e.tensor_scalar(out=dstT[:, k, s * 128:(s + 1) * 128],
                                                  in0=bank_bf(mb)[:, k * 128:(k + 1) * 128],
                                                  scalar1=mc[:, gcol + k:gcol + k + 1],
                                                  scalar2=mc[:, shcol + k:shcol + k + 1], op0=ALU.mult, op1=ALU.add)
                        return ins
                    sc.op(DVE, ev, reads=[Bps(mb)] + MODB, writes=[buf(dstT_buf + "_%d" % s)])

        def mixer(i):
            slot = i % 3
            XB = [buf("xb%d_0" % slot), buf("xb%d_1" % slot)]
            hT = hP[:, i % 2]
            HT = [buf("hP%d_0" % (i % 2)), buf("hP%d_1" % (i % 2))]
            norm_A(slot, c_ss1, "ss1", ch1, r1, br1)
            yield
            yield
            yield
            norm_B(hT, "hP%d" % (i % 2), G1, SH1)
            yield
            if i > 0:
                sc.op(DVE, lambda e: e.tensor_copy(out=Z[:, :, 0:16], in_=Z[:, :, 256:272]),
                      reads=[buf("Z")], writes=[buf("Z")])
            for gp_ in range(2):
                mb = MB[gp_]

                def mm_z(e, gp_=gp_, mb=mb):
                    ins = None
                    for gg in range(2):
                        g = gp_ * 2 + gg
                        for k in range(8):
                            ins = e.matmul(ps[:, mb, gg * T:(gg + 1) * T],
                                           lhsT=w_in_sb[:, k, 1024 + g * 128:1024 + (g + 1) * 128],
                                           rhs=hT[:, k, :], start=(k == 0), stop=(k == 7))
                    return ins
                sc.op(PE, mm_z, reads=HT + W_IN, writes=[Bps(mb)])
                sc.op(ACT, lambda e, gp_=gp_, mb=mb: e.activation(
                    out=Z[:, 2 * gp_:2 * gp_ + 2, 16:272], in_=ps[:, mb, :].rearrange("p (g t) -> p g t", g=2),
                    func=AF.Identity), reads=[Bps(mb)], writes=[buf("Z")])
            yield
            for cp in range(2):
                mb = MB[cp]

                def mm_u(e, cp=cp, mb=mb):
                    ins = None
                    for cc in range(2):
                        c = cp * 2 + cc
                        for k in range(8):
                            ins = e.matmul(ps[:, mb, cc * T:(cc + 1) * T], lhsT=w_in_sb[:, k, c * 128:(c + 1) * 128],
                                           rhs=hT[:, k, :], start=(k == 0), stop=(k == 7))
                    return ins
                sc.op(PE, mm_u, reads=HT + W_IN, writes=[Bps(mb)])
                sc.op(ACT, lambda e, cp=cp, mb=mb: e.activation(
                    out=ubf[:, 2 * cp:2 * cp + 2, :].rearrange("p c t -> p (c t)"), in_=ps[:, mb, :],
                    func=AF.Gelu_apprx_tanh), reads=[Bps(mb)], writes=[buf("ubf%d" % cp)])
            for s in range(2):
                mb = MB[s]

                def mm_v(e, s=s, mb=mb):
                    ins = None
                    for k in range(8):
                        ins = e.matmul(ps[:, mb, :], lhsT=hT[:, k, s * 128:(s + 1) * 128], rhs=w_in_sb[:, k, 512:1024],
                                       start=(k == 0), stop=(k == 7))
                    return ins
                sc.op(PE, mm_v, reads=HT + W_IN, writes=[Bps(mb)])
                sc.op(ACT, lambda e, s=s, mb=mb: e.activation(out=vg[:, s, :], in_=ps[:, mb, :], func=AF.Gelu_apprx_tanh,
                                                              accum_out=sm[:, c_vs + s:c_vs + s + 1]),
                      reads=[Bps(mb)], writes=[buf("vg%d" % s), buf("vs%d" % s)])
                sc.op(ACT, lambda e, s=s: e.activation(out=junk[:, 0:512], in_=vg[:, s, :], func=AF.Square,
                                                       accum_out=sm[:, c_vq + s:c_vq + s + 1]),
                      reads=[buf("vg%d" % s)], writes=[buf("junk"), buf("vq%d" % s)])
            sc.op(DVE, lambda e: e.tensor_scalar(out=sm[:, c_mean:c_mean + 2], in0=sm[:, c_vs:c_vs + 2],
                                                 scalar1=1.0 / 512, scalar2=None, op0=ALU.mult),
                  reads=[buf("vs0"), buf("vs1")], writes=[buf("mean")])
            sc.op(DVE, lambda e: e.tensor_tensor(out=sm[:, c_msq:c_msq + 2], in0=sm[:, c_mean:c_mean + 2],
                                                 in1=sm[:, c_mean:c_mean + 2], op=ALU.mult),
                  reads=[buf("mean")], writes=[buf("msq")])
            sc.op(DVE, lambda e: e.scalar_tensor_tensor(out=sm[:, c_var:c_var + 2], in0=sm[:, c_vq:c_vq + 2],
                                                        scalar=1.0 / 512, in1=sm[:, c_msq:c_msq + 2],
                                                        op0=ALU.mult, op1=ALU.subtract),
                  reads=[buf("vq0"), buf("vq1"), buf("msq")], writes=[buf("var")])
            chv()
            for s in range(2):
                sc.op(DVE, lambda e, s=s: e.tensor_scalar(out=vn[:, s, :], in0=vg[:, s, :],
                                                           scalar1=sm[:, c_mean + s:c_mean + s + 1],
                                                           scalar2=rv[:, s:s + 1], op0=ALU.subtract, op1=ALU.mult),
                      reads=[buf("vg%d" % s), buf("mean"), brv], writes=[buf("vn%d" % s)])
            def pooling(g):
                m = g + 1
                w = 1 << m
                src = Z[:, g, :]
                src_b = buf("Z")
                for k in range(m):
                    lo = (1 << (k + 1)) - 1
                    sh = 1 << k
                    dst = pt[:, k % 2, :]
                    dst_b = buf("pt%d" % (k % 2))
                    sc.op(DVE, lambda e, src=src, dst=dst, lo=lo, sh=sh: e.tensor_tensor(
                        out=dst[:, lo:272], in0=src[:, lo:272], in1=src[:, lo - sh:272 - sh], op=ALU.add),
                        reads=[src_b], writes=[dst_b])
                    src, src_b = dst, dst_b
                sc.op(DVE, lambda e, src=src, g=g, w=w: e.scalar_tensor_tensor(
                    out=diff[:, g, :], in0=src[:, 16:272], scalar=1.0 / w, in1=Z[:, g, 16:272],
                    op0=ALU.mult, op1=ALU.subtract), reads=[src_b, buf("Z")], writes=[buf("diff")])
                if i == 0:
                    oth = pt[:, (m % 2), 0:16]
                    oth_b = buf("pt%d" % (m % 2))
                    sc.op(DVE, lambda e, src=src, g=g, oth=oth: e.tensor_tensor(
                        out=oth, in0=src[:, 16:32], in1=cv[:, C_RC + 16 * g:C_RC + 16 * g + 16], op=ALU.mult),
                        reads=[src_b, buf("cv")], writes=[oth_b])
                    sc.op(DVE, lambda e, g=g, oth=oth: e.tensor_tensor(
                        out=diff[:, g, 0:16], in0=oth, in1=Z[:, g, 16:32], op=ALU.subtract),
                        reads=[oth_b, buf("Z"), buf("diff")], writes=[buf("diff")])
            pooling(0)
            pooling(1)
            yield
            for s in range(2):
                mb = MB[s]

                def mm_s(e, s=s, mb=mb):
                    ins = None
                    for h in range(4):
                        ins = e.matmul(ps[:, mb, h * 128:(h + 1) * 128], lhsT=vn[:, s, h * 128:(h + 1) * 128],
                                       rhs=wsp_sb[:, h, :], start=True, stop=True)
                    return ins
                sc.op(PE, mm_s, reads=[buf("vn%d" % s), buf("wsp")], writes=[Bps(mb)])

                def ev_s(e, s=s, mb=mb):
                    ins = None
                    for h in range(4):
                        ins = e.scalar_tensor_tensor(out=tmpS[:, h, :], in0=ps[:, mb, h * 128:(h + 1) * 128],
                                                     scalar=cv[:, C_LG + h:C_LG + h + 1], in1=Bt[:, h, :],
                                                     op0=ALU.mult, op1=ALU.add)
                    return ins
                sc.op(DVE, ev_s, reads=[Bps(mb), buf("cv"), buf("Bt")], writes=[buf("tmpS")])
                sc.op(POOL, lambda e, s=s: e.tensor_tensor(out=yT[:, 0:4, s * 128:(s + 1) * 128], in0=tmpS[:],
                                                           in1=ubf[:, :, s * 128:(s + 1) * 128], op=ALU.mult),
                      reads=[buf("tmpS"), buf("ubf0"), buf("ubf1")], writes=[buf("yTa%d" % s)])
            pooling(2)
            pooling(3)
            yield
            for gp_ in range(2):
                mb = MB[gp_]

                def mm_p(e, gp_=gp_, mb=mb):
                    ins = None
                    for gg in range(2):
                        g = gp_ * 2 + gg
                        ins = e.matmul(ps[:, mb, gg * T:(gg + 1) * T], lhsT=wpool_sb[:, g, :], rhs=diff[:, g, :],
                                       start=True, stop=True)
                    return ins
                sc.op(PE, mm_p, reads=[buf("diff"), buf("wpool")], writes=[Bps(mb)])

                def ev_p(e, gp_=gp_, mb=mb):
                    ins = None
                    for gg in range(2):
                        g = gp_ * 2 + gg
                        ins = e.tensor_scalar(out=yT[:, 4 + g, :], in0=ps[:, mb, gg * T:(gg + 1) * T],
                                              scalar1=cv[:, C_BP + g:C_BP + g + 1], scalar2=cv[:, C_PS + g:C_PS + g + 1],
                                              op0=ALU.add, op1=ALU.mult)
                    return ins
                sc.op(DVE, ev_p, reads=[Bps(mb), buf("cv")], writes=[buf("yTb%d" % gp_)])
            yield
            YT = [buf("yTa0"), buf("yTa1"), buf("yTb0"), buf("yTb1")]
            for s in range(2):
                for hf in range(2):
                    mb = MB[hf]

                    def mm_o(e, s=s, hf=hf, mb=mb):
                        ins = None
                        for k in range(8):
                            ins = e.matmul(ps[:, mb, :], lhsT=yT[:, k, s * 128:(s + 1) * 128],
                                           rhs=w_out_sb[:, k, hf * 512:(hf + 1) * 512], start=(k == 0), stop=(k == 7))
                        return ins
                    sc.op(PE, mm_o, reads=YT + [buf("w_out")], writes=[Bps(mb)])
                    sc.op(ACT, lambda e, hf=hf, mb=mb: e.activation(out=junk[:, 0:512], in_=ps[:, mb, :], func=AF.Square,
                                                                    accum_out=sm[:, c_ssm + hf:c_ssm + hf + 1]),
                          reads=[Bps(mb)], writes=[buf("junk"), buf("ssm%d" % hf)])
                for hf in range(2):
                    mb = MB[hf]
                    ti = 2 * s + hf
                    sc.op(DVE, lambda e, hf=hf, mb=mb, ti=ti: e.tensor_tensor(
                        out=stg4(ti), in0=ps[:, mb, :], in1=gp1[:, hf * 512:(hf + 1) * 512], op=ALU.mult),
                        reads=[Bps(mb), buf("gp1"), buf("ssm%d" % hf)], writes=[stg4_buf(ti)])
                sc.op(DVE, lambda e: e.tensor_tensor(out=sm[:, c_ssms:c_ssms + 1], in0=sm[:, c_ssm:c_ssm + 1],
                                                     in1=sm[:, c_ssm + 1:c_ssm + 2], op=ALU.add),
                      reads=[buf("ssm0"), buf("ssm1")], writes=[buf("ssms")])
                chm()
                for hf in range(2):
                    ti = 2 * s + hf
                    sc.op(DVE, lambda e, s=s, hf=hf, ti=ti: e.scalar_tensor_tensor(
                        out=xs(slot, s)[:, hf * 512:(hf + 1) * 512], in0=stg4(ti), scalar=rm[:, 0:1],
                        in1=xs(slot, s)[:, hf * 512:(hf + 1) * 512], op0=ALU.mult, op1=ALU.add),
                        reads=[stg4_buf(ti), brm, XB[s]], writes=[XB[s]])
                yield
            norm_A(slot, c_ss2, "ss2", ch2, r2, br2)
            yield
            yield
            yield
            yield
            norm_B(hT, "hP%d" % (i % 2), G2, SH2)
            yield

        def slab_load(G):
            i, q = divmod(G, 16)
            if i >= NT:
                return
            sl = G % 3
            rows = slice(q * 256, (q + 1) * 256)
            if i == 0:
                sc.dma(POOL, lambda e: e.dma_start(out=fc2buf[:, sl, :, :],
                                                   in_=fc2_d[rows, :].rearrange("(j p) n -> p j n", p=128)),
                       "f2p_%d" % sl, writes=[buf("f2b%d" % sl)])
                sc.dma(SP, lambda e: e.dma_start(out=fc2s_d[rows, :].rearrange("(j p) n -> p j n", p=128),
                                                 in_=fc2buf[:, sl, :, :]),
                       "f2w%d" % sl, reads=[buf("f2b%d" % sl)], writes=[buf("f2s%d" % q)])
            else:
                sc.dma(SP, lambda e: e.dma_start(out=fc2buf[:, sl, :, :],
                                                 in_=fc2s_d[rows, :].rearrange("(j p) n -> p j n", p=128)),
                       "f2_%d" % sl, reads=[buf("f2s%d" % q)], writes=[buf("f2b%d" % sl)])

        def ffn(i):
            slot = i % 3
            XB = [buf("xb%d_0" % slot), buf("xb%d_1" % slot)]
            h2T = hP[:, i % 2]
            H2T = [buf("hP%d_0" % (i % 2)), buf("hP%d_1" % (i % 2))]
            if i == 0:
                slab_load(0)
                slab_load(1)
                slab_load(2)

            def fc1(jp):
                fb = FB[jp % 2]

                def mm(e):
                    ins = None
                    for jj in range(2):
                        j = jp * 2 + jj
                        for k in range(8):
                            ins = e.matmul(ps[:, fb, jj * T:(jj + 1) * T], lhsT=fc1_sb[:, k, j * 128:(j + 1) * 128],
                                           rhs=h2T[:, k, :], start=(k == 0), stop=(k == 7))
                    return ins
                sc.op(PE, mm, reads=H2T + FC1, writes=[Bps(fb)])
                sc.op(ACT, lambda e: e.activation(out=rl[:, jp % 2, :], in_=ps[:, fb, :], func=AF.Relu),
                      reads=[Bps(fb)], writes=[buf("rl%d" % (jp % 2))])
                sc.op(POOL, lambda e: e.tensor_tensor(out=hid[:, jp % 3, :], in0=rl[:, jp % 2, :], in1=rl[:, jp % 2, :],
                                                      op=ALU.mult),
                      reads=[buf("rl%d" % (jp % 2))], writes=[buf("hid%d" % (jp % 3))])

            def fc2(jp):
                sl = (16 * i + jp) % 3

                def mm(e):
                    ins = None
                    for jj in range(2):
                        j = jp * 2 + jj
                        for s in range(2):
                            for hf in range(2):
                                ins = e.matmul(ps[:, 2 * s + hf, :],
                                               lhsT=hid[:, jp % 3, jj * T + s * 128:jj * T + (s + 1) * 128],
                                               rhs=fc2buf[:, sl, jj, hf * 512:(hf + 1) * 512],
                                               start=(j == 0), stop=(j == NJ - 1))
                    return ins
                sc.op(PE, mm, reads=[buf("hid%d" % (jp % 3)), buf("f2b%d" % sl)], writes=[Bps(b) for b in range(4)])

            for jp in range(16):
                fc1(jp)
                if jp >= 2:
                    fc2(jp - 2)
                    slab_load(16 * i + jp + 1)
                yield
            fc2(14)
            slab_load(16 * i + 17)
            fc2(15)
            slab_load(16 * i + 18)
            for s in range(2):
                for hf in range(2):
                    b_ = 2 * s + hf
                    sc.op(ACT, lambda e, b_=b_: e.activation(out=junk[:, 0:512], in_=ps[:, b_, :], func=AF.Square,
                                                             accum_out=sm[:, c_ssf + b_:c_ssf + b_ + 1]),
                          reads=[Bps(b_)], writes=[buf("junk"), buf("ssf%d" % b_)])
            for s in range(2):
                for hf in range(2):
                    b_ = 2 * s + hf
                    sc.op(DVE, lambda e, hf=hf, b_=b_: e.tensor_tensor(
                        out=stg4(b_), in0=ps[:, b_, :], in1=gp2[:, hf * 512:(hf + 1) * 512], op=ALU.mult),
                        reads=[Bps(b_), buf("gp2"), buf("ssf%d" % b_)], writes=[stg4_buf(b_)])
            sc.op(DVE, lambda e: e.tensor_tensor(out=sm[:, c_ssfs:c_ssfs + 2],
                                                 in0=sm[:, c_ssf:c_ssf + 4].rearrange("p (s h) -> p s h", h=2)[:, :, 0],
                                                 in1=sm[:, c_ssf:c_ssf + 4].rearrange("p (s h) -> p s h", h=2)[:, :, 1],
                                                 op=ALU.add),
                  reads=[buf("ssf%d" % b_) for b_ in range(4)], writes=[buf("ssfs")])
            chf()
            for s in range(2):
                for hf in range(2):
                    b_ = 2 * s + hf
                    sc.op(DVE, lambda e, s=s, hf=hf, b_=b_: e.scalar_tensor_tensor(
                        out=xs(slot, s)[:, hf * 512:(hf + 1) * 512], in0=stg4(b_), scalar=rf[:, s:s + 1],
                        in1=xs(slot, s)[:, hf * 512:(hf + 1) * 512], op0=ALU.mult, op1=ALU.add),
                        reads=[stg4_buf(b_), brf, XB[s]], writes=[XB[s]])
            dst = out_d[i * T:(i + 1) * T, :].rearrange("(s p) d -> p s d", p=128)
            srcv = xb[:, slot, :].rearrange("p (s d) -> p s d", s=2)
            ev = sc.dma(SP, lambda e: e.dma_start(out=dst, in_=srcv), "xs%d" % slot, reads=XB)
            stores.append(ev)
            yield

        stores = []
        if NT > 1:
            x_load(1)
        for _ in mixer(0):
            pass
        for i in range(NT):
            gm = mixer(i + 1) if i + 1 < NT else None
            step = 0
            for _ in ffn(i):
                step += 1
                if step == 4 and i + 2 < NT:
                    x_load(i + 2)
                if gm is not None and step >= 2:
                    try:
                        next(gm)
                    except StopIteration:
                        gm = None
            if gm is not None:
                for _ in gm:
                    pass
        sc.wait(SP, stores)

        all_keys = set()
        for e_ in ENGS:
            for (deps, fn, key, amt) in sc.ops[e_]:
                if key is not None:
                    all_keys.add(key)
        with contextlib.ExitStack() as stack:
            for k_ in sorted(all_keys):
                sems[k_] = stack.enter_context(nc.semaphore("s_" + k_))
            block = stack.enter_context(nc.Block())

            def run(eng_name, eng):
                waited = {}
                for (deps, fn, key, amt) in sc.ops[eng_name]:
                    for (k_, v_) in deps:
                        if waited.get(k_, 0) >= v_:
                            continue
                        eng.wait_ge(sems[k_], v_)
                        waited[k_] = v_
                    if fn is None:
                        continue
                    ins = fn(eng)
                    ins.then_inc(sems[key], amt)

            @block.tensor
            def _(e):
                run(PE, e)

            @block.scalar
            def _(e):
                run(ACT, e)

            @block.vector
            def _(e):
                run(DVE, e)

            @block.gpsimd
            def _(e):
                run(POOL, e)

            @block.sync
            def _(e):
                run(SP, e)
    return nc


def _host_inputs(inputs, NT=16):
    f = lambda a: np.ascontiguousarray(np.asarray(a, dtype=np.float32))
    S = NT * T
    x = f(inputs["x"])
    c = f(inputs["c"])
    pmaj = lambda w: np.ascontiguousarray(f(w).reshape(8, 128, -1).transpose(1, 0, 2).reshape(128, -1))
    col = lambda v, n: np.ascontiguousarray(f(v).reshape(n, 128).T)
    rc = np.zeros((4, 16), np.float32)
    for g in range(4):
        w = 2 << g
        for t in range(16):
            rc[g, t] = 1.0 / min(t + 1, w)
    shared = {
        "n1pb": np.ascontiguousarray(np.broadcast_to(f(inputs["norm1_post"])[None, :], (128, D))),
        "n2pb": np.ascontiguousarray(np.broadcast_to(f(inputs["norm2_post"])[None, :], (128, D))),
        "bspb": np.ascontiguousarray(np.broadcast_to(f(inputs["b_spatial"]).reshape(1, 512), (128, 512))),
        "mask": np.ascontiguousarray(np.tile(np.triu(np.ones((128, 128), np.float32)), (1, 4))),
        "wspT": np.ascontiguousarray(f(inputs["w_spatial"]).transpose(2, 0, 1).reshape(128, 512)),
        "ident": np.eye(128, dtype=np.float32),
        "wpool": np.ascontiguousarray(f(inputs["w_pool"]).transpose(1, 0, 2).reshape(128, 512)),
        "bada": f(inputs["b_ada"]).reshape(1, 6144),
        "wada": pmaj(inputs["w_ada"]),
        "w_in": pmaj(inputs["w_in"]),
        "w_out": pmaj(inputs["w_out"]),
        "w_fc1": pmaj(inputs["w_fc1"]),
        "w_fc2": f(inputs["w_fc2"]),
    }
    in_maps = []
    for b in range(x.shape[0]):
        cvec = np.zeros((128, NCV), np.float32)
        cvec[:, C_C:C_C + 8] = col(c[b], 8)
        cvec[:, C_N1:C_N1 + 8] = col(inputs["norm1_pre"], 8)
        cvec[:, C_N2:C_N2 + 8] = col(inputs["norm2_pre"], 8)
        cvec[:, C_LG:C_LG + 4] = col(inputs["ln_v_gain"], 4)
        cvec[:, C_LB:C_LB + 4] = col(inputs["ln_v_bias"], 4)
        cvec[:, C_BP:C_BP + 4] = col(np.asarray(inputs["b_pool"]).reshape(-1), 4)
        cvec[:, C_PS:C_PS + 4] = col(inputs["pool_scale"], 4)
        cvec[:, C_RC:C_RC + 64] = rc.reshape(1, 64)
        m = dict(shared)
        m["x"] = np.ascontiguousarray(x[b, :S])
        m["cvec"] = cvec
        in_maps.append(m)
    return in_maps


def kernel(**inputs):
    in_maps = _host_inputs(inputs, 16)
    nc = build(16)
    res = run_bass_kernel_spmd(nc, in_maps, core_ids=list(range(len(in_maps))))
    return np.stack([np.asarray(r["out"], dtype=np.float32) for r in res.results], axis=0)
```

```python
import numpy as np
import concourse.bass as bass
import concourse.mybir as mybir
from concourse.bass_utils import run_bass_kernel_spmd

F32 = mybir.dt.float32
BF16 = mybir.dt.bfloat16
I32 = mybir.dt.int32
AF = mybir.ActivationFunctionType
ALU = mybir.AluOpType

D = 1024
SEQ = 4096
T = 256
NJ = 32
EPS = 1e-6
PE, ACT, DVE, POOL, SP = "pe", "act", "dve", "pool", "sp"
ENGS = (PE, ACT, DVE, POOL, SP)

C_C, C_N1, C_N2, C_LG, C_LB, C_BP, C_PS, C_RC = 0, 8, 16, 24, 28, 32, 36, 40
NCV = 40 + 64


class Buf:
    __slots__ = ("name", "w", "r")

    def __init__(self, name):
        self.name = name
        self.w = None
        self.r = []


class Sched:
    def __init__(self):
        self.ops = {e: [] for e in ENGS}
        self.count = {}

    def _deps(self, eng, reads, writes):
        deps = []
        for b in reads:
            if b.w is not None:
                deps.append((b.w, "raw"))
        for b in writes:
            if b.w is not None:
                deps.append((b.w, "waw"))
            for r in b.r:
                deps.append((r, "war"))
        out = []
        for (ev, kind) in deps:
            if ev[0] == eng:
                if eng == PE:
                    continue
            out.append(ev)
        return out

    def _record(self, eng, fn, key, amt, reads, writes):
        deps = self._deps(eng, reads, writes)
        self.count[key] = self.count.get(key, 0) + amt
        ev = (key, self.count[key])
        self.ops[eng].append((deps, fn, key, amt))
        for b in reads:
            b.r.append(ev)
        for b in writes:
            b.w = ev
            b.r = []
        return ev

    def op(self, eng, fn, reads=(), writes=()):
        return self._record(eng, fn, eng, 1, reads, writes)

    def dma(self, queue, fn, key, reads=(), writes=()):
        return self._record(queue, fn, key, 16, reads, writes)

    def wait(self, eng, events):
        self.ops[eng].append((list(events), None, None, 0))


def build(NT=16):
    S = NT * T
    nc = bass.Bass("TRN2", target_bir_lowering=False)
    dt_in = lambda name, shape: nc.dram_tensor(name, shape, F32, kind="ExternalInput").ap()
    x_d = dt_in("x", [S, D])
    cvec_d = dt_in("cvec", [128, NCV])
    n1pb_d = dt_in("n1pb", [128, D])
    n2pb_d = dt_in("n2pb", [128, D])
    bspb_d = dt_in("bspb", [128, 512])
    mask_d = dt_in("mask", [128, 512])
    wspT_d = dt_in("wspT", [128, 512])
    ident_d = dt_in("ident", [128, 128])
    wpool_d = dt_in("wpool", [128, 512])
    bada_d = dt_in("bada", [1, 6144])
    wada_d = dt_in("wada", [128, 8 * 6144])
    win_d = dt_in("w_in", [128, 8 * 1536])
    wout_d = dt_in("w_out", [128, 8 * 1024])
    fc1_d = dt_in("w_fc1", [128, 8 * 4096])
    fc2_d = dt_in("w_fc2", [4096, D])
    out_d = nc.dram_tensor("out", [S, D], F32, kind="ExternalOutput").ap()
    fc2s_d = nc.dram_tensor("fc2s", [4096, D], BF16, kind="Internal").ap()

    sc = Sched()
    sems = {}

    import contextlib
    with contextlib.ExitStack() as _st:
        w_in_sb = _st.enter_context(nc.sbuf_tensor("w_in_sb", [128, 8, 1536], BF16))
        w_out_sb = _st.enter_context(nc.sbuf_tensor("w_out_sb", [128, 8, 1024], BF16))
        fc1_sb = _st.enter_context(nc.sbuf_tensor("fc1_sb", [128, 8, 4096], BF16))
        fc2buf = _st.enter_context(nc.sbuf_tensor("fc2buf", [128, 3, 2, 1024], BF16))
        wsp_sb = _st.enter_context(nc.sbuf_tensor("wsp_sb", [128, 4, 128], BF16))
        wpool_sb = _st.enter_context(nc.sbuf_tensor("wpool_sb", [128, 4, 128], BF16))
        ident = _st.enter_context(nc.sbuf_tensor("ident_sb", [128, 128], BF16))
        ones = _st.enter_context(nc.sbuf_tensor("ones_sb", [128, 128], F32))
        gp1 = _st.enter_context(nc.sbuf_tensor("gp1", [128, D], F32))
        gp2 = _st.enter_context(nc.sbuf_tensor("gp2", [128, D], F32))
        Bt = _st.enter_context(nc.sbuf_tensor("Bt", [128, 4, 128], F32))
        cv = _st.enter_context(nc.sbuf_tensor("cv", [128, NCV], F32))
        mc = _st.enter_context(nc.sbuf_tensor("mc", [128, 64], F32))
        sm = _st.enter_context(nc.sbuf_tensor("sm", [128, 96], F32))
        xb = _st.enter_context(nc.sbuf_tensor("xb", [128, 3, 2048], F32))
        hbf = _st.enter_context(nc.sbuf_tensor("hbf", [128, 2, 1024], BF16))
        hP = _st.enter_context(nc.sbuf_tensor("hP", [128, 2, 8, T], BF16))
        junk = _st.enter_context(nc.sbuf_tensor("junk", [128, 1024], BF16))
        ubf = _st.enter_context(nc.sbuf_tensor("ubf", [128, 4, T], BF16))
        vg = _st.enter_context(nc.sbuf_tensor("vg", [128, 2, 512], F32))
        vn = _st.enter_context(nc.sbuf_tensor("vn", [128, 2, 512], BF16))
        Z = _st.enter_context(nc.sbuf_tensor("Z", [128, 4, 272], F32))
        pt = _st.enter_context(nc.sbuf_tensor("pt", [128, 2, 272], F32))
        diff = _st.enter_context(nc.sbuf_tensor("diff", [128, 4, T], BF16))
        tmpS = _st.enter_context(nc.sbuf_tensor("tmpS", [128, 4, 128], F32))
        yT = _st.enter_context(nc.sbuf_tensor("yT", [128, 8, T], BF16))
        tmp = _st.enter_context(nc.sbuf_tensor("tmp", [128, 2, 512], F32))
        rl = _st.enter_context(nc.sbuf_tensor("rl", [128, 2, 512], F32))
        hid = _st.enter_context(nc.sbuf_tensor("hid", [128, 3, 512], BF16))
        ps = _st.enter_context(nc.psum_tensor("ps", [128, 8, 512], F32))
        B = {}

        def buf(name):
            if name not in B:
                B[name] = Buf(name)
            return B[name]

        def bank(b):
            return ps[:, b, :]

        def bank_bf(b):
            return ps[:, b, :].bitcast(BF16)

        def Bps(b):
            return buf("ps%d" % b)

        def stg4(ti):
            return tmp[:, ti, :] if ti < 2 else vg[:, ti - 2, :]

        def stg4_buf(ti):
            return buf("tmp%d" % ti) if ti < 2 else buf("vg%d" % (ti - 2))

        def xs(slot, s):
            return xb[:, slot, s * 1024:(s + 1) * 1024]

        sm_next = [0]

        def smcol(n):
            a = sm_next[0]
            sm_next[0] += n
            assert sm_next[0] <= 96
            return a

        def rstd_chain(src_ap, src_bufs, n, inv_d, tag):
            c0 = smcol(4 * n)
            vv = sm[:, c0:c0 + n]
            r = sm[:, c0 + n:c0 + 2 * n]
            t = sm[:, c0 + 2 * n:c0 + 3 * n]
            u = sm[:, c0 + 3 * n:c0 + 4 * n]
            bv, br, bt_, bu = (buf(tag + "_vv"), buf(tag + "_r"), buf(tag + "_t"), buf(tag + "_u"))

            def chain():
                sc.op(DVE, lambda e: e.tensor_scalar(out=vv, in0=src_ap, scalar1=inv_d, scalar2=EPS,
                                                     op0=ALU.mult, op1=ALU.add),
                      reads=src_bufs, writes=[bv])
                sc.op(DVE, lambda e: e.tensor_scalar(out=r.bitcast(I32), in0=vv.bitcast(I32), scalar1=-0.5,
                                                     scalar2=1597463007.0, op0=ALU.mult, op1=ALU.add),
                      reads=[bv], writes=[br])
                for _ in range(3):
                    sc.op(DVE, lambda e: e.tensor_tensor(out=t, in0=r, in1=r, op=ALU.mult),
                          reads=[br], writes=[bt_])
                    sc.op(DVE, lambda e: e.scalar_tensor_tensor(out=u, in0=t, scalar=-0.5, in1=vv,
                                                                op0=ALU.mult, op1=ALU.mult),
                          reads=[bt_, bv], writes=[bu])
                    sc.op(DVE, lambda e: e.scalar_tensor_tensor(out=r, in0=u, scalar=1.5, in1=r,
                                                                op0=ALU.add, op1=ALU.mult),
                          reads=[bu, br], writes=[br])
            return chain, r, br

        sc.op(POOL, lambda e: e.memset(ones[:], 1.0), writes=[buf("ones")])
        sc.op(POOL, lambda e: e.memset(Z[:, :, 0:16], 0.0), writes=[buf("Z")])

        sc.dma(SP, lambda e: e.dma_start(out=cv[:], in_=cvec_d[:, :]), "c_cv", writes=[buf("cv")])
        sc.dma(SP, lambda e: e.dma_start(out=gp1[:], in_=n1pb_d[:, :]), "c_g1", writes=[buf("gp1")])
        sc.dma(SP, lambda e: e.dma_start(out=gp2[:], in_=n2pb_d[:, :]), "c_g2", writes=[buf("gp2")])
        sc.dma(SP, lambda e: e.dma_start(out=Bt[:].rearrange("p h t -> p (h t)"), in_=bspb_d[:, :]), "c_bt",
               writes=[buf("Bt")])
        sc.dma(SP, lambda e: e.dma_start(out=vg[:, 0, :], in_=wspT_d[:, :]), "c_ws", writes=[buf("vg0")])
        sc.dma(SP, lambda e: e.dma_start(out=vg[:, 1, :], in_=mask_d[:, :]), "c_mk", writes=[buf("vg1")])
        sc.dma(POOL, lambda e: e.dma_start(out=ident[:], in_=ident_d[:, :]), "c_id", writes=[buf("ident")])
        sc.dma(POOL, lambda e: e.dma_start(out=wpool_sb[:].rearrange("p g d -> p (g d)"), in_=wpool_d[:, :]),
               "c_wp", writes=[buf("wpool")])
        def x_load(i):
            slot = i % 3
            src = x_d[i * T:(i + 1) * T, :].rearrange("(s p) d -> p s d", p=128)
            dst = xb[:, slot, :].rearrange("p (s d) -> p s d", s=2)
            sc.dma(SP, lambda e: e.dma_start(out=dst, in_=src), "xl%d" % slot,
                   writes=[buf("xb%d_0" % slot), buf("xb%d_1" % slot)])
        x_load(0)
        win_v = win_d.rearrange("p (k n) -> p k n", k=8)
        for hh in range(2):
            sc.dma(POOL, lambda e, hh=hh: e.dma_start(out=w_in_sb[:, hh * 4:(hh + 1) * 4, :],
                                                      in_=win_v[:, hh * 4:(hh + 1) * 4, :]),
                   "w_in%d" % hh, writes=[buf("w_in%d" % hh)])
        W_IN = [buf("w_in0"), buf("w_in1")]
        sc.dma(POOL, lambda e: e.dma_start(out=w_out_sb[:], in_=wout_d.rearrange("p (k n) -> p k n", k=8)),
               "w_out", writes=[buf("w_out")])
        fc1_v = fc1_d.rearrange("p (k n) -> p k n", k=8)
        for q in range(4):
            sc.dma(POOL, lambda e, q=q: e.dma_start(out=fc1_sb[:, 2 * q:2 * q + 2, :], in_=fc1_v[:, 2 * q:2 * q + 2, :]),
                   "fc1_%d" % q, writes=[buf("fc1_%d" % q)])
        FC1 = [buf("fc1_%d" % q) for q in range(4)]

        c_th = smcol(8); c_hf = smcol(8); c_sc = smcol(8)
        sc.op(ACT, lambda e: e.activation(out=sm[:, c_th:c_th + 8], in_=cv[:, C_C:C_C + 8], func=AF.Tanh, scale=0.5),
              reads=[buf("cv")], writes=[buf("s_th")])
        sc.op(DVE, lambda e: e.tensor_scalar(out=sm[:, c_hf:c_hf + 8], in0=sm[:, c_th:c_th + 8], scalar1=1.0,
                                             scalar2=0.5, op0=ALU.add, op1=ALU.mult),
              reads=[buf("s_th")], writes=[buf("s_hf")])
        sc.op(DVE, lambda e: e.tensor_tensor(out=sm[:, c_sc:c_sc + 8], in0=sm[:, c_hf:c_hf + 8],
                                             in1=cv[:, C_C:C_C + 8], op=ALU.mult),
              reads=[buf("s_hf"), buf("cv")], writes=[buf("s_sc")])
        scv = sm[:, c_sc:c_sc + 8]

        wada_v = wada_d.rearrange("p (k n) -> p k n", k=8)
        CH_COL = {0: 0, 1: 8, 3: 16, 4: 24}
        for b in range(24):
            st = 1 + b % 2
            stg = xb[:, st, :].rearrange("p (k c) -> p k c", c=256)
            stg_bufs = [buf("xb%d_0" % st), buf("xb%d_1" % st)]
            sc.dma(SP, lambda e, b=b, stg=stg: e.dma_start(out=stg, in_=wada_v[:, :, b * 256:(b + 1) * 256]),
                   "wa%d" % (b % 2), writes=stg_bufs)
            sc.dma(SP, lambda e, b=b: e.dma_start(out=rl[0:1, b % 2, 0:256], in_=bada_d[0:1, b * 256:(b + 1) * 256]),
                   "ba%d" % (b % 2), writes=[buf("rl%d" % (b % 2))])
            pr = 4 + b % 2

            def mm_mod(e, b=b, stg=stg, pr=pr):
                for k in range(8):
                    e.matmul(ps[0:1, pr, 0:256], lhsT=scv[:, k:k + 1], rhs=stg[:, k, :], start=(k == 0), stop=False)
                return e.matmul(ps[0:1, pr, 0:256], lhsT=ones[0:1, 0:1], rhs=rl[0:1, b % 2, 0:256], start=False, stop=True)
            sc.op(PE, mm_mod, reads=stg_bufs + [buf("s_sc"), buf("ones"), buf("rl%d" % (b % 2))], writes=[Bps(pr)])
            sc.op(DVE, lambda e, b=b, pr=pr: e.tensor_copy(out=tmp[0:1, b % 2, 0:256], in_=ps[0:1, pr, 0:256]),
                  reads=[Bps(pr)], writes=[buf("tmp%d" % (b % 2))])
            chunk = b // 4
            if chunk in (2, 5):
                gp = gp1 if chunk == 2 else gp2
                gpb = buf("gp1") if chunk == 2 else buf("gp2")
                pb = 6 + b % 2
                cols = slice((b % 4) * 256, (b % 4) * 256 + 256)
                sc.op(PE, lambda e, b=b, pb=pb: e.matmul(ps[:, pb, 0:256], lhsT=ones[0:1, :], rhs=tmp[0:1, b % 2, 0:256],
                                                        start=True, stop=True),
                      reads=[buf("tmp%d" % (b % 2)), buf("ones")], writes=[Bps(pb)])
                sc.op(DVE, lambda e, gp=gp, pb=pb, cols=cols: e.tensor_tensor(out=gp[:, cols], in0=ps[:, pb, 0:256],
                                                                          in1=gp[:, cols], op=ALU.mult),
                      reads=[Bps(pb), gpb], writes=[gpb])
            else:
                col0 = CH_COL[chunk] + (b % 4) * 2

                def mm_col(e, b=b, col0=col0):
                    ins = None
                    for q in range(2):
                        ins = e.matmul(ps[:, 0, col0 + q:col0 + q + 1], lhsT=tmp[0:1, b % 2, q * 128:(q + 1) * 128],
                                       rhs=ones[0:1, 0:1], start=True, stop=True)
                    return ins
                sc.op(PE, mm_col, reads=[buf("tmp%d" % (b % 2)), buf("ones")], writes=[Bps(0)])
        sc.op(DVE, lambda e: e.tensor_copy(out=mc[:, 0:32], in_=ps[:, 0, 0:32]), reads=[Bps(0)], writes=[buf("mc_raw")])
        sc.op(DVE, lambda e: e.scalar_tensor_tensor(out=mc[:, 32:40], in0=mc[:, 8:16], scalar=1.0,
                                                    in1=cv[:, C_N1:C_N1 + 8], op0=ALU.add, op1=ALU.mult),
              reads=[buf("mc_raw"), buf("cv")], writes=[buf("mc_g1")])
        sc.op(DVE, lambda e: e.scalar_tensor_tensor(out=mc[:, 40:48], in0=mc[:, 24:32], scalar=1.0,
                                                    in1=cv[:, C_N2:C_N2 + 8], op0=ALU.add, op1=ALU.mult),
              reads=[buf("mc_raw"), buf("cv")], writes=[buf("mc_g2")])
        MODB = [buf("mc_raw"), buf("mc_g1"), buf("mc_g2")]
        G1, SH1, G2, SH2 = 32, 0, 40, 16

        sc.op(DVE, lambda e: e.tensor_tensor(out=vg[:, 0, :], in0=vg[:, 0, :], in1=vg[:, 1, :], op=ALU.mult),
              reads=[buf("vg0"), buf("vg1")], writes=[buf("vg0")])
        sc.op(ACT, lambda e: e.activation(out=wsp_sb[:].rearrange("p h t -> p (h t)"), in_=vg[:, 0, :], func=AF.Identity),
              reads=[buf("vg0")], writes=[buf("wsp")])
        sc.op(PE, lambda e: e.matmul(ps[:, 1, :], lhsT=ones[:, :], rhs=vg[:, 0, :], start=True, stop=True),
              reads=[buf("vg0"), buf("ones")], writes=[Bps(1)])

        def bt_fix(e):
            ins = None
            for h in range(4):
                ins = e.scalar_tensor_tensor(out=Bt[:, h, :], in0=ps[:, 1, h * 128:(h + 1) * 128],
                                             scalar=cv[:, C_LB + h:C_LB + h + 1], in1=Bt[:, h, :],
                                             op0=ALU.mult, op1=ALU.add)
            return ins
        sc.op(DVE, bt_fix, reads=[Bps(1), buf("cv"), buf("Bt")], writes=[buf("Bt")])

        c_ss1 = smcol(2)
        ch1, r1, br1 = rstd_chain(sm[:, c_ss1:c_ss1 + 2], [buf("ss1_0"), buf("ss1_1")], 2, 1.0 / D, "r1")
        c_vs = smcol(2); c_vq = smcol(2); c_mean = smcol(2); c_msq = smcol(2); c_var = smcol(2)
        chv, rv, brv = rstd_chain(sm[:, c_var:c_var + 2], [buf("var")], 2, 1.0, "rv")
        c_ssm = smcol(2); c_ssms = smcol(1)
        chm, rm, brm = rstd_chain(sm[:, c_ssms:c_ssms + 1], [buf("ssms")], 1, 1.0 / D, "rm")
        c_ss2 = smcol(2)
        ch2, r2, br2 = rstd_chain(sm[:, c_ss2:c_ss2 + 2], [buf("ss2_0"), buf("ss2_1")], 2, 1.0 / D, "r2")
        c_ssf = smcol(4); c_ssfs = smcol(2)
        chf, rf, brf = rstd_chain(sm[:, c_ssfs:c_ssfs + 2], [buf("ssfs")], 2, 1.0 / D, "rf")

        MB = (6, 7)
        FB = (4, 5)

        def norm_A(slot, ss_col, ss_name, chain, r_ap, r_buf):
            for s in range(2):
                sc.op(ACT, lambda e, s=s: e.activation(out=junk[:], in_=xs(slot, s), func=AF.Square,
                                                       accum_out=sm[:, ss_col + s:ss_col + s + 1]),
                      reads=[buf("xb%d_%d" % (slot, s))], writes=[buf("junk"), buf("%s_%d" % (ss_name, s))])
            chain()

        def norm_A2(slot, r_ap, r_buf):
            for s in range(2):
                sc.op(ACT, lambda e, s=s: e.activation(out=hbf[:, s, :], in_=xs(slot, s), func=AF.Identity,
                                                       scale=r_ap[:, s:s + 1]),
                      reads=[buf("xb%d_%d" % (slot, s)), r_buf], writes=[buf("hbf_%d" % s)])

        def norm_B(dstT, dstT_buf, gcol, shcol):
            for s in range(2):
                mb = MB[s]

                def tr(e, s=s, mb=mb):
                    ins = None
                    for k in range(8):
                        ins = e.transpose(out=bank_bf(mb)[:, k * 128:(k + 1) * 128],
                                          in_=hbf[:, s, k * 128:(k + 1) * 128], identity=ident[:])
                    return ins
                sc.op(PE, tr, reads=[buf("hbf_%d" % s), buf("ident")], writes=[Bps(mb)])

                if s == 0:
                    def ev(e, s=s, mb=mb):
                        ins = None
                        for k in range(8):
                            ins = e.activation(out=dstT[:, k, s * 128:(s + 1) * 128],
                                               in_=bank_bf(mb)[:, k * 128:(k + 1) * 128], func=AF.Identity,
                                               scale=mc[:, gcol + k:gcol + k + 1], bias=mc[:, shcol + k:shcol + k + 1])
                        return ins
                    sc.op(ACT, ev, reads=[Bps(mb)] + MODB, writes=[buf(dstT_buf + "_%d" % s)])
                else:
                    def ev(e, s=s, mb=mb):
                        ins = None
                        for k in range(8):
                            ins = e.tensor_scalar(out=dstT[:, k, s * 128:(s + 1) * 128],
                                                  in0=bank_bf(mb)[:, k * 128:(k + 1) * 128],
                                                  scalar1=mc[:, gcol + k:gcol + k + 1],
                                                  scalar2=mc[:, shcol + k:shcol + k + 1], op0=ALU.mult, op1=ALU.add)
                        return ins
                    sc.op(DVE, ev, reads=[Bps(mb)] + MODB, writes=[buf(dstT_buf + "_%d" % s)])

        def mixer(i):
            slot = i % 3
            XB = [buf("xb%d_0" % slot), buf("xb%d_1" % slot)]
            hT = hP[:, i % 2]
            HT = [buf("hP%d_0" % (i % 2)), buf("hP%d_1" % (i % 2))]
            norm_A(slot, c_ss1, "ss1", ch1, r1, br1)
            yield
            yield
            norm_A2(slot, r1, br1)
            yield
            norm_B(hT, "hP%d" % (i % 2), G1, SH1)
            yield
            if i > 0:
                sc.op(DVE, lambda e: e.tensor_copy(out=Z[:, :, 0:16], in_=Z[:, :, 256:272]),
                      reads=[buf("Z")], writes=[buf("Z")])
            for gp_ in range(2):
                mb = MB[gp_]

                def mm_z(e, gp_=gp_, mb=mb):
                    ins = None
                    for gg in range(2):
                        g = gp_ * 2 + gg
                        for k in range(8):
                            ins = e.matmul(ps[:, mb, gg * T:(gg + 1) * T],
                                           lhsT=w_in_sb[:, k, 1024 + g * 128:1024 + (g + 1) * 128],
                                           rhs=hT[:, k, :], start=(k == 0), stop=(k == 7))
                    return ins
                sc.op(PE, mm_z, reads=HT + W_IN, writes=[Bps(mb)])
                sc.op(ACT, lambda e, gp_=gp_, mb=mb: e.activation(
                    out=Z[:, 2 * gp_:2 * gp_ + 2, 16:272], in_=ps[:, mb, :].rearrange("p (g t) -> p g t", g=2),
                    func=AF.Identity), reads=[Bps(mb)], writes=[buf("Z")])
            yield
            for cp in range(2):
                mb = MB[cp]

                def mm_u(e, cp=cp, mb=mb):
                    ins = None
                    for cc in range(2):
                        c = cp * 2 + cc
                        for k in range(8):
                            ins = e.matmul(ps[:, mb, cc * T:(cc + 1) * T], lhsT=w_in_sb[:, k, c * 128:(c + 1) * 128],
                                           rhs=hT[:, k, :], start=(k == 0), stop=(k == 7))
                    return ins
                sc.op(PE, mm_u, reads=HT + W_IN, writes=[Bps(mb)])
                sc.op(ACT, lambda e, cp=cp, mb=mb: e.activation(
                    out=ubf[:, 2 * cp:2 * cp + 2, :].rearrange("p c t -> p (c t)"), in_=ps[:, mb, :],
                    func=AF.Gelu_apprx_tanh), reads=[Bps(mb)], writes=[buf("ubf%d" % cp)])
            for s in range(2):
                mb = MB[s]

                def mm_v(e, s=s, mb=mb):
                    ins = None
                    for k in range(8):
                        ins = e.matmul(ps[:, mb, :], lhsT=hT[:, k, s * 128:(s + 1) * 128], rhs=w_in_sb[:, k, 512:1024],
                                       start=(k == 0), stop=(k == 7))
                    return ins
                sc.op(PE, mm_v, reads=HT + W_IN, writes=[Bps(mb)])
                sc.op(ACT, lambda e, s=s, mb=mb: e.activation(out=vg[:, s, :], in_=ps[:, mb, :], func=AF.Gelu_apprx_tanh,
                                                              accum_out=sm[:, c_vs + s:c_vs + s + 1]),
                      reads=[Bps(mb)], writes=[buf("vg%d" % s), buf("vs%d" % s)])
                sc.op(ACT, lambda e, s=s: e.activation(out=junk[:, 0:512], in_=vg[:, s, :], func=AF.Square,
                                                       accum_out=sm[:, c_vq + s:c_vq + s + 1]),
                      reads=[buf("vg%d" % s)], writes=[buf("junk"), buf("vq%d" % s)])
            sc.op(DVE, lambda e: e.tensor_scalar(out=sm[:, c_mean:c_mean + 2], in0=sm[:, c_vs:c_vs + 2],
                                                 scalar1=1.0 / 512, scalar2=None, op0=ALU.mult),
                  reads=[buf("vs0"), buf("vs1")], writes=[buf("mean")])
            sc.op(DVE, lambda e: e.tensor_tensor(out=sm[:, c_msq:c_msq + 2], in0=sm[:, c_mean:c_mean + 2],
                                                 in1=sm[:, c_mean:c_mean + 2], op=ALU.mult),
                  reads=[buf("mean")], writes=[buf("msq")])
            sc.op(DVE, lambda e: e.scalar_tensor_tensor(out=sm[:, c_var:c_var + 2], in0=sm[:, c_vq:c_vq + 2],
                                                        scalar=1.0 / 512, in1=sm[:, c_msq:c_msq + 2],
                                                        op0=ALU.mult, op1=ALU.subtract),
                  reads=[buf("vq0"), buf("vq1"), buf("msq")], writes=[buf("var")])
            chv()
            for s in range(2):
                sc.op(DVE, lambda e, s=s: e.tensor_scalar(out=vn[:, s, :], in0=vg[:, s, :],
                                                           scalar1=sm[:, c_mean + s:c_mean + s + 1],
                                                           scalar2=rv[:, s:s + 1], op0=ALU.subtract, op1=ALU.mult),
                      reads=[buf("vg%d" % s), buf("mean"), brv], writes=[buf("vn%d" % s)])
            def pooling(g):
                m = g + 1
                w = 1 << m
                src = Z[:, g, :]
                src_b = buf("Z")
                for k in range(m):
                    lo = (1 << (k + 1)) - 1
                    sh = 1 << k
                    dst = pt[:, k % 2, :]
                    dst_b = buf("pt%d" % (k % 2))
                    sc.op(DVE, lambda e, src=src, dst=dst, lo=lo, sh=sh: e.tensor_tensor(
                        out=dst[:, lo:272], in0=src[:, lo:272], in1=src[:, lo - sh:272 - sh], op=ALU.add),
                        reads=[src_b], writes=[dst_b])
                    src, src_b = dst, dst_b
                sc.op(DVE, lambda e, src=src, g=g, w=w: e.scalar_tensor_tensor(
                    out=diff[:, g, :], in0=src[:, 16:272], scalar=1.0 / w, in1=Z[:, g, 16:272],
                    op0=ALU.mult, op1=ALU.subtract), reads=[src_b, buf("Z")], writes=[buf("diff")])
                if i == 0:
                    oth = pt[:, (m % 2), 0:16]
                    oth_b = buf("pt%d" % (m % 2))
                    sc.op(DVE, lambda e, src=src, g=g, oth=oth: e.tensor_tensor(
                        out=oth, in0=src[:, 16:32], in1=cv[:, C_RC + 16 * g:C_RC + 16 * g + 16], op=ALU.mult),
                        reads=[src_b, buf("cv")], writes=[oth_b])
                    sc.op(DVE, lambda e, g=g, oth=oth: e.tensor_tensor(
                        out=diff[:, g, 0:16], in0=oth, in1=Z[:, g, 16:32], op=ALU.subtract),
                        reads=[oth_b, buf("Z"), buf("diff")], writes=[buf("diff")])
            pooling(0)
            pooling(1)
            yield
            for s in range(2):
                mb = MB[s]

                def mm_s(e, s=s, mb=mb):
                    ins = None
                    for h in range(4):
                        ins = e.matmul(ps[:, mb, h * 128:(h + 1) * 128], lhsT=vn[:, s, h * 128:(h + 1) * 128],
                                       rhs=wsp_sb[:, h, :], start=True, stop=True)
                    return ins
                sc.op(PE, mm_s, reads=[buf("vn%d" % s), buf("wsp")], writes=[Bps(mb)])

                def ev_s(e, s=s, mb=mb):
                    ins = None
                    for h in range(4):
                        ins = e.scalar_tensor_tensor(out=tmpS[:, h, :], in0=ps[:, mb, h * 128:(h + 1) * 128],
                                                     scalar=cv[:, C_LG + h:C_LG + h + 1], in1=Bt[:, h, :],
                                                     op0=ALU.mult, op1=ALU.add)
                    return ins
                sc.op(DVE, ev_s, reads=[Bps(mb), buf("cv"), buf("Bt")], writes=[buf("tmpS")])
                sc.op(POOL, lambda e, s=s: e.tensor_tensor(out=yT[:, 0:4, s * 128:(s + 1) * 128], in0=tmpS[:],
                                                           in1=ubf[:, :, s * 128:(s + 1) * 128], op=ALU.mult),
                      reads=[buf("tmpS"), buf("ubf0"), buf("ubf1")], writes=[buf("yTa%d" % s)])
            pooling(2)
            pooling(3)
            yield
            for gp_ in range(2):
                mb = MB[gp_]

                def mm_p(e, gp_=gp_, mb=mb):
                    ins = None
                    for gg in range(2):
                        g = gp_ * 2 + gg
                        ins = e.matmul(ps[:, mb, gg * T:(gg + 1) * T], lhsT=wpool_sb[:, g, :], rhs=diff[:, g, :],
                                       start=True, stop=True)
                    return ins
                sc.op(PE, mm_p, reads=[buf("diff"), buf("wpool")], writes=[Bps(mb)])

                def ev_p(e, gp_=gp_, mb=mb):
                    ins = None
                    for gg in range(2):
                        g = gp_ * 2 + gg
                        ins = e.tensor_scalar(out=yT[:, 4 + g, :], in0=ps[:, mb, gg * T:(gg + 1) * T],
                                              scalar1=cv[:, C_BP + g:C_BP + g + 1], scalar2=cv[:, C_PS + g:C_PS + g + 1],
                                              op0=ALU.add, op1=ALU.mult)
                    return ins
                sc.op(DVE, ev_p, reads=[Bps(mb), buf("cv")], writes=[buf("yTb%d" % gp_)])
            yield
            YT = [buf("yTa0"), buf("yTa1"), buf("yTb0"), buf("yTb1")]
            for s in range(2):
                for hf in range(2):
                    mb = MB[hf]

                    def mm_o(e, s=s, hf=hf, mb=mb):
                        ins = None
                        for k in range(8):
                            ins = e.matmul(ps[:, mb, :], lhsT=yT[:, k, s * 128:(s + 1) * 128],
                                           rhs=w_out_sb[:, k, hf * 512:(hf + 1) * 512], start=(k == 0), stop=(k == 7))
                        return ins
                    sc.op(PE, mm_o, reads=YT + [buf("w_out")], writes=[Bps(mb)])
                    sc.op(ACT, lambda e, hf=hf, mb=mb: e.activation(out=junk[:, 0:512], in_=ps[:, mb, :], func=AF.Square,
                                                                    accum_out=sm[:, c_ssm + hf:c_ssm + hf + 1]),
                          reads=[Bps(mb)], writes=[buf("junk"), buf("ssm%d" % hf)])
                for hf in range(2):
                    mb = MB[hf]
                    ti = 2 * s + hf
                    sc.op(DVE, lambda e, hf=hf, mb=mb, ti=ti: e.tensor_tensor(
                        out=stg4(ti), in0=ps[:, mb, :], in1=gp1[:, hf * 512:(hf + 1) * 512], op=ALU.mult),
                        reads=[Bps(mb), buf("gp1"), buf("ssm%d" % hf)], writes=[stg4_buf(ti)])
                sc.op(DVE, lambda e: e.tensor_tensor(out=sm[:, c_ssms:c_ssms + 1], in0=sm[:, c_ssm:c_ssm + 1],
                                                     in1=sm[:, c_ssm + 1:c_ssm + 2], op=ALU.add),
                      reads=[buf("ssm0"), buf("ssm1")], writes=[buf("ssms")])
                chm()
                for hf in range(2):
                    ti = 2 * s + hf
                    sc.op(DVE, lambda e, s=s, hf=hf, ti=ti: e.scalar_tensor_tensor(
                        out=xs(slot, s)[:, hf * 512:(hf + 1) * 512], in0=stg4(ti), scalar=rm[:, 0:1],
                        in1=xs(slot, s)[:, hf * 512:(hf + 1) * 512], op0=ALU.mult, op1=ALU.add),
                        reads=[stg4_buf(ti), brm, XB[s]], writes=[XB[s]])
                yield
            yield
            yield
            norm_A(slot, c_ss2, "ss2", ch2, r2, br2)
            yield
            norm_A2(slot, r2, br2)
            yield
            norm_B(hT, "hP%d" % (i % 2), G2, SH2)
            yield

        def slab_load(G):
            i, q = divmod(G, 16)
            if i >= NT:
                return
            sl = G % 3
            rows = slice(q * 256, (q + 1) * 256)
            if i == 0:
                sc.dma(POOL, lambda e: e.dma_start(out=fc2buf[:, sl, :, :],
                                                   in_=fc2_d[rows, :].rearrange("(j p) n -> p j n", p=128)),
                       "f2p_%d" % sl, writes=[buf("f2b%d" % sl)])
                sc.dma(SP, lambda e: e.dma_start(out=fc2s_d[rows, :].rearrange("(j p) n -> p j n", p=128),
                                                 in_=fc2buf[:, sl, :, :]),
                       "f2w%d" % sl, reads=[buf("f2b%d" % sl)], writes=[buf("f2s%d" % q)])
            else:
                sc.dma(SP, lambda e: e.dma_start(out=fc2buf[:, sl, :, :],
                                                 in_=fc2s_d[rows, :].rearrange("(j p) n -> p j n", p=128)),
                       "f2_%d" % sl, reads=[buf("f2s%d" % q)], writes=[buf("f2b%d" % sl)])

        def ffn(i):
            slot = i % 3
            XB = [buf("xb%d_0" % slot), buf("xb%d_1" % slot)]
            h2T = hP[:, i % 2]
            H2T = [buf("hP%d_0" % (i % 2)), buf("hP%d_1" % (i % 2))]
            if i == 0:
                slab_load(0)
                slab_load(1)
                slab_load(2)

            def fc1(jp):
                fb = FB[jp % 2]

                def mm(e):
                    ins = None
                    for jj in range(2):
                        j = jp * 2 + jj
                        for k in range(8):
                            ins = e.matmul(ps[:, fb, jj * T:(jj + 1) * T], lhsT=fc1_sb[:, k, j * 128:(j + 1) * 128],
                                           rhs=h2T[:, k, :], start=(k == 0), stop=(k == 7))
                    return ins
                sc.op(PE, mm, reads=H2T + FC1, writes=[Bps(fb)])
                sc.op(ACT, lambda e: e.activation(out=rl[:, jp % 2, :], in_=ps[:, fb, :], func=AF.Relu),
                      reads=[Bps(fb)], writes=[buf("rl%d" % (jp % 2))])
                sc.op(POOL, lambda e: e.tensor_tensor(out=hid[:, jp % 3, :], in0=rl[:, jp % 2, :], in1=rl[:, jp % 2, :],
                                                      op=ALU.mult),
                      reads=[buf("rl%d" % (jp % 2))], writes=[buf("hid%d" % (jp % 3))])

            def fc2(jp):
                sl = (16 * i + jp) % 3

                def mm(e):
                    ins = None
                    for jj in range(2):
                        j = jp * 2 + jj
                        for s in range(2):
                            for hf in range(2):
                                ins = e.matmul(ps[:, 2 * s + hf, :],
                                               lhsT=hid[:, jp % 3, jj * T + s * 128:jj * T + (s + 1) * 128],
                                               rhs=fc2buf[:, sl, jj, hf * 512:(hf + 1) * 512],
                                               start=(j == 0), stop=(j == NJ - 1))
                    return ins
                sc.op(PE, mm, reads=[buf("hid%d" % (jp % 3)), buf("f2b%d" % sl)], writes=[Bps(b) for b in range(4)])

            for jp in range(16):
                fc1(jp)
                if jp >= 2:
                    fc2(jp - 2)
                    slab_load(16 * i + jp + 1)
                yield
            fc2(14)
            slab_load(16 * i + 17)
            fc2(15)
            slab_load(16 * i + 18)
            for s in range(2):
                for hf in range(2):
                    b_ = 2 * s + hf
                    sc.op(ACT, lambda e, b_=b_: e.activation(out=junk[:, 0:512], in_=ps[:, b_, :], func=AF.Square,
                                                             accum_out=sm[:, c_ssf + b_:c_ssf + b_ + 1]),
                          reads=[Bps(b_)], writes=[buf("junk"), buf("ssf%d" % b_)])
            for s in range(2):
                for hf in range(2):
                    b_ = 2 * s + hf
                    sc.op(DVE, lambda e, hf=hf, b_=b_: e.tensor_tensor(
                        out=stg4(b_), in0=ps[:, b_, :], in1=gp2[:, hf * 512:(hf + 1) * 512], op=ALU.mult),
                        reads=[Bps(b_), buf("gp2"), buf("ssf%d" % b_)], writes=[stg4_buf(b_)])
            sc.op(DVE, lambda e: e.tensor_tensor(out=sm[:, c_ssfs:c_ssfs + 2],
                                                 in0=sm[:, c_ssf:c_ssf + 4].rearrange("p (s h) -> p s h", h=2)[:, :, 0],
                                                 in1=sm[:, c_ssf:c_ssf + 4].rearrange("p (s h) -> p s h", h=2)[:, :, 1],
                                                 op=ALU.add),
                  reads=[buf("ssf%d" % b_) for b_ in range(4)], writes=[buf("ssfs")])
            chf()
            for s in range(2):
                for hf in range(2):
                    b_ = 2 * s + hf
                    sc.op(DVE, lambda e, s=s, hf=hf, b_=b_: e.scalar_tensor_tensor(
                        out=xs(slot, s)[:, hf * 512:(hf + 1) * 512], in0=stg4(b_), scalar=rf[:, s:s + 1],
                        in1=xs(slot, s)[:, hf * 512:(hf + 1) * 512], op0=ALU.mult, op1=ALU.add),
                        reads=[stg4_buf(b_), brf, XB[s]], writes=[XB[s]])
            dst = out_d[i * T:(i + 1) * T, :].rearrange("(s p) d -> p s d", p=128)
            srcv = xb[:, slot, :].rearrange("p (s d) -> p s d", s=2)
            ev = sc.dma(SP, lambda e: e.dma_start(out=dst, in_=srcv), "xs%d" % slot, reads=XB)
            stores.append(ev)
            yield

        stores = []
        if NT > 1:
            x_load(1)
        for _ in mixer(0):
            pass
        for i in range(NT):
            gm = mixer(i + 1) if i + 1 < NT else None
            step = 0
            for _ in ffn(i):
                step += 1
                if step == 4 and i + 2 < NT:
                    x_load(i + 2)
                if gm is not None and step >= 2:
                    try:
                        next(gm)
                    except StopIteration:
                        gm = None
            if gm is not None:
                for _ in gm:
                    pass
        sc.wait(SP, stores)

        all_keys = set()
        for e_ in ENGS:
            for (deps, fn, key, amt) in sc.ops[e_]:
                if key is not None:
                    all_keys.add(key)
        with contextlib.ExitStack() as stack:
            for k_ in sorted(all_keys):
                sems[k_] = stack.enter_context(nc.semaphore("s_" + k_))
            block = stack.enter_context(nc.Block())

            def run(eng_name, eng):
                waited = {}
                for (deps, fn, key, amt) in sc.ops[eng_name]:
                    for (k_, v_) in deps:
                        if waited.get(k_, 0) >= v_:
                            continue
                        eng.wait_ge(sems[k_], v_)
                        waited[k_] = v_
                    if fn is None:
                        continue
                    ins = fn(eng)
                    ins.then_inc(sems[key], amt)

            @block.tensor
            def _(e):
                run(PE, e)

            @block.scalar
            def _(e):
                run(ACT, e)

            @block.vector
            def _(e):
                run(DVE, e)

            @block.gpsimd
            def _(e):
                run(POOL, e)

            @block.sync
            def _(e):
                run(SP, e)
    return nc


def _host_inputs(inputs, NT=16):
    f = lambda a: np.ascontiguousarray(np.asarray(a, dtype=np.float32))
    S = NT * T
    x = f(inputs["x"])
    c = f(inputs["c"])
    pmaj = lambda w: np.ascontiguousarray(f(w).reshape(8, 128, -1).transpose(1, 0, 2).reshape(128, -1))
    col = lambda v, n: np.ascontiguousarray(f(v).reshape(n, 128).T)
    rc = np.zeros((4, 16), np.float32)
    for g in range(4):
        w = 2 << g
        for t in range(16):
            rc[g, t] = 1.0 / min(t + 1, w)
    shared = {
        "n1pb": np.ascontiguousarray(np.broadcast_to(f(inputs["norm1_post"])[None, :], (128, D))),
        "n2pb": np.ascontiguousarray(np.broadcast_to(f(inputs["norm2_post"])[None, :], (128, D))),
        "bspb": np.ascontiguousarray(np.broadcast_to(f(inputs["b_spatial"]).reshape(1, 512), (128, 512))),
        "mask": np.ascontiguousarray(np.tile(np.triu(np.ones((128, 128), np.float32)), (1, 4))),
        "wspT": np.ascontiguousarray(f(inputs["w_spatial"]).transpose(2, 0, 1).reshape(128, 512)),
        "ident": np.eye(128, dtype=np.float32),
        "wpool": np.ascontiguousarray(f(inputs["w_pool"]).transpose(1, 0, 2).reshape(128, 512)),
        "bada": f(inputs["b_ada"]).reshape(1, 6144),
        "wada": pmaj(inputs["w_ada"]),
        "w_in": pmaj(inputs["w_in"]),
        "w_out": pmaj(inputs["w_out"]),
        "w_fc1": pmaj(inputs["w_fc1"]),
        "w_fc2": f(inputs["w_fc2"]),
    }
    in_maps = []
    for b in range(x.shape[0]):
        cvec = np.zeros((128, NCV), np.float32)
        cvec[:, C_C:C_C + 8] = col(c[b], 8)
        cvec[:, C_N1:C_N1 + 8] = col(inputs["norm1_pre"], 8)
        cvec[:, C_N2:C_N2 + 8] = col(inputs["norm2_pre"], 8)
        cvec[:, C_LG:C_LG + 4] = col(inputs["ln_v_gain"], 4)
        cvec[:, C_LB:C_LB + 4] = col(inputs["ln_v_bias"], 4)
        cvec[:, C_BP:C_BP + 4] = col(np.asarray(inputs["b_pool"]).reshape(-1), 4)
        cvec[:, C_PS:C_PS + 4] = col(inputs["pool_scale"], 4)
        cvec[:, C_RC:C_RC + 64] = rc.reshape(1, 64)
        m = dict(shared)
        m["x"] = np.ascontiguousarray(x[b, :S])
        m["cvec"] = cvec
        in_maps.append(m)
    return in_maps


def kernel(**inputs):
    in_maps = _host_inputs(inputs, 16)
    nc = build(16)
    res = run_bass_kernel_spmd(nc, in_maps, core_ids=list(range(len(in_maps))))
    return np.stack([np.asarray(r["out"], dtype=np.float32) for r in res.results], axis=0)
```

```python
import numpy as np
import concourse.bass as bass
import concourse.mybir as mybir
from concourse.bass_utils import run_bass_kernel_spmd

F32 = mybir.dt.float32
BF16 = mybir.dt.bfloat16
I32 = mybir.dt.int32
AF = mybir.ActivationFunctionType
ALU = mybir.AluOpType

D = 1024
SEQ = 4096
T = 256
NJ = 32
EPS = 1e-6
PE, ACT, DVE, POOL, SP = "pe", "act", "dve", "pool", "sp"
ENGS = (PE, ACT, DVE, POOL, SP)

C_C, C_N1, C_N2, C_LG, C_LB, C_BP, C_PS, C_RC = 0, 8, 16, 24, 28, 32, 36, 40
NCV = 40 + 64


class Buf:
    __slots__ = ("name", "w", "r")

    def __init__(self, name):
        self.name = name
        self.w = None
        self.r = []


class Sched:
    def __init__(self):
        self.ops = {e: [] for e in ENGS}
        self.count = {}

    def _deps(self, eng, reads, writes):
        deps = []
        for b in reads:
            if b.w is not None:
                deps.append((b.w, "raw"))
        for b in writes:
            if b.w is not None:
                deps.append((b.w, "waw"))
            for r in b.r:
                deps.append((r, "war"))
        out = []
        for (ev, kind) in deps:
            if ev[0] == eng:
                if eng == PE:
                    continue
            out.append(ev)
        return out

    def _record(self, eng, fn, key, amt, reads, writes):
        deps = self._deps(eng, reads, writes)
        self.count[key] = self.count.get(key, 0) + amt
        ev = (key, self.count[key])
        self.ops[eng].append((deps, fn, key, amt))
        for b in reads:
            b.r.append(ev)
        for b in writes:
            b.w = ev
            b.r = []
        return ev

    def op(self, eng, fn, reads=(), writes=()):
        return self._record(eng, fn, eng, 1, reads, writes)

    def dma(self, queue, fn, key, reads=(), writes=()):
        return self._record(queue, fn, key, 16, reads, writes)

    def wait(self, eng, events):
        self.ops[eng].append((list(events), None, None, 0))


def build(NT=16):
    S = NT * T
    nc = bass.Bass("TRN2", target_bir_lowering=False)
    dt_in = lambda name, shape: nc.dram_tensor(name, shape, F32, kind="ExternalInput").ap()
    x_d = dt_in("x", [S, D])
    cvec_d = dt_in("cvec", [128, NCV])
    n1pb_d = dt_in("n1pb", [128, D])
    n2pb_d = dt_in("n2pb", [128, D])
    bspb_d = dt_in("bspb", [128, 512])
    mask_d = dt_in("mask", [128, 512])
    wspT_d = dt_in("wspT", [128, 512])
    ident_d = dt_in("ident", [128, 128])
    wpool_d = dt_in("wpool", [128, 512])
    bada_d = dt_in("bada", [1, 6144])
    wada_d = dt_in("wada", [128, 8 * 6144])
    win_d = dt_in("w_in", [128, 8 * 1536])
    wout_d = dt_in("w_out", [128, 8 * 1024])
    fc1_d = dt_in("w_fc1", [128, 8 * 4096])
    fc2_d = dt_in("w_fc2", [4096, D])
    out_d = nc.dram_tensor("out", [S, D], F32, kind="ExternalOutput").ap()
    fc2s_d = nc.dram_tensor("fc2s", [4096, D], BF16, kind="Internal").ap()

    sc = Sched()
    sems = {}

    import contextlib
    with contextlib.ExitStack() as _st:
        w_in_sb = _st.enter_context(nc.sbuf_tensor("w_in_sb", [128, 8, 1536], BF16))
        w_out_sb = _st.enter_context(nc.sbuf_tensor("w_out_sb", [128, 8, 1024], BF16))
        fc1_sb = _st.enter_context(nc.sbuf_tensor("fc1_sb", [128, 8, 4096], BF16))
        fc2buf = _st.enter_context(nc.sbuf_tensor("fc2buf", [128, 3, 2, 1024], BF16))
        wsp_sb = _st.enter_context(nc.sbuf_tensor("wsp_sb", [128, 4, 128], BF16))
        wpool_sb = _st.enter_context(nc.sbuf_tensor("wpool_sb", [128, 4, 128], BF16))
        ident = _st.enter_context(nc.sbuf_tensor("ident_sb", [128, 128], BF16))
        ones = _st.enter_context(nc.sbuf_tensor("ones_sb", [128, 128], F32))
        gp1 = _st.enter_context(nc.sbuf_tensor("gp1", [128, D], F32))
        gp2 = _st.enter_context(nc.sbuf_tensor("gp2", [128, D], F32))
        Bt = _st.enter_context(nc.sbuf_tensor("Bt", [128, 4, 128], F32))
        cv = _st.enter_context(nc.sbuf_tensor("cv", [128, NCV], F32))
        mc = _st.enter_context(nc.sbuf_tensor("mc", [128, 64], F32))
        sm = _st.enter_context(nc.sbuf_tensor("sm", [128, 96], F32))
        xb = _st.enter_context(nc.sbuf_tensor("xb", [128, 3, 2048], F32))
        hbf = _st.enter_context(nc.sbuf_tensor("hbf", [128, 2, 1024], BF16))
        hP = _st.enter_context(nc.sbuf_tensor("hP", [128, 2, 8, T], BF16))
        junk = _st.enter_context(nc.sbuf_tensor("junk", [128, 1024], BF16))
        ubf = _st.enter_context(nc.sbuf_tensor("ubf", [128, 4, T], BF16))
        vg = _st.enter_context(nc.sbuf_tensor("vg", [128, 2, 512], F32))
        vn = _st.enter_context(nc.sbuf_tensor("vn", [128, 2, 512], BF16))
        Z = _st.enter_context(nc.sbuf_tensor("Z", [128, 4, 272], F32))
        pt = _st.enter_context(nc.sbuf_tensor("pt", [128, 2, 272], F32))
        diff = _st.enter_context(nc.sbuf_tensor("diff", [128, 4, T], BF16))
        tmpS = _st.enter_context(nc.sbuf_tensor("tmpS", [128, 4, 128], F32))
        yT = _st.enter_context(nc.sbuf_tensor("yT", [128, 8, T], BF16))
        tmp = _st.enter_context(nc.sbuf_tensor("tmp", [128, 2, 512], F32))
        rl = _st.enter_context(nc.sbuf_tensor("rl", [128, 2, 512], F32))
        hid = _st.enter_context(nc.sbuf_tensor("hid", [128, 3, 512], BF16))
        ps = _st.enter_context(nc.psum_tensor("ps", [128, 8, 512], F32))
        B = {}

        def buf(name):
            if name not in B:
                B[name] = Buf(name)
            return B[name]

        def bank(b):
            return ps[:, b, :]

        def bank_bf(b):
            return ps[:, b, :].bitcast(BF16)

        def Bps(b):
            return buf("ps%d" % b)

        def stg4(ti):
            return tmp[:, ti, :] if ti < 2 else vg[:, ti - 2, :]

        def stg4_buf(ti):
            return buf("tmp%d" % ti) if ti < 2 else buf("vg%d" % (ti - 2))

        def xs(slot, s):
            return xb[:, slot, s * 1024:(s + 1) * 1024]

        sm_next = [0]

        def smcol(n):
            a = sm_next[0]
            sm_next[0] += n
            assert sm_next[0] <= 96
            return a

        def rstd_chain(src_ap, src_bufs, n, inv_d, tag):
            c0 = smcol(4 * n)
            vv = sm[:, c0:c0 + n]
            r = sm[:, c0 + n:c0 + 2 * n]
            t = sm[:, c0 + 2 * n:c0 + 3 * n]
            u = sm[:, c0 + 3 * n:c0 + 4 * n]
            bv, br, bt_, bu = (buf(tag + "_vv"), buf(tag + "_r"), buf(tag + "_t"), buf(tag + "_u"))

            def chain():
                sc.op(DVE, lambda e: e.tensor_scalar(out=vv, in0=src_ap, scalar1=inv_d, scalar2=EPS,
                                                     op0=ALU.mult, op1=ALU.add),
                      reads=src_bufs, writes=[bv])
                sc.op(DVE, lambda e: e.tensor_scalar(out=r.bitcast(I32), in0=vv.bitcast(I32), scalar1=-0.5,
                                                     scalar2=1597463007.0, op0=ALU.mult, op1=ALU.add),
                      reads=[bv], writes=[br])
                for _ in range(3):
                    sc.op(DVE, lambda e: e.tensor_tensor(out=t, in0=r, in1=r, op=ALU.mult),
                          reads=[br], writes=[bt_])
                    sc.op(DVE, lambda e: e.scalar_tensor_tensor(out=u, in0=t, scalar=-0.5, in1=vv,
                                                                op0=ALU.mult, op1=ALU.mult),
                          reads=[bt_, bv], writes=[bu])
                    sc.op(DVE, lambda e: e.scalar_tensor_tensor(out=r, in0=u, scalar=1.5, in1=r,
                                                                op0=ALU.add, op1=ALU.mult),
                          reads=[bu, br], writes=[br])
            return chain, r, br

        sc.op(POOL, lambda e: e.memset(ones[:], 1.0), writes=[buf("ones")])
        sc.op(POOL, lambda e: e.memset(Z[:, :, 0:16], 0.0), writes=[buf("Z")])

        sc.dma(SP, lambda e: e.dma_start(out=cv[:], in_=cvec_d[:, :]), "c_cv", writes=[buf("cv")])
        sc.dma(SP, lambda e: e.dma_start(out=gp1[:], in_=n1pb_d[:, :]), "c_g1", writes=[buf("gp1")])
        sc.dma(SP, lambda e: e.dma_start(out=gp2[:], in_=n2pb_d[:, :]), "c_g2", writes=[buf("gp2")])
        sc.dma(SP, lambda e: e.dma_start(out=Bt[:].rearrange("p h t -> p (h t)"), in_=bspb_d[:, :]), "c_bt",
               writes=[buf("Bt")])
        sc.dma(SP, lambda e: e.dma_start(out=vg[:, 0, :], in_=wspT_d[:, :]), "c_ws", writes=[buf("vg0")])
        sc.dma(SP, lambda e: e.dma_start(out=vg[:, 1, :], in_=mask_d[:, :]), "c_mk", writes=[buf("vg1")])
        sc.dma(POOL, lambda e: e.dma_start(out=ident[:], in_=ident_d[:, :]), "c_id", writes=[buf("ident")])
        sc.dma(POOL, lambda e: e.dma_start(out=wpool_sb[:].rearrange("p g d -> p (g d)"), in_=wpool_d[:, :]),
               "c_wp", writes=[buf("wpool")])
        def x_load(i):
            slot = i % 3
            src = x_d[i * T:(i + 1) * T, :].rearrange("(s p) d -> p s d", p=128)
            dst = xb[:, slot, :].rearrange("p (s d) -> p s d", s=2)
            sc.dma(SP, lambda e: e.dma_start(out=dst, in_=src), "xl%d" % slot,
                   writes=[buf("xb%d_0" % slot), buf("xb%d_1" % slot)])
        win_v = win_d.rearrange("p (k n) -> p k n", k=8)
        for hh in range(2):
            sc.dma(POOL, lambda e, hh=hh: e.dma_start(out=w_in_sb[:, hh * 4:(hh + 1) * 4, :],
                                                      in_=win_v[:, hh * 4:(hh + 1) * 4, :]),
                   "w_in%d" % hh, writes=[buf("w_in%d" % hh)])
        W_IN = [buf("w_in0"), buf("w_in1")]
        c_th = smcol(8); c_hf = smcol(8); c_sc = smcol(8)
        sc.op(ACT, lambda e: e.activation(out=sm[:, c_th:c_th + 8], in_=cv[:, C_C:C_C + 8], func=AF.Tanh, scale=0.5),
              reads=[buf("cv")], writes=[buf("s_th")])
        sc.op(DVE, lambda e: e.tensor_scalar(out=sm[:, c_hf:c_hf + 8], in0=sm[:, c_th:c_th + 8], scalar1=1.0,
                                             scalar2=0.5, op0=ALU.add, op1=ALU.mult),
              reads=[buf("s_th")], writes=[buf("s_hf")])
        sc.op(DVE, lambda e: e.tensor_tensor(out=sm[:, c_sc:c_sc + 8], in0=sm[:, c_hf:c_hf + 8],
                                             in1=cv[:, C_C:C_C + 8], op=ALU.mult),
              reads=[buf("s_hf"), buf("cv")], writes=[buf("s_sc")])
        scv = sm[:, c_sc:c_sc + 8]

        wada_v = wada_d.rearrange("p (k n) -> p k n", k=8)
        CH_COL = {0: 0, 1: 8, 3: 16, 4: 24}
        for b in range(24):
            st = b % 3
            stg = xb[:, st, :].rearrange("p (k c) -> p k c", c=256)
            stg_bufs = [buf("xb%d_0" % st), buf("xb%d_1" % st)]
            sc.dma(SP, lambda e, b=b, stg=stg: e.dma_start(out=stg, in_=wada_v[:, :, b * 256:(b + 1) * 256]),
                   "wa%d" % (b % 3), writes=stg_bufs)
            sc.dma(SP, lambda e, b=b: e.dma_start(out=rl[0:1, b % 2, 0:256], in_=bada_d[0:1, b * 256:(b + 1) * 256]),
                   "ba%d" % (b % 2), writes=[buf("rl%d" % (b % 2))])
            pr = 4 + b % 2

            def mm_mod(e, b=b, stg=stg, pr=pr):
                for k in range(8):
                    e.matmul(ps[0:1, pr, 0:256], lhsT=scv[:, k:k + 1], rhs=stg[:, k, :], start=(k == 0), stop=False)
                return e.matmul(ps[0:1, pr, 0:256], lhsT=ones[0:1, 0:1], rhs=rl[0:1, b % 2, 0:256], start=False, stop=True)
            sc.op(PE, mm_mod, reads=stg_bufs + [buf("s_sc"), buf("ones"), buf("rl%d" % (b % 2))],
                  writes=[Bps(pr), buf("modgate%d" % b)])
            sc.op(DVE, lambda e, b=b, pr=pr: e.tensor_copy(out=tmp[0:1, b % 2, 0:256], in_=ps[0:1, pr, 0:256]),
                  reads=[Bps(pr)], writes=[buf("tmp%d" % (b % 2))])
            chunk = b // 4
            if chunk in (2, 5):
                gp = gp1 if chunk == 2 else gp2
                gpb = buf("gp1") if chunk == 2 else buf("gp2")
                pb = 6 + b % 2
                cols = slice((b % 4) * 256, (b % 4) * 256 + 256)
                sc.op(PE, lambda e, b=b, pb=pb: e.matmul(ps[:, pb, 0:256], lhsT=ones[0:1, :], rhs=tmp[0:1, b % 2, 0:256],
                                                        start=True, stop=True),
                      reads=[buf("tmp%d" % (b % 2)), buf("ones")], writes=[Bps(pb)])
                sc.op(DVE, lambda e, gp=gp, pb=pb, cols=cols: e.tensor_tensor(out=gp[:, cols], in0=ps[:, pb, 0:256],
                                                                          in1=gp[:, cols], op=ALU.mult),
                      reads=[Bps(pb), gpb], writes=[gpb])
            else:
                col0 = CH_COL[chunk] + (b % 4) * 2

                def mm_col(e, b=b, col0=col0):
                    ins = None
                    for q in range(2):
                        ins = e.matmul(ps[:, 0, col0 + q:col0 + q + 1], lhsT=tmp[0:1, b % 2, q * 128:(q + 1) * 128],
                                       rhs=ones[0:1, 0:1], start=True, stop=True)
                    return ins
                sc.op(PE, mm_col, reads=[buf("tmp%d" % (b % 2)), buf("ones")], writes=[Bps(0)])
        sc.dma(POOL, lambda e: e.dma_start(out=w_out_sb[:], in_=wout_d.rearrange("p (k n) -> p k n", k=8)),
               "w_out", reads=[buf("modgate8")], writes=[buf("w_out")])
        fc1_v = fc1_d.rearrange("p (k n) -> p k n", k=8)
        for q in range(4):
            sc.dma(POOL, lambda e, q=q: e.dma_start(out=fc1_sb[:, 2 * q:2 * q + 2, :], in_=fc1_v[:, 2 * q:2 * q + 2, :]),
                   "fc1_%d" % q, reads=[buf("modgate%d" % (12 + 4 * q if q < 3 else 23))], writes=[buf("fc1_%d" % q)])
        FC1 = [buf("fc1_%d" % q) for q in range(4)]

        x_load(0)
        sc.op(DVE, lambda e: e.tensor_copy(out=mc[:, 0:32], in_=ps[:, 0, 0:32]), reads=[Bps(0)], writes=[buf("mc_raw")])
        sc.op(DVE, lambda e: e.scalar_tensor_tensor(out=mc[:, 32:40], in0=mc[:, 8:16], scalar=1.0,
                                                    in1=cv[:, C_N1:C_N1 + 8], op0=ALU.add, op1=ALU.mult),
              reads=[buf("mc_raw"), buf("cv")], writes=[buf("mc_g1")])
        sc.op(DVE, lambda e: e.scalar_tensor_tensor(out=mc[:, 40:48], in0=mc[:, 24:32], scalar=1.0,
                                                    in1=cv[:, C_N2:C_N2 + 8], op0=ALU.add, op1=ALU.mult),
              reads=[buf("mc_raw"), buf("cv")], writes=[buf("mc_g2")])
        MODB = [buf("mc_raw"), buf("mc_g1"), buf("mc_g2")]
        G1, SH1, G2, SH2 = 32, 0, 40, 16

        sc.op(DVE, lambda e: e.tensor_tensor(out=vg[:, 0, :], in0=vg[:, 0, :], in1=vg[:, 1, :], op=ALU.mult),
              reads=[buf("vg0"), buf("vg1")], writes=[buf("vg0")])
        sc.op(ACT, lambda e: e.activation(out=wsp_sb[:].rearrange("p h t -> p (h t)"), in_=vg[:, 0, :], func=AF.Identity),
              reads=[buf("vg0")], writes=[buf("wsp")])
        sc.op(PE, lambda e: e.matmul(ps[:, 1, :], lhsT=ones[:, :], rhs=vg[:, 0, :], start=True, stop=True),
              reads=[buf("vg0"), buf("ones")], writes=[Bps(1)])

        def bt_fix(e):
            ins = None
            for h in range(4):
                ins = e.scalar_tensor_tensor(out=Bt[:, h, :], in0=ps[:, 1, h * 128:(h + 1) * 128],
                                             scalar=cv[:, C_LB + h:C_LB + h + 1], in1=Bt[:, h, :],
                                             op0=ALU.mult, op1=ALU.add)
            return ins
        sc.op(DVE, bt_fix, reads=[Bps(1), buf("cv"), buf("Bt")], writes=[buf("Bt")])

        c_ss1 = smcol(2)
        ch1, r1, br1 = rstd_chain(sm[:, c_ss1:c_ss1 + 2], [buf("ss1_0"), buf("ss1_1")], 2, 1.0 / D, "r1")
        c_vs = smcol(2); c_vq = smcol(2); c_mean = smcol(2); c_msq = smcol(2); c_var = smcol(2)
        chv, rv, brv = rstd_chain(sm[:, c_var:c_var + 2], [buf("var")], 2, 1.0, "rv")
        c_ssm = smcol(2); c_ssms = smcol(1)
        chm, rm, brm = rstd_chain(sm[:, c_ssms:c_ssms + 1], [buf("ssms")], 1, 1.0 / D, "rm")
        c_ss2 = smcol(2)
        ch2, r2, br2 = rstd_chain(sm[:, c_ss2:c_ss2 + 2], [buf("ss2_0"), buf("ss2_1")], 2, 1.0 / D, "r2")
        c_ssf = smcol(4); c_ssfs = smcol(2)
        chf, rf, brf = rstd_chain(sm[:, c_ssfs:c_ssfs + 2], [buf("ssfs")], 2, 1.0 / D, "rf")

        MB = (6, 7)
        FB = (4, 5)

        def norm_A(slot, ss_col, ss_name, chain, r_ap, r_buf):
            for s in range(2):
                sc.op(ACT, lambda e, s=s: e.activation(out=junk[:], in_=xs(slot, s), func=AF.Square,
                                                       accum_out=sm[:, ss_col + s:ss_col + s + 1]),
                      reads=[buf("xb%d_%d" % (slot, s))], writes=[buf("junk"), buf("%s_%d" % (ss_name, s))])
            chain()

        def norm_A2(slot, r_ap, r_buf):
            for s in range(2):
                sc.op(ACT, lambda e, s=s: e.activation(out=hbf[:, s, :], in_=xs(slot, s), func=AF.Identity,
                                                       scale=r_ap[:, s:s + 1]),
                      reads=[buf("xb%d_%d" % (slot, s)), r_buf], writes=[buf("hbf_%d" % s)])

        def norm_B(dstT, dstT_buf, gcol, shcol):
            for s in range(2):
                mb = MB[s]

                def tr(e, s=s, mb=mb):
                    ins = None
                    for k in range(8):
                        ins = e.transpose(out=bank_bf(mb)[:, k * 128:(k + 1) * 128],
                                          in_=hbf[:, s, k * 128:(k + 1) * 128], identity=ident[:])
                    return ins
                sc.op(PE, tr, reads=[buf("hbf_%d" % s), buf("ident")], writes=[Bps(mb)])

                if s == 0:
                    def ev(e, s=s, mb=mb):
                        ins = None
                        for k in range(8):
                            ins = e.activation(out=dstT[:, k, s * 128:(s + 1) * 128],
                                               in_=bank_bf(mb)[:, k * 128:(k + 1) * 128], func=AF.Identity,
                                               scale=mc[:, gcol + k:gcol + k + 1], bias=mc[:, shcol + k:shcol + k + 1])
                        return ins
                    sc.op(ACT, ev, reads=[Bps(mb)] + MODB, writes=[buf(dstT_buf + "_%d" % s)])
                else:
                    def ev(e, s=s, mb=mb):
                        ins = None
                        for k in range(8):
                            ins = e.tensor_scalar(out=dstT[:, k, s * 128:(s + 1) * 128],
                                                  in0=bank_bf(mb)[:, k * 128:(k + 1) * 128],
                                                  scalar1=mc[:, gcol + k:gcol + k + 1],
                                                  scalar2=mc[:, shcol + k:shcol + k + 1], op0=ALU.mult, op1=ALU.add)
                        return ins
                    sc.op(DVE, ev, reads=[Bps(mb)] + MODB, writes=[buf(dstT_buf + "_%d" % s)])

        def mixer(i):
            slot = i % 3
            XB = [buf("xb%d_0" % slot), buf("xb%d_1" % slot)]
            hT = hP[:, i % 2]
            HT = [buf("hP%d_0" % (i % 2)), buf("hP%d_1" % (i % 2))]
            norm_A(slot, c_ss1, "ss1", ch1, r1, br1)
            yield
            yield
            norm_A2(slot, r1, br1)
            yield
            norm_B(hT, "hP%d" % (i % 2), G1, SH1)
            yield
            if i > 0:
                sc.op(DVE, lambda e: e.tensor_copy(out=Z[:, :, 0:16], in_=Z[:, :, 256:272]),
                      reads=[buf("Z")], writes=[buf("Z")])
            for gp_ in range(2):
                mb = MB[gp_]

                def mm_z(e, gp_=gp_, mb=mb):
                    ins = None
                    for gg in range(2):
                        g = gp_ * 2 + gg
                        for k in range(8):
                            ins = e.matmul(ps[:, mb, gg * T:(gg + 1) * T],
                                           lhsT=w_in_sb[:, k, 1024 + g * 128:1024 + (g + 1) * 128],
                                           rhs=hT[:, k, :], start=(k == 0), stop=(k == 7))
                    return ins
                sc.op(PE, mm_z, reads=HT + W_IN, writes=[Bps(mb)])
                sc.op(ACT, lambda e, gp_=gp_, mb=mb: e.activation(
                    out=Z[:, 2 * gp_:2 * gp_ + 2, 16:272], in_=ps[:, mb, :].rearrange("p (g t) -> p g t", g=2),
                    func=AF.Identity), reads=[Bps(mb)], writes=[buf("Z")])
            yield
            def pooling(g):
                m = g + 1
                w = 1 << m
                src = Z[:, g, :]
                src_b = buf("Z")
                for k in range(m):
                    lo = (1 << (k + 1)) - 1
                    sh = 1 << k
                    dst = pt[:, k % 2, :]
                    dst_b = buf("pt%d" % (k % 2))
                    sc.op(DVE, lambda e, src=src, dst=dst, lo=lo, sh=sh: e.tensor_tensor(
                        out=dst[:, lo:272], in0=src[:, lo:272], in1=src[:, lo - sh:272 - sh], op=ALU.add),
                        reads=[src_b], writes=[dst_b])
                    src, src_b = dst, dst_b
                sc.op(DVE, lambda e, src=src, g=g, w=w: e.scalar_tensor_tensor(
                    out=diff[:, g, :], in0=src[:, 16:272], scalar=1.0 / w, in1=Z[:, g, 16:272],
                    op0=ALU.mult, op1=ALU.subtract), reads=[src_b, buf("Z")], writes=[buf("diff")])
                if i == 0:
                    oth = pt[:, (m % 2), 0:16]
                    oth_b = buf("pt%d" % (m % 2))
                    sc.op(DVE, lambda e, src=src, g=g, oth=oth: e.tensor_tensor(
                        out=oth, in0=src[:, 16:32], in1=cv[:, C_RC + 16 * g:C_RC + 16 * g + 16], op=ALU.mult),
                        reads=[src_b, buf("cv")], writes=[oth_b])
                    sc.op(DVE, lambda e, g=g, oth=oth: e.tensor_tensor(
                        out=diff[:, g, 0:16], in0=oth, in1=Z[:, g, 16:32], op=ALU.subtract),
                        reads=[oth_b, buf("Z"), buf("diff")], writes=[buf("diff")])
            for cp in range(2):
                mb = MB[cp]

                def mm_u(e, cp=cp, mb=mb):
                    ins = None
                    for cc in range(2):
                        c = cp * 2 + cc
                        for k in range(8):
                            ins = e.matmul(ps[:, mb, cc * T:(cc + 1) * T], lhsT=w_in_sb[:, k, c * 128:(c + 1) * 128],
                                           rhs=hT[:, k, :], start=(k == 0), stop=(k == 7))
                    return ins
                sc.op(PE, mm_u, reads=HT + W_IN, writes=[Bps(mb)])
                sc.op(ACT, lambda e, cp=cp, mb=mb: e.activation(
                    out=ubf[:, 2 * cp:2 * cp + 2, :].rearrange("p c t -> p (c t)"), in_=ps[:, mb, :],
                    func=AF.Gelu_apprx_tanh), reads=[Bps(mb)], writes=[buf("ubf%d" % cp)])
            pooling(0)
            pooling(1)
            for s in range(2):
                mb = MB[s]

                def mm_v(e, s=s, mb=mb):
                    ins = None
                    for k in range(8):
                        ins = e.matmul(ps[:, mb, :], lhsT=hT[:, k, s * 128:(s + 1) * 128], rhs=w_in_sb[:, k, 512:1024],
                                       start=(k == 0), stop=(k == 7))
                    return ins
                sc.op(PE, mm_v, reads=HT + W_IN, writes=[Bps(mb)])
                sc.op(ACT, lambda e, s=s, mb=mb: e.activation(out=vg[:, s, :], in_=ps[:, mb, :], func=AF.Gelu_apprx_tanh,
                                                              accum_out=sm[:, c_vs + s:c_vs + s + 1]),
                      reads=[Bps(mb)], writes=[buf("vg%d" % s), buf("vs%d" % s)])
                sc.op(ACT, lambda e, s=s: e.activation(out=junk[:, 0:512], in_=vg[:, s, :], func=AF.Square,
                                                       accum_out=sm[:, c_vq + s:c_vq + s + 1]),
                      reads=[buf("vg%d" % s)], writes=[buf("junk"), buf("vq%d" % s)])
            sc.op(DVE, lambda e: e.tensor_scalar(out=sm[:, c_mean:c_mean + 2], in0=sm[:, c_vs:c_vs + 2],
                                                 scalar1=1.0 / 512, scalar2=None, op0=ALU.mult),
                  reads=[buf("vs0"), buf("vs1")], writes=[buf("mean")])
            sc.op(DVE, lambda e: e.tensor_tensor(out=sm[:, c_msq:c_msq + 2], in0=sm[:, c_mean:c_mean + 2],
                                                 in1=sm[:, c_mean:c_mean + 2], op=ALU.mult),
                  reads=[buf("mean")], writes=[buf("msq")])
            sc.op(DVE, lambda e: e.scalar_tensor_tensor(out=sm[:, c_var:c_var + 2], in0=sm[:, c_vq:c_vq + 2],
                                                        scalar=1.0 / 512, in1=sm[:, c_msq:c_msq + 2],
                                                        op0=ALU.mult, op1=ALU.subtract),
                  reads=[buf("vq0"), buf("vq1"), buf("msq")], writes=[buf("var")])
            chv()
            for s in range(2):
                sc.op(DVE, lambda e, s=s: e.tensor_scalar(out=vn[:, s, :], in0=vg[:, s, :],
                                                           scalar1=sm[:, c_mean + s:c_mean + s + 1],
                                                           scalar2=rv[:, s:s + 1], op0=ALU.subtract, op1=ALU.mult),
                      reads=[buf("vg%d" % s), buf("mean"), brv], writes=[buf("vn%d" % s)])
            yield
            pooling(2)
            pooling(3)
            yield
            for s in range(2):
                mb = MB[s]

                def mm_s(e, s=s, mb=mb):
                    ins = None
                    for h in range(4):
                        ins = e.matmul(ps[:, mb, h * 128:(h + 1) * 128], lhsT=vn[:, s, h * 128:(h + 1) * 128],
                                       rhs=wsp_sb[:, h, :], start=True, stop=True)
                    return ins
                sc.op(PE, mm_s, reads=[buf("vn%d" % s), buf("wsp")], writes=[Bps(mb)])

                def ev_s(e, s=s, mb=mb):
                    ins = None
                    for h in range(4):
                        ins = e.scalar_tensor_tensor(out=tmpS[:, h, :], in0=ps[:, mb, h * 128:(h + 1) * 128],
                                                     scalar=cv[:, C_LG + h:C_LG + h + 1], in1=Bt[:, h, :],
                                                     op0=ALU.mult, op1=ALU.add)
                    return ins
                sc.op(DVE, ev_s, reads=[Bps(mb), buf("cv"), buf("Bt")], writes=[buf("tmpS")])
                sc.op(POOL, lambda e, s=s: e.tensor_tensor(out=yT[:, 0:4, s * 128:(s + 1) * 128], in0=tmpS[:],
                                                           in1=ubf[:, :, s * 128:(s + 1) * 128], op=ALU.mult),
                      reads=[buf("tmpS"), buf("ubf0"), buf("ubf1")], writes=[buf("yTa%d" % s)])
            yield
            for gp_ in range(2):
                mb = MB[gp_]

                def mm_p(e, gp_=gp_, mb=mb):
                    ins = None
                    for gg in range(2):
                        g = gp_ * 2 + gg
                        ins = e.matmul(ps[:, mb, gg * T:(gg + 1) * T], lhsT=wpool_sb[:, g, :], rhs=diff[:, g, :],
                                       start=True, stop=True)
                    return ins
                sc.op(PE, mm_p, reads=[buf("diff"), buf("wpool")], writes=[Bps(mb)])

                def ev_p(e, gp_=gp_, mb=mb):
                    ins = None
                    for gg in range(2):
                        g = gp_ * 2 + gg
                        ins = e.tensor_scalar(out=yT[:, 4 + g, :], in0=ps[:, mb, gg * T:(gg + 1) * T],
                                              scalar1=cv[:, C_BP + g:C_BP + g + 1], scalar2=cv[:, C_PS + g:C_PS + g + 1],
                                              op0=ALU.add, op1=ALU.mult)
                    return ins
                sc.op(DVE, ev_p, reads=[Bps(mb), buf("cv")], writes=[buf("yTb%d" % gp_)])
            yield
            YT = [buf("yTa0"), buf("yTa1"), buf("yTb0"), buf("yTb1")]
            for s in range(2):
                for hf in range(2):
                    mb = MB[hf]

                    def mm_o(e, s=s, hf=hf, mb=mb):
                        ins = None
                        for k in range(8):
                            ins = e.matmul(ps[:, mb, :], lhsT=yT[:, k, s * 128:(s + 1) * 128],
                                           rhs=w_out_sb[:, k, hf * 512:(hf + 1) * 512], start=(k == 0), stop=(k == 7))
                        return ins
                    sc.op(PE, mm_o, reads=YT + [buf("w_out")], writes=[Bps(mb)])
                    sc.op(ACT, lambda e, hf=hf, mb=mb: e.activation(out=junk[:, 0:512], in_=ps[:, mb, :], func=AF.Square,
                                                                    accum_out=sm[:, c_ssm + hf:c_ssm + hf + 1]),
                          reads=[Bps(mb)], writes=[buf("junk"), buf("ssm%d" % hf)])
                for hf in range(2):
                    mb = MB[hf]
                    ti = 2 * s + hf
                    sc.op(DVE, lambda e, hf=hf, mb=mb, ti=ti: e.tensor_tensor(
                        out=stg4(ti), in0=ps[:, mb, :], in1=gp1[:, hf * 512:(hf + 1) * 512], op=ALU.mult),
                        reads=[Bps(mb), buf("gp1"), buf("ssm%d" % hf)], writes=[stg4_buf(ti)])
                sc.op(DVE, lambda e: e.tensor_tensor(out=sm[:, c_ssms:c_ssms + 1], in0=sm[:, c_ssm:c_ssm + 1],
                                                     in1=sm[:, c_ssm + 1:c_ssm + 2], op=ALU.add),
                      reads=[buf("ssm0"), buf("ssm1")], writes=[buf("ssms")])
                chm()
                for hf in range(2):
                    ti = 2 * s + hf
                    sc.op(DVE, lambda e, s=s, hf=hf, ti=ti: e.scalar_tensor_tensor(
                        out=xs(slot, s)[:, hf * 512:(hf + 1) * 512], in0=stg4(ti), scalar=rm[:, 0:1],
                        in1=xs(slot, s)[:, hf * 512:(hf + 1) * 512], op0=ALU.mult, op1=ALU.add),
                        reads=[stg4_buf(ti), brm, XB[s]], writes=[XB[s]])
                yield
            yield
            norm_A(slot, c_ss2, "ss2", ch2, r2, br2)
            yield
            norm_A2(slot, r2, br2)
            yield
            norm_B(hT, "hP%d" % (i % 2), G2, SH2)
            yield

        def slab_load(G):
            i, q = divmod(G, 16)
            if i >= NT:
                return
            sl = G % 3
            rows = slice(q * 256, (q + 1) * 256)
            if i == 0:
                sc.dma(POOL, lambda e: e.dma_start(out=fc2buf[:, sl, :, :],
                                                   in_=fc2_d[rows, :].rearrange("(j p) n -> p j n", p=128)),
                       "f2p_%d" % sl, writes=[buf("f2b%d" % sl)])
                sc.dma(SP, lambda e: e.dma_start(out=fc2s_d[rows, :].rearrange("(j p) n -> p j n", p=128),
                                                 in_=fc2buf[:, sl, :, :]),
                       "f2w%d" % sl, reads=[buf("f2b%d" % sl)], writes=[buf("f2s%d" % q)])
            else:
                sc.dma(SP, lambda e: e.dma_start(out=fc2buf[:, sl, :, :],
                                                 in_=fc2s_d[rows, :].rearrange("(j p) n -> p j n", p=128)),
                       "f2_%d" % sl, reads=[buf("f2s%d" % q)], writes=[buf("f2b%d" % sl)])

        def ffn(i):
            slot = i % 3
            XB = [buf("xb%d_0" % slot), buf("xb%d_1" % slot)]
            h2T = hP[:, i % 2]
            H2T = [buf("hP%d_0" % (i % 2)), buf("hP%d_1" % (i % 2))]
            if i == 0:
                slab_load(0)
                slab_load(1)
                slab_load(2)

            def fc1(jp):
                fb = FB[jp % 2]

                def mm(e):
                    ins = None
                    for jj in range(2):
                        j = jp * 2 + jj
                        for k in range(8):
                            ins = e.matmul(ps[:, fb, jj * T:(jj + 1) * T], lhsT=fc1_sb[:, k, j * 128:(j + 1) * 128],
                                           rhs=h2T[:, k, :], start=(k == 0), stop=(k == 7))
                    return ins
                sc.op(PE, mm, reads=H2T + FC1, writes=[Bps(fb)])
                sc.op(ACT, lambda e: e.activation(out=rl[:, jp % 2, :], in_=ps[:, fb, :], func=AF.Relu),
                      reads=[Bps(fb)], writes=[buf("rl%d" % (jp % 2))])
                sc.op(POOL, lambda e: e.tensor_tensor(out=hid[:, jp % 3, :], in0=rl[:, jp % 2, :], in1=rl[:, jp % 2, :],
                                                      op=ALU.mult),
                      reads=[buf("rl%d" % (jp % 2))], writes=[buf("hid%d" % (jp % 3))])

            def fc2(jp):
                sl = (16 * i + jp) % 3

                def mm(e):
                    ins = None
                    for jj in range(2):
                        j = jp * 2 + jj
                        for s in range(2):
                            for hf in range(2):
                                ins = e.matmul(ps[:, 2 * s + hf, :],
                                               lhsT=hid[:, jp % 3, jj * T + s * 128:jj * T + (s + 1) * 128],
                                               rhs=fc2buf[:, sl, jj, hf * 512:(hf + 1) * 512],
                                               start=(j == 0), stop=(j == NJ - 1))
                    return ins
                sc.op(PE, mm, reads=[buf("hid%d" % (jp % 3)), buf("f2b%d" % sl)], writes=[Bps(b) for b in range(4)])

            for jp in range(16):
                fc1(jp)
                if jp >= 2:
                    fc2(jp - 2)
                    slab_load(16 * i + jp + 1)
                yield
            fc2(14)
            slab_load(16 * i + 17)
            fc2(15)
            slab_load(16 * i + 18)
            for s in range(2):
                for hf in range(2):
                    b_ = 2 * s + hf
                    sc.op(ACT, lambda e, b_=b_: e.activation(out=junk[:, 0:512], in_=ps[:, b_, :], func=AF.Square,
                                                             accum_out=sm[:, c_ssf + b_:c_ssf + b_ + 1]),
                          reads=[Bps(b_)], writes=[buf("junk"), buf("ssf%d" % b_)])
            for s in range(2):
                for hf in range(2):
                    b_ = 2 * s + hf
                    sc.op(DVE, lambda e, hf=hf, b_=b_: e.tensor_tensor(
                        out=stg4(b_), in0=ps[:, b_, :], in1=gp2[:, hf * 512:(hf + 1) * 512], op=ALU.mult),
                        reads=[Bps(b_), buf("gp2"), buf("ssf%d" % b_)], writes=[stg4_buf(b_)])
            sc.op(DVE, lambda e: e.tensor_tensor(out=sm[:, c_ssfs:c_ssfs + 2],
                                                 in0=sm[:, c_ssf:c_ssf + 4].rearrange("p (s h) -> p s h", h=2)[:, :, 0],
                                                 in1=sm[:, c_ssf:c_ssf + 4].rearrange("p (s h) -> p s h", h=2)[:, :, 1],
                                                 op=ALU.add),
                  reads=[buf("ssf%d" % b_) for b_ in range(4)], writes=[buf("ssfs")])
            chf()
            for s in range(2):
                for hf in range(2):
                    b_ = 2 * s + hf
                    sc.op(DVE, lambda e, s=s, hf=hf, b_=b_: e.scalar_tensor_tensor(
                        out=xs(slot, s)[:, hf * 512:(hf + 1) * 512], in0=stg4(b_), scalar=rf[:, s:s + 1],
                        in1=xs(slot, s)[:, hf * 512:(hf + 1) * 512], op0=ALU.mult, op1=ALU.add),
                        reads=[stg4_buf(b_), brf, XB[s]], writes=[XB[s]])
            dst = out_d[i * T:(i + 1) * T, :].rearrange("(s p) d -> p s d", p=128)
            srcv = xb[:, slot, :].rearrange("p (s d) -> p s d", s=2)
            ev = sc.dma(SP, lambda e: e.dma_start(out=dst, in_=srcv), "xs%d" % slot, reads=XB)
            stores.append(ev)
            yield

        stores = []
        if NT > 1:
            x_load(1)
        for _ in mixer(0):
            pass
        for i in range(NT):
            gm = mixer(i + 1) if i + 1 < NT else None
            step = 0
            for _ in ffn(i):
                step += 1
                if step == 4 and i + 2 < NT:
                    x_load(i + 2)
                if gm is not None and step >= 2:
                    try:
                        next(gm)
                    except StopIteration:
                        gm = None
            if gm is not None:
                for _ in gm:
                    pass
        sc.wait(SP, stores)

        all_keys = set()
        for e_ in ENGS:
            for (deps, fn, key, amt) in sc.ops[e_]:
                if key is not None:
                    all_keys.add(key)
        with contextlib.ExitStack() as stack:
            for k_ in sorted(all_keys):
                sems[k_] = stack.enter_context(nc.semaphore("s_" + k_))
            block = stack.enter_context(nc.Block())

            def run(eng_name, eng):
                waited = {}
                for (deps, fn, key, amt) in sc.ops[eng_name]:
                    for (k_, v_) in deps:
                        if waited.get(k_, 0) >= v_:
                            continue
                        eng.wait_ge(sems[k_], v_)
                        waited[k_] = v_
                    if fn is None:
                        continue
                    ins = fn(eng)
                    ins.then_inc(sems[key], amt)

            @block.tensor
            def _(e):
                run(PE, e)

            @block.scalar
            def _(e):
                run(ACT, e)

            @block.vector
            def _(e):
                run(DVE, e)

            @block.gpsimd
            def _(e):
                run(POOL, e)

            @block.sync
            def _(e):
                run(SP, e)
    return nc


def _host_inputs(inputs, NT=16):
    f = lambda a: np.ascontiguousarray(np.asarray(a, dtype=np.float32))
    S = NT * T
    x = f(inputs["x"])
    c = f(inputs["c"])
    pmaj = lambda w: np.ascontiguousarray(f(w).reshape(8, 128, -1).transpose(1, 0, 2).reshape(128, -1))
    col = lambda v, n: np.ascontiguousarray(f(v).reshape(n, 128).T)
    rc = np.zeros((4, 16), np.float32)
    for g in range(4):
        w = 2 << g
        for t in range(16):
            rc[g, t] = 1.0 / min(t + 1, w)
    shared = {
        "n1pb": np.ascontiguousarray(np.broadcast_to(f(inputs["norm1_post"])[None, :], (128, D))),
        "n2pb": np.ascontiguousarray(np.broadcast_to(f(inputs["norm2_post"])[None, :], (128, D))),
        "bspb": np.ascontiguousarray(np.broadcast_to(f(inputs["b_spatial"]).reshape(1, 512), (128, 512))),
        "mask": np.ascontiguousarray(np.tile(np.triu(np.ones((128, 128), np.float32)), (1, 4))),
        "wspT": np.ascontiguousarray(f(inputs["w_spatial"]).transpose(2, 0, 1).reshape(128, 512)),
        "ident": np.eye(128, dtype=np.float32),
        "wpool": np.ascontiguousarray(f(inputs["w_pool"]).transpose(1, 0, 2).reshape(128, 512)),
        "bada": f(inputs["b_ada"]).reshape(1, 6144),
        "wada": pmaj(inputs["w_ada"]),
        "w_in": pmaj(inputs["w_in"]),
        "w_out": pmaj(inputs["w_out"]),
        "w_fc1": pmaj(inputs["w_fc1"]),
        "w_fc2": f(inputs["w_fc2"]),
    }
    in_maps = []
    for b in range(x.shape[0]):
        cvec = np.zeros((128, NCV), np.float32)
        cvec[:, C_C:C_C + 8] = col(c[b], 8)
        cvec[:, C_N1:C_N1 + 8] = col(inputs["norm1_pre"], 8)
        cvec[:, C_N2:C_N2 + 8] = col(inputs["norm2_pre"], 8)
        cvec[:, C_LG:C_LG + 4] = col(inputs["ln_v_gain"], 4)
        cvec[:, C_LB:C_LB + 4] = col(inputs["ln_v_bias"], 4)
        cvec[:, C_BP:C_BP + 4] = col(np.asarray(inputs["b_pool"]).reshape(-1), 4)
        cvec[:, C_PS:C_PS + 4] = col(inputs["pool_scale"], 4)
        cvec[:, C_RC:C_RC + 64] = rc.reshape(1, 64)
        m = dict(shared)
        m["x"] = np.ascontiguousarray(x[b, :S])
        m["cvec"] = cvec
        in_maps.append(m)
    return in_maps


def kernel(**inputs):
    in_maps = _host_inputs(inputs, 16)
    nc = build(16)
    res = run_bass_kernel_spmd(nc, in_maps, core_ids=list(range(len(in_maps))))
    return np.stack([np.asarray(r["out"], dtype=np.float32) for r in res.results], axis=0)
```

```python
import numpy as np
import concourse.bass as bass
import concourse.mybir as mybir
from concourse.bass_utils import run_bass_kernel_spmd

F32 = mybir.dt.float32
BF16 = mybir.dt.bfloat16
I32 = mybir.dt.int32
AF = mybir.ActivationFunctionType
ALU = mybir.AluOpType

D = 1024
SEQ = 4096
T = 256
NJ = 32
EPS = 1e-6
PE, ACT, DVE, POOL, SP = "pe", "act", "dve", "pool", "sp"
ENGS = (PE, ACT, DVE, POOL, SP)

C_C, C_N1, C_N2, C_LG, C_LB, C_BP, C_PS, C_RC = 0, 8, 16, 24, 28, 32, 36, 40
NCV = 40 + 64


class Buf:
    __slots__ = ("name", "w", "r")

    def __init__(self, name):
        self.name = name
        self.w = None
        self.r = []


class Sched:
    def __init__(self):
        self.ops = {e: [] for e in ENGS}
        self.count = {}

    def _deps(self, eng, reads, writes):
        deps = []
        for b in reads:
            if b.w is not None:
                deps.append((b.w, "raw"))
        for b in writes:
            if b.w is not None:
                deps.append((b.w, "waw"))
            for r in b.r:
                deps.append((r, "war"))
        out = []
        for (ev, kind) in deps:
            if ev[0] == eng:
                if eng == PE:
                    continue
            out.append(ev)
        return out

    def _record(self, eng, fn, key, amt, reads, writes):
        deps = self._deps(eng, reads, writes)
        self.count[key] = self.count.get(key, 0) + amt
        ev = (key, self.count[key])
        self.ops[eng].append((deps, fn, key, amt))
        for b in reads:
            b.r.append(ev)
        for b in writes:
            b.w = ev
            b.r = []
        return ev

    def op(self, eng, fn, reads=(), writes=()):
        return self._record(eng, fn, eng, 1, reads, writes)

    def dma(self, queue, fn, key, reads=(), writes=()):
        return self._record(queue, fn, key, 16, reads, writes)

    def wait(self, eng, events):
        self.ops[eng].append((list(events), None, None, 0))


def build(NT=16):
    S = NT * T
    nc = bass.Bass("TRN2", target_bir_lowering=False)
    dt_in = lambda name, shape: nc.dram_tensor(name, shape, F32, kind="ExternalInput").ap()
    x_d = dt_in("x", [S, D])
    cvec_d = dt_in("cvec", [128, NCV])
    n1pb_d = dt_in("n1pb", [128, D])
    n2pb_d = dt_in("n2pb", [128, D])
    bspb_d = dt_in("bspb", [128, 512])
    mask_d = dt_in("mask", [128, 512])
    wspT_d = dt_in("wspT", [128, 512])
    ident_d = dt_in("ident", [128, 128])
    wpool_d = dt_in("wpool", [128, 512])
    bada_d = dt_in("bada", [1, 6144])
    wada_d = dt_in("wada", [128, 8 * 6144])
    win_d = dt_in("w_in", [128, 8 * 1536])
    wout_d = dt_in("w_out", [128, 8 * 1024])
    fc1_d = dt_in("w_fc1", [128, 8 * 4096])
    fc2_d = dt_in("w_fc2", [4096, D])
    out_d = nc.dram_tensor("out", [S, D], F32, kind="ExternalOutput").ap()
    fc2s_d = nc.dram_tensor("fc2s", [4096, D], BF16, kind="Internal").ap()

    sc = Sched()
    sems = {}

    import contextlib
    with contextlib.ExitStack() as _st:
        w_in_sb = _st.enter_context(nc.sbuf_tensor("w_in_sb", [128, 8, 1536], BF16))
        w_out_sb = _st.enter_context(nc.sbuf_tensor("w_out_sb", [128, 8, 1024], BF16))
        fc1_sb = _st.enter_context(nc.sbuf_tensor("fc1_sb", [128, 8, 4096], BF16))
        fc2buf = _st.enter_context(nc.sbuf_tensor("fc2buf", [128, 3, 2, 1024], BF16))
        wsp_sb = _st.enter_context(nc.sbuf_tensor("wsp_sb", [128, 4, 128], BF16))
        wpool_sb = _st.enter_context(nc.sbuf_tensor("wpool_sb", [128, 4, 128], BF16))
        ident = _st.enter_context(nc.sbuf_tensor("ident_sb", [128, 128], BF16))
        ones = _st.enter_context(nc.sbuf_tensor("ones_sb", [128, 128], F32))
        gp1 = _st.enter_context(nc.sbuf_tensor("gp1", [128, D], F32))
        gp2 = _st.enter_context(nc.sbuf_tensor("gp2", [128, D], F32))
        Bt = _st.enter_context(nc.sbuf_tensor("Bt", [128, 4, 128], F32))
        cv = _st.enter_context(nc.sbuf_tensor("cv", [128, NCV], F32))
        mc = _st.enter_context(nc.sbuf_tensor("mc", [128, 64], F32))
        sm = _st.enter_context(nc.sbuf_tensor("sm", [128, 96], F32))
        xb = _st.enter_context(nc.sbuf_tensor("xb", [128, 3, 2048], F32))
        hbf = _st.enter_context(nc.sbuf_tensor("hbf", [128, 2, 1024], BF16))
        hP = _st.enter_context(nc.sbuf_tensor("hP", [128, 2, 8, T], BF16))
        junk = _st.enter_context(nc.sbuf_tensor("junk", [128, 1024], BF16))
        ubf = _st.enter_context(nc.sbuf_tensor("ubf", [128, 4, T], BF16))
        vg = _st.enter_context(nc.sbuf_tensor("vg", [128, 2, 512], F32))
        vn = _st.enter_context(nc.sbuf_tensor("vn", [128, 2, 512], BF16))
        Z = _st.enter_context(nc.sbuf_tensor("Z", [128, 4, 272], F32))
        pt = _st.enter_context(nc.sbuf_tensor("pt", [128, 2, 272], F32))
        diff = _st.enter_context(nc.sbuf_tensor("diff", [128, 4, T], BF16))
        tmpS = _st.enter_context(nc.sbuf_tensor("tmpS", [128, 4, 128], F32))
        yT = _st.enter_context(nc.sbuf_tensor("yT", [128, 8, T], BF16))
        tmp = _st.enter_context(nc.sbuf_tensor("tmp", [128, 2, 512], F32))
        rl = _st.enter_context(nc.sbuf_tensor("rl", [128, 2, 512], F32))
        hid = _st.enter_context(nc.sbuf_tensor("hid", [128, 3, 512], BF16))
        ps = _st.enter_context(nc.psum_tensor("ps", [128, 8, 512], F32))
        B = {}

        def buf(name):
            if name not in B:
                B[name] = Buf(name)
            return B[name]

        def bank(b):
            return ps[:, b, :]

        def bank_bf(b):
            return ps[:, b, :].bitcast(BF16)

        def Bps(b):
            return buf("ps%d" % b)

        def stg4(ti):
            return tmp[:, ti, :] if ti < 2 else vg[:, ti - 2, :]

        def stg4_buf(ti):
            return buf("tmp%d" % ti) if ti < 2 else buf("vg%d" % (ti - 2))

        def xs(slot, s):
            return xb[:, slot, s * 1024:(s + 1) * 1024]

        sm_next = [0]

        def smcol(n):
            a = sm_next[0]
            sm_next[0] += n
            assert sm_next[0] <= 96
            return a

        def rstd_chain(src_ap, src_bufs, n, inv_d, tag):
            c0 = smcol(4 * n)
            vv = sm[:, c0:c0 + n]
            r = sm[:, c0 + n:c0 + 2 * n]
            t = sm[:, c0 + 2 * n:c0 + 3 * n]
            u = sm[:, c0 + 3 * n:c0 + 4 * n]
            bv, br, bt_, bu = (buf(tag + "_vv"), buf(tag + "_r"), buf(tag + "_t"), buf(tag + "_u"))

            def chain():
                sc.op(DVE, lambda e: e.tensor_scalar(out=vv, in0=src_ap, scalar1=inv_d, scalar2=EPS,
                                                     op0=ALU.mult, op1=ALU.add),
                      reads=src_bufs, writes=[bv])
                sc.op(DVE, lambda e: e.tensor_scalar(out=r.bitcast(I32), in0=vv.bitcast(I32), scalar1=-0.5,
                                                     scalar2=1597463007.0, op0=ALU.mult, op1=ALU.add),
                      reads=[bv], writes=[br])
                for _ in range(3):
                    sc.op(DVE, lambda e: e.tensor_tensor(out=t, in0=r, in1=r, op=ALU.mult),
                          reads=[br], writes=[bt_])
                    sc.op(DVE, lambda e: e.scalar_tensor_tensor(out=u, in0=t, scalar=-0.5, in1=vv,
                                                                op0=ALU.mult, op1=ALU.mult),
                          reads=[bt_, bv], writes=[bu])
                    sc.op(DVE, lambda e: e.scalar_tensor_tensor(out=r, in0=u, scalar=1.5, in1=r,
                                                                op0=ALU.add, op1=ALU.mult),
                          reads=[bu, br], writes=[br])
            return chain, r, br

        sc.op(POOL, lambda e: e.memset(ones[:], 1.0), writes=[buf("ones")])
        sc.op(POOL, lambda e: e.memset(Z[:, :, 0:16], 0.0), writes=[buf("Z")])

        sc.dma(SP, lambda e: e.dma_start(out=cv[:], in_=cvec_d[:, :]), "c_cv", writes=[buf("cv")])
        sc.dma(SP, lambda e: e.dma_start(out=gp1[:], in_=n1pb_d[:, :]), "c_g1", writes=[buf("gp1")])
        sc.dma(SP, lambda e: e.dma_start(out=gp2[:], in_=n2pb_d[:, :]), "c_g2", writes=[buf("gp2")])
        sc.dma(SP, lambda e: e.dma_start(out=Bt[:].rearrange("p h t -> p (h t)"), in_=bspb_d[:, :]), "c_bt",
               writes=[buf("Bt")])
        sc.dma(SP, lambda e: e.dma_start(out=vg[:, 0, :], in_=wspT_d[:, :]), "c_ws", writes=[buf("vg0")])
        sc.dma(SP, lambda e: e.dma_start(out=vg[:, 1, :], in_=mask_d[:, :]), "c_mk", writes=[buf("vg1")])
        sc.dma(POOL, lambda e: e.dma_start(out=ident[:], in_=ident_d[:, :]), "c_id", writes=[buf("ident")])
        sc.dma(POOL, lambda e: e.dma_start(out=wpool_sb[:].rearrange("p g d -> p (g d)"), in_=wpool_d[:, :]),
               "c_wp", writes=[buf("wpool")])
        def x_load(i):
            slot = i % 3
            src = x_d[i * T:(i + 1) * T, :].rearrange("(s p) d -> p s d", p=128)
            dst = xb[:, slot, :].rearrange("p (s d) -> p s d", s=2)
            sc.dma(SP, lambda e: e.dma_start(out=dst, in_=src), "xl%d" % slot,
                   writes=[buf("xb%d_0" % slot), buf("xb%d_1" % slot)])
        win_v = win_d.rearrange("p (k n) -> p k n", k=8)
        for hh in range(2):
            sc.dma(POOL, lambda e, hh=hh: e.dma_start(out=w_in_sb[:, hh * 4:(hh + 1) * 4, :],
                                                      in_=win_v[:, hh * 4:(hh + 1) * 4, :]),
                   "w_in%d" % hh, writes=[buf("w_in%d" % hh)])
        W_IN = [buf("w_in0"), buf("w_in1")]
        c_th = smcol(8); c_hf = smcol(8); c_sc = smcol(8)
        sc.op(ACT, lambda e: e.activation(out=sm[:, c_th:c_th + 8], in_=cv[:, C_C:C_C + 8], func=AF.Tanh, scale=0.5),
              reads=[buf("cv")], writes=[buf("s_th")])
        sc.op(DVE, lambda e: e.tensor_scalar(out=sm[:, c_hf:c_hf + 8], in0=sm[:, c_th:c_th + 8], scalar1=1.0,
                                             scalar2=0.5, op0=ALU.add, op1=ALU.mult),
              reads=[buf("s_th")], writes=[buf("s_hf")])
        sc.op(DVE, lambda e: e.tensor_tensor(out=sm[:, c_sc:c_sc + 8], in0=sm[:, c_hf:c_hf + 8],
                                             in1=cv[:, C_C:C_C + 8], op=ALU.mult),
              reads=[buf("s_hf"), buf("cv")], writes=[buf("s_sc")])
        scv = sm[:, c_sc:c_sc + 8]

        wada_v = wada_d.rearrange("p (b k c) -> p b k c", b=24, k=8)
        CH_COL = {0: 0, 1: 8, 3: 16, 4: 24}
        for b in range(24):
            st = b % 3
            stg = xb[:, st, :].rearrange("p (k c) -> p k c", c=256)
            stg_bufs = [buf("xb%d_0" % st), buf("xb%d_1" % st)]
            sc.dma(SP, lambda e, b=b, stg=stg: e.dma_start(out=stg, in_=wada_v[:, b, :, :]),
                   "wa%d" % (b % 3), writes=stg_bufs)
            sc.dma(SP, lambda e, b=b: e.dma_start(out=rl[0:1, b % 2, 0:256], in_=bada_d[0:1, b * 256:(b + 1) * 256]),
                   "ba%d" % (b % 2), writes=[buf("rl%d" % (b % 2))])
            pr = 4 + b % 2

            def mm_mod(e, b=b, stg=stg, pr=pr):
                for k in range(8):
                    e.matmul(ps[0:1, pr, 0:256], lhsT=scv[:, k:k + 1], rhs=stg[:, k, :], start=(k == 0), stop=False)
                return e.matmul(ps[0:1, pr, 0:256], lhsT=ones[0:1, 0:1], rhs=rl[0:1, b % 2, 0:256], start=False, stop=True)
            sc.op(PE, mm_mod, reads=stg_bufs + [buf("s_sc"), buf("ones"), buf("rl%d" % (b % 2))],
                  writes=[Bps(pr), buf("modgate%d" % b)])
            sc.op(DVE, lambda e, b=b, pr=pr: e.tensor_copy(out=tmp[0:1, b % 2, 0:256], in_=ps[0:1, pr, 0:256]),
                  reads=[Bps(pr)], writes=[buf("tmp%d" % (b % 2))])
            chunk = b // 4
            if chunk in (2, 5):
                gp = gp1 if chunk == 2 else gp2
                gpb = buf("gp1") if chunk == 2 else buf("gp2")
                pb = 6 + b % 2
                cols = slice((b % 4) * 256, (b % 4) * 256 + 256)
                sc.op(PE, lambda e, b=b, pb=pb: e.matmul(ps[:, pb, 0:256], lhsT=ones[0:1, :], rhs=tmp[0:1, b % 2, 0:256],
                                                        start=True, stop=True),
                      reads=[buf("tmp%d" % (b % 2)), buf("ones")], writes=[Bps(pb)])
                sc.op(DVE, lambda e, gp=gp, pb=pb, cols=cols: e.tensor_tensor(out=gp[:, cols], in0=ps[:, pb, 0:256],
                                                                          in1=gp[:, cols], op=ALU.mult),
                      reads=[Bps(pb), gpb], writes=[gpb])
            else:
                col0 = CH_COL[chunk] + (b % 4) * 2

                def mm_col(e, b=b, col0=col0):
                    ins = None
                    for q in range(2):
                        ins = e.matmul(ps[:, 0, col0 + q:col0 + q + 1], lhsT=tmp[0:1, b % 2, q * 128:(q + 1) * 128],
                                       rhs=ones[0:1, 0:1], start=True, stop=True)
                    return ins
                sc.op(PE, mm_col, reads=[buf("tmp%d" % (b % 2)), buf("ones")], writes=[Bps(0)])
        sc.dma(POOL, lambda e: e.dma_start(out=w_out_sb[:], in_=wout_d.rearrange("p (k n) -> p k n", k=8)),
               "w_out", reads=[buf("modgate8")], writes=[buf("w_out")])
        fc1_v = fc1_d.rearrange("p (k n) -> p k n", k=8)
        for q in range(4):
            sc.dma(POOL, lambda e, q=q: e.dma_start(out=fc1_sb[:, 2 * q:2 * q + 2, :], in_=fc1_v[:, 2 * q:2 * q + 2, :]),
                   "fc1_%d" % q, reads=[buf("modgate%d" % (12 + 4 * q if q < 3 else 23))], writes=[buf("fc1_%d" % q)])
        FC1 = [buf("fc1_%d" % q) for q in range(4)]

        x_load(0)
        sc.op(DVE, lambda e: e.tensor_copy(out=mc[:, 0:32], in_=ps[:, 0, 0:32]), reads=[Bps(0)], writes=[buf("mc_raw")])
        sc.op(DVE, lambda e: e.scalar_tensor_tensor(out=mc[:, 32:40], in0=mc[:, 8:16], scalar=1.0,
                                                    in1=cv[:, C_N1:C_N1 + 8], op0=ALU.add, op1=ALU.mult),
              reads=[buf("mc_raw"), buf("cv")], writes=[buf("mc_g1")])
        sc.op(DVE, lambda e: e.scalar_tensor_tensor(out=mc[:, 40:48], in0=mc[:, 24:32], scalar=1.0,
                                                    in1=cv[:, C_N2:C_N2 + 8], op0=ALU.add, op1=ALU.mult),
              reads=[buf("mc_raw"), buf("cv")], writes=[buf("mc_g2")])
        MODB = [buf("mc_raw"), buf("mc_g1"), buf("mc_g2")]
        G1, SH1, G2, SH2 = 32, 0, 40, 16

        sc.op(DVE, lambda e: e.tensor_tensor(out=vg[:, 0, :], in0=vg[:, 0, :], in1=vg[:, 1, :], op=ALU.mult),
              reads=[buf("vg0"), buf("vg1")], writes=[buf("vg0")])
        sc.op(ACT, lambda e: e.activation(out=wsp_sb[:].rearrange("p h t -> p (h t)"), in_=vg[:, 0, :], func=AF.Identity),
              reads=[buf("vg0")], writes=[buf("wsp")])
        sc.op(PE, lambda e: e.matmul(ps[:, 1, :], lhsT=ones[:, :], rhs=vg[:, 0, :], start=True, stop=True),
              reads=[buf("vg0"), buf("ones")], writes=[Bps(1)])

        def bt_fix(e):
            ins = None
            for h in range(4):
                ins = e.scalar_tensor_tensor(out=Bt[:, h, :], in0=ps[:, 1, h * 128:(h + 1) * 128],
                                             scalar=cv[:, C_LB + h:C_LB + h + 1], in1=Bt[:, h, :],
                                             op0=ALU.mult, op1=ALU.add)
            return ins
        sc.op(DVE, bt_fix, reads=[Bps(1), buf("cv"), buf("Bt")], writes=[buf("Bt")])

        c_ss1 = smcol(2)
        ch1, r1, br1 = rstd_chain(sm[:, c_ss1:c_ss1 + 2], [buf("ss1_0"), buf("ss1_1")], 2, 1.0 / D, "r1")
        c_vs = smcol(2); c_vq = smcol(2); c_mean = smcol(2); c_msq = smcol(2); c_var = smcol(2)
        chv, rv, brv = rstd_chain(sm[:, c_var:c_var + 2], [buf("var")], 2, 1.0, "rv")
        c_ssm = smcol(2); c_ssms = smcol(1)
        chm, rm, brm = rstd_chain(sm[:, c_ssms:c_ssms + 1], [buf("ssms")], 1, 1.0 / D, "rm")
        c_ss2 = smcol(2)
        ch2, r2, br2 = rstd_chain(sm[:, c_ss2:c_ss2 + 2], [buf("ss2_0"), buf("ss2_1")], 2, 1.0 / D, "r2")
        c_ssf = smcol(4); c_ssfs = smcol(2)
        chf, rf, brf = rstd_chain(sm[:, c_ssfs:c_ssfs + 2], [buf("ssfs")], 2, 1.0 / D, "rf")

        MB = (6, 7)
        FB = (4, 5)

        def norm_A(slot, ss_col, ss_name, chain, r_ap, r_buf):
            for s in range(2):
                sc.op(ACT, lambda e, s=s: e.activation(out=junk[:], in_=xs(slot, s), func=AF.Square,
                                                       accum_out=sm[:, ss_col + s:ss_col + s + 1]),
                      reads=[buf("xb%d_%d" % (slot, s))], writes=[buf("junk"), buf("%s_%d" % (ss_name, s))])
            chain()

        def norm_A2(slot, r_ap, r_buf):
            for s in range(2):
                sc.op(ACT, lambda e, s=s: e.activation(out=hbf[:, s, :], in_=xs(slot, s), func=AF.Identity,
                                                       scale=r_ap[:, s:s + 1]),
                      reads=[buf("xb%d_%d" % (slot, s)), r_buf], writes=[buf("hbf_%d" % s)])

        def norm_B(dstT, dstT_buf, gcol, shcol):
            for s in range(2):
                mb = MB[s]

                def tr(e, s=s, mb=mb):
                    ins = None
                    for k in range(8):
                        ins = e.transpose(out=bank_bf(mb)[:, k * 128:(k + 1) * 128],
                                          in_=hbf[:, s, k * 128:(k + 1) * 128], identity=ident[:])
                    return ins
                sc.op(PE, tr, reads=[buf("hbf_%d" % s), buf("ident")], writes=[Bps(mb)])

                if s == 0:
                    def ev(e, s=s, mb=mb):
                        ins = None
                        for k in range(8):
                            ins = e.activation(out=dstT[:, k, s * 128:(s + 1) * 128],
                                               in_=bank_bf(mb)[:, k * 128:(k + 1) * 128], func=AF.Identity,
                                               scale=mc[:, gcol + k:gcol + k + 1], bias=mc[:, shcol + k:shcol + k + 1])
                        return ins
                    sc.op(ACT, ev, reads=[Bps(mb)] + MODB, writes=[buf(dstT_buf + "_%d" % s)])
                else:
                    def ev(e, s=s, mb=mb):
                        ins = None
                        for k in range(8):
                            ins = e.tensor_scalar(out=dstT[:, k, s * 128:(s + 1) * 128],
                                                  in0=bank_bf(mb)[:, k * 128:(k + 1) * 128],
                                                  scalar1=mc[:, gcol + k:gcol + k + 1],
                                                  scalar2=mc[:, shcol + k:shcol + k + 1], op0=ALU.mult, op1=ALU.add)
                        return ins
                    sc.op(DVE, ev, reads=[Bps(mb)] + MODB, writes=[buf(dstT_buf + "_%d" % s)])

        def mixer(i):
            slot = i % 3
            XB = [buf("xb%d_0" % slot), buf("xb%d_1" % slot)]
            hT = hP[:, i % 2]
            HT = [buf("hP%d_0" % (i % 2)), buf("hP%d_1" % (i % 2))]
            norm_A(slot, c_ss1, "ss1", ch1, r1, br1)
            yield
            yield
            norm_A2(slot, r1, br1)
            yield
            norm_B(hT, "hP%d" % (i % 2), G1, SH1)
            yield
            if i > 0:
                sc.op(DVE, lambda e: e.tensor_copy(out=Z[:, :, 0:16], in_=Z[:, :, 256:272]),
                      reads=[buf("Z")], writes=[buf("Z")])
            for gp_ in range(2):
                mb = MB[gp_]

                def mm_z(e, gp_=gp_, mb=mb):
                    ins = None
                    for gg in range(2):
                        g = gp_ * 2 + gg
                        for k in range(8):
                            ins = e.matmul(ps[:, mb, gg * T:(gg + 1) * T],
                                           lhsT=w_in_sb[:, k, 1024 + g * 128:1024 + (g + 1) * 128],
                                           rhs=hT[:, k, :], start=(k == 0), stop=(k == 7))
                    return ins
                sc.op(PE, mm_z, reads=HT + W_IN, writes=[Bps(mb)])
                sc.op(ACT, lambda e, gp_=gp_, mb=mb: e.activation(
                    out=Z[:, 2 * gp_:2 * gp_ + 2, 16:272], in_=ps[:, mb, :].rearrange("p (g t) -> p g t", g=2),
                    func=AF.Identity), reads=[Bps(mb)], writes=[buf("Z")])
            yield
            def pooling(g):
                m = g + 1
                w = 1 << m
                src = Z[:, g, :]
                src_b = buf("Z")
                for k in range(m):
                    lo = (1 << (k + 1)) - 1
                    sh = 1 << k
                    dst = pt[:, k % 2, :]
                    dst_b = buf("pt%d" % (k % 2))
                    sc.op(DVE, lambda e, src=src, dst=dst, lo=lo, sh=sh: e.tensor_tensor(
                        out=dst[:, lo:272], in0=src[:, lo:272], in1=src[:, lo - sh:272 - sh], op=ALU.add),
                        reads=[src_b], writes=[dst_b])
                    src, src_b = dst, dst_b
                sc.op(DVE, lambda e, src=src, g=g, w=w: e.scalar_tensor_tensor(
                    out=diff[:, g, :], in0=src[:, 16:272], scalar=1.0 / w, in1=Z[:, g, 16:272],
                    op0=ALU.mult, op1=ALU.subtract), reads=[src_b, buf("Z")], writes=[buf("diff")])
                if i == 0:
                    oth = pt[:, (m % 2), 0:16]
                    oth_b = buf("pt%d" % (m % 2))
                    sc.op(DVE, lambda e, src=src, g=g, oth=oth: e.tensor_tensor(
                        out=oth, in0=src[:, 16:32], in1=cv[:, C_RC + 16 * g:C_RC + 16 * g + 16], op=ALU.mult),
                        reads=[src_b, buf("cv")], writes=[oth_b])
                    sc.op(DVE, lambda e, g=g, oth=oth: e.tensor_tensor(
                        out=diff[:, g, 0:16], in0=oth, in1=Z[:, g, 16:32], op=ALU.subtract),
                        reads=[oth_b, buf("Z"), buf("diff")], writes=[buf("diff")])
            for cp in range(2):
                mb = MB[cp]

                def mm_u(e, cp=cp, mb=mb):
                    ins = None
                    for cc in range(2):
                        c = cp * 2 + cc
                        for k in range(8):
                            ins = e.matmul(ps[:, mb, cc * T:(cc + 1) * T], lhsT=w_in_sb[:, k, c * 128:(c + 1) * 128],
                                           rhs=hT[:, k, :], start=(k == 0), stop=(k == 7))
                    return ins
                sc.op(PE, mm_u, reads=HT + W_IN, writes=[Bps(mb)])
                sc.op(ACT, lambda e, cp=cp, mb=mb: e.activation(
                    out=ubf[:, 2 * cp:2 * cp + 2, :].rearrange("p c t -> p (c t)"), in_=ps[:, mb, :],
                    func=AF.Gelu_apprx_tanh), reads=[Bps(mb)], writes=[buf("ubf%d" % cp)])
            pooling(0)
            pooling(1)
            for s in range(2):
                mb = MB[s]

                def mm_v(e, s=s, mb=mb):
                    ins = None
                    for k in range(8):
                        ins = e.matmul(ps[:, mb, :], lhsT=hT[:, k, s * 128:(s + 1) * 128], rhs=w_in_sb[:, k, 512:1024],
                                       start=(k == 0), stop=(k == 7))
                    return ins
                sc.op(PE, mm_v, reads=HT + W_IN, writes=[Bps(mb)])
                sc.op(ACT, lambda e, s=s, mb=mb: e.activation(out=vg[:, s, :], in_=ps[:, mb, :], func=AF.Gelu_apprx_tanh,
                                                              accum_out=sm[:, c_vs + s:c_vs + s + 1]),
                      reads=[Bps(mb)], writes=[buf("vg%d" % s), buf("vs%d" % s)])
                sc.op(ACT, lambda e, s=s: e.activation(out=junk[:, 0:512], in_=vg[:, s, :], func=AF.Square,
                                                       accum_out=sm[:, c_vq + s:c_vq + s + 1]),
                      reads=[buf("vg%d" % s)], writes=[buf("junk"), buf("vq%d" % s)])
            sc.op(DVE, lambda e: e.tensor_scalar(out=sm[:, c_mean:c_mean + 2], in0=sm[:, c_vs:c_vs + 2],
                                                 scalar1=1.0 / 512, scalar2=None, op0=ALU.mult),
                  reads=[buf("vs0"), buf("vs1")], writes=[buf("mean")])
            sc.op(DVE, lambda e: e.tensor_tensor(out=sm[:, c_msq:c_msq + 2], in0=sm[:, c_mean:c_mean + 2],
                                                 in1=sm[:, c_mean:c_mean + 2], op=ALU.mult),
                  reads=[buf("mean")], writes=[buf("msq")])
            sc.op(DVE, lambda e: e.scalar_tensor_tensor(out=sm[:, c_var:c_var + 2], in0=sm[:, c_vq:c_vq + 2],
                                                        scalar=1.0 / 512, in1=sm[:, c_msq:c_msq + 2],
                                                        op0=ALU.mult, op1=ALU.subtract),
                  reads=[buf("vq0"), buf("vq1"), buf("msq")], writes=[buf("var")])
            chv()
            for s in range(2):
                sc.op(DVE, lambda e, s=s: e.tensor_scalar(out=vn[:, s, :], in0=vg[:, s, :],
                                                           scalar1=sm[:, c_mean + s:c_mean + s + 1],
                                                           scalar2=rv[:, s:s + 1], op0=ALU.subtract, op1=ALU.mult),
                      reads=[buf("vg%d" % s), buf("mean"), brv], writes=[buf("vn%d" % s)])
            yield
            pooling(2)
            pooling(3)
            yield
            for s in range(2):
                mb = MB[s]

                def mm_s(e, s=s, mb=mb):
                    ins = None
                    for h in range(4):
                        ins = e.matmul(ps[:, mb, h * 128:(h + 1) * 128], lhsT=vn[:, s, h * 128:(h + 1) * 128],
                                       rhs=wsp_sb[:, h, :], start=True, stop=True)
                    return ins
                sc.op(PE, mm_s, reads=[buf("vn%d" % s), buf("wsp")], writes=[Bps(mb)])

                def ev_s(e, s=s, mb=mb):
                    ins = None
                    for h in range(4):
                        ins = e.scalar_tensor_tensor(out=tmpS[:, h, :], in0=ps[:, mb, h * 128:(h + 1) * 128],
                                                     scalar=cv[:, C_LG + h:C_LG + h + 1], in1=Bt[:, h, :],
                                                     op0=ALU.mult, op1=ALU.add)
                    return ins
                sc.op(DVE, ev_s, reads=[Bps(mb), buf("cv"), buf("Bt")], writes=[buf("tmpS")])
                sc.op(POOL, lambda e, s=s: e.tensor_tensor(out=yT[:, 0:4, s * 128:(s + 1) * 128], in0=tmpS[:],
                                                           in1=ubf[:, :, s * 128:(s + 1) * 128], op=ALU.mult),
                      reads=[buf("tmpS"), buf("ubf0"), buf("ubf1")], writes=[buf("yTa%d" % s)])
            yield
            for gp_ in range(2):
                mb = MB[gp_]

                def mm_p(e, gp_=gp_, mb=mb):
                    ins = None
                    for gg in range(2):
                        g = gp_ * 2 + gg
                        ins = e.matmul(ps[:, mb, gg * T:(gg + 1) * T], lhsT=wpool_sb[:, g, :], rhs=diff[:, g, :],
                                       start=True, stop=True)
                    return ins
                sc.op(PE, mm_p, reads=[buf("diff"), buf("wpool")], writes=[Bps(mb)])

                def ev_p(e, gp_=gp_, mb=mb):
                    ins = None
                    for gg in range(2):
                        g = gp_ * 2 + gg
                        ins = e.tensor_scalar(out=yT[:, 4 + g, :], in0=ps[:, mb, gg * T:(gg + 1) * T],
                                              scalar1=cv[:, C_BP + g:C_BP + g + 1], scalar2=cv[:, C_PS + g:C_PS + g + 1],
                                              op0=ALU.add, op1=ALU.mult)
                    return ins
                sc.op(DVE, ev_p, reads=[Bps(mb), buf("cv")], writes=[buf("yTb%d" % gp_)])
            yield
            YT = [buf("yTa0"), buf("yTa1"), buf("yTb0"), buf("yTb1")]
            for s in range(2):
                for hf in range(2):
                    mb = MB[hf]

                    def mm_o(e, s=s, hf=hf, mb=mb):
                        ins = None
                        for k in range(8):
                            ins = e.matmul(ps[:, mb, :], lhsT=yT[:, k, s * 128:(s + 1) * 128],
                                           rhs=w_out_sb[:, k, hf * 512:(hf + 1) * 512], start=(k == 0), stop=(k == 7))
                        return ins
                    sc.op(PE, mm_o, reads=YT + [buf("w_out")], writes=[Bps(mb)])
                    sc.op(ACT, lambda e, hf=hf, mb=mb: e.activation(out=junk[:, 0:512], in_=ps[:, mb, :], func=AF.Square,
                                                                    accum_out=sm[:, c_ssm + hf:c_ssm + hf + 1]),
                          reads=[Bps(mb)], writes=[buf("junk"), buf("ssm%d" % hf)])
                for hf in range(2):
                    mb = MB[hf]
                    ti = 2 * s + hf
                    sc.op(DVE, lambda e, hf=hf, mb=mb, ti=ti: e.tensor_tensor(
                        out=stg4(ti), in0=ps[:, mb, :], in1=gp1[:, hf * 512:(hf + 1) * 512], op=ALU.mult),
                        reads=[Bps(mb), buf("gp1"), buf("ssm%d" % hf)], writes=[stg4_buf(ti)])
                sc.op(DVE, lambda e: e.tensor_tensor(out=sm[:, c_ssms:c_ssms + 1], in0=sm[:, c_ssm:c_ssm + 1],
                                                     in1=sm[:, c_ssm + 1:c_ssm + 2], op=ALU.add),
                      reads=[buf("ssm0"), buf("ssm1")], writes=[buf("ssms")])
                chm()
                for hf in range(2):
                    ti = 2 * s + hf
                    sc.op(DVE, lambda e, s=s, hf=hf, ti=ti: e.scalar_tensor_tensor(
                        out=xs(slot, s)[:, hf * 512:(hf + 1) * 512], in0=stg4(ti), scalar=rm[:, 0:1],
                        in1=xs(slot, s)[:, hf * 512:(hf + 1) * 512], op0=ALU.mult, op1=ALU.add),
                        reads=[stg4_buf(ti), brm, XB[s]], writes=[XB[s]])
                yield
            yield
            norm_A(slot, c_ss2, "ss2", ch2, r2, br2)
            yield
            norm_A2(slot, r2, br2)
            yield
            norm_B(hT, "hP%d" % (i % 2), G2, SH2)
            yield

        def slab_load(G):
            i, q = divmod(G, 16)
            if i >= NT:
                return
            sl = G % 3
            rows = slice(q * 256, (q + 1) * 256)
            if i == 0:
                sc.dma(POOL, lambda e: e.dma_start(out=fc2buf[:, sl, :, :],
                                                   in_=fc2_d[rows, :].rearrange("(j p) n -> p j n", p=128)),
                       "f2p_%d" % sl, writes=[buf("f2b%d" % sl)])
                sc.dma(SP, lambda e: e.dma_start(out=fc2s_d[rows, :].rearrange("(j p) n -> p j n", p=128),
                                                 in_=fc2buf[:, sl, :, :]),
                       "f2w%d" % sl, reads=[buf("f2b%d" % sl)], writes=[buf("f2s%d" % q)])
            else:
                sc.dma(SP, lambda e: e.dma_start(out=fc2buf[:, sl, :, :],
                                                 in_=fc2s_d[rows, :].rearrange("(j p) n -> p j n", p=128)),
                       "f2_%d" % sl, reads=[buf("f2s%d" % q)], writes=[buf("f2b%d" % sl)])

        def ffn(i):
            slot = i % 3
            XB = [buf("xb%d_0" % slot), buf("xb%d_1" % slot)]
            h2T = hP[:, i % 2]
            H2T = [buf("hP%d_0" % (i % 2)), buf("hP%d_1" % (i % 2))]
            if i == 0:
                slab_load(0)
                slab_load(1)
                slab_load(2)

            def fc1(jp):
                fb = FB[jp % 2]

                def mm(e):
                    ins = None
                    for jj in range(2):
                        j = jp * 2 + jj
                        for k in range(8):
                            ins = e.matmul(ps[:, fb, jj * T:(jj + 1) * T], lhsT=fc1_sb[:, k, j * 128:(j + 1) * 128],
                                           rhs=h2T[:, k, :], start=(k == 0), stop=(k == 7))
                    return ins
                sc.op(PE, mm, reads=H2T + FC1, writes=[Bps(fb)])
                sc.op(ACT, lambda e: e.activation(out=rl[:, jp % 2, :], in_=ps[:, fb, :], func=AF.Relu),
                      reads=[Bps(fb)], writes=[buf("rl%d" % (jp % 2))])
                sc.op(POOL, lambda e: e.tensor_tensor(out=hid[:, jp % 3, :], in0=rl[:, jp % 2, :], in1=rl[:, jp % 2, :],
                                                      op=ALU.mult),
                      reads=[buf("rl%d" % (jp % 2))], writes=[buf("hid%d" % (jp % 3))])

            def fc2(jp):
                sl = (16 * i + jp) % 3

                def mm(e):
                    ins = None
                    for jj in range(2):
                        j = jp * 2 + jj
                        for s in range(2):
                            for hf in range(2):
                                ins = e.matmul(ps[:, 2 * s + hf, :],
                                               lhsT=hid[:, jp % 3, jj * T + s * 128:jj * T + (s + 1) * 128],
                                               rhs=fc2buf[:, sl, jj, hf * 512:(hf + 1) * 512],
                                               start=(j == 0), stop=(j == NJ - 1))
                    return ins
                sc.op(PE, mm, reads=[buf("hid%d" % (jp % 3)), buf("f2b%d" % sl)], writes=[Bps(b) for b in range(4)])

            for jp in range(16):
                fc1(jp)
                if jp >= 2:
                    fc2(jp - 2)
                    slab_load(16 * i + jp + 1)
                yield
            fc2(14)
            slab_load(16 * i + 17)
            fc2(15)
            slab_load(16 * i + 18)
            for s in range(2):
                for hf in range(2):
                    b_ = 2 * s + hf
                    sc.op(ACT, lambda e, b_=b_: e.activation(out=junk[:, 0:512], in_=ps[:, b_, :], func=AF.Square,
                                                             accum_out=sm[:, c_ssf + b_:c_ssf + b_ + 1]),
                          reads=[Bps(b_)], writes=[buf("junk"), buf("ssf%d" % b_)])
            for s in range(2):
                for hf in range(2):
                    b_ = 2 * s + hf
                    sc.op(DVE, lambda e, hf=hf, b_=b_: e.tensor_tensor(
                        out=stg4(b_), in0=ps[:, b_, :], in1=gp2[:, hf * 512:(hf + 1) * 512], op=ALU.mult),
                        reads=[Bps(b_), buf("gp2"), buf("ssf%d" % b_)], writes=[stg4_buf(b_)])
            sc.op(DVE, lambda e: e.tensor_tensor(out=sm[:, c_ssfs:c_ssfs + 2],
                                                 in0=sm[:, c_ssf:c_ssf + 4].rearrange("p (s h) -> p s h", h=2)[:, :, 0],
                                                 in1=sm[:, c_ssf:c_ssf + 4].rearrange("p (s h) -> p s h", h=2)[:, :, 1],
                                                 op=ALU.add),
                  reads=[buf("ssf%d" % b_) for b_ in range(4)], writes=[buf("ssfs")])
            chf()
            for s in range(2):
                for hf in range(2):
                    b_ = 2 * s + hf
                    sc.op(DVE, lambda e, s=s, hf=hf, b_=b_: e.scalar_tensor_tensor(
                        out=xs(slot, s)[:, hf * 512:(hf + 1) * 512], in0=stg4(b_), scalar=rf[:, s:s + 1],
                        in1=xs(slot, s)[:, hf * 512:(hf + 1) * 512], op0=ALU.mult, op1=ALU.add),
                        reads=[stg4_buf(b_), brf, XB[s]], writes=[XB[s]])
            dst = out_d[i * T:(i + 1) * T, :].rearrange("(s p) d -> p s d", p=128)
            srcv = xb[:, slot, :].rearrange("p (s d) -> p s d", s=2)
            ev = sc.dma(SP, lambda e: e.dma_start(out=dst, in_=srcv), "xs%d" % slot, reads=XB)
            stores.append(ev)
            yield

        stores = []
        if NT > 1:
            x_load(1)
        for _ in mixer(0):
            pass
        for i in range(NT):
            gm = mixer(i + 1) if i + 1 < NT else None
            step = 0
            for _ in ffn(i):
                step += 1
                if step == 4 and i + 2 < NT:
                    x_load(i + 2)
                if gm is not None and step >= 2:
                    try:
                        next(gm)
                    except StopIteration:
                        gm = None
            if gm is not None:
                for _ in gm:
                    pass
        sc.wait(SP, stores)

        all_keys = set()
        for e_ in ENGS:
            for (deps, fn, key, amt) in sc.ops[e_]:
                if key is not None:
                    all_keys.add(key)
        with contextlib.ExitStack() as stack:
            for k_ in sorted(all_keys):
                sems[k_] = stack.enter_context(nc.semaphore("s_" + k_))
            block = stack.enter_context(nc.Block())

            def run(eng_name, eng):
                waited = {}
                for (deps, fn, key, amt) in sc.ops[eng_name]:
                    for (k_, v_) in deps:
                        if waited.get(k_, 0) >= v_:
                            continue
                        eng.wait_ge(sems[k_], v_)
                        waited[k_] = v_
                    if fn is None:
                        continue
                    ins = fn(eng)
                    ins.then_inc(sems[key], amt)

            @block.tensor
            def _(e):
                run(PE, e)

            @block.scalar
            def _(e):
                run(ACT, e)

            @block.vector
            def _(e):
                run(DVE, e)

            @block.gpsimd
            def _(e):
                run(POOL, e)

            @block.sync
            def _(e):
                run(SP, e)
    return nc


def _host_inputs(inputs, NT=16):
    f = lambda a: np.ascontiguousarray(np.asarray(a, dtype=np.float32))
    S = NT * T
    x = f(inputs["x"])
    c = f(inputs["c"])
    pmaj = lambda w: np.ascontiguousarray(f(w).reshape(8, 128, -1).transpose(1, 0, 2).reshape(128, -1))
    col = lambda v, n: np.ascontiguousarray(f(v).reshape(n, 128).T)
    rc = np.zeros((4, 16), np.float32)
    for g in range(4):
        w = 2 << g
        for t in range(16):
            rc[g, t] = 1.0 / min(t + 1, w)
    shared = {
        "n1pb": np.ascontiguousarray(np.broadcast_to(f(inputs["norm1_post"])[None, :], (128, D))),
        "n2pb": np.ascontiguousarray(np.broadcast_to(f(inputs["norm2_post"])[None, :], (128, D))),
        "bspb": np.ascontiguousarray(np.broadcast_to(f(inputs["b_spatial"]).reshape(1, 512), (128, 512))),
        "mask": np.ascontiguousarray(np.tile(np.triu(np.ones((128, 128), np.float32)), (1, 4))),
        "wspT": np.ascontiguousarray(f(inputs["w_spatial"]).transpose(2, 0, 1).reshape(128, 512)),
        "ident": np.eye(128, dtype=np.float32),
        "wpool": np.ascontiguousarray(f(inputs["w_pool"]).transpose(1, 0, 2).reshape(128, 512)),
        "bada": f(inputs["b_ada"]).reshape(1, 6144),
        "wada": np.ascontiguousarray(f(inputs["w_ada"]).reshape(8, 128, 24, 256).transpose(1, 2, 0, 3).reshape(128, -1)),
        "w_in": pmaj(inputs["w_in"]),
        "w_out": pmaj(inputs["w_out"]),
        "w_fc1": pmaj(inputs["w_fc1"]),
        "w_fc2": f(inputs["w_fc2"]),
    }
    in_maps = []
    for b in range(x.shape[0]):
        cvec = np.zeros((128, NCV), np.float32)
        cvec[:, C_C:C_C + 8] = col(c[b], 8)
        cvec[:, C_N1:C_N1 + 8] = col(inputs["norm1_pre"], 8)
        cvec[:, C_N2:C_N2 + 8] = col(inputs["norm2_pre"], 8)
        cvec[:, C_LG:C_LG + 4] = col(inputs["ln_v_gain"], 4)
        cvec[:, C_LB:C_LB + 4] = col(inputs["ln_v_bias"], 4)
        cvec[:, C_BP:C_BP + 4] = col(np.asarray(inputs["b_pool"]).reshape(-1), 4)
        cvec[:, C_PS:C_PS + 4] = col(inputs["pool_scale"], 4)
        cvec[:, C_RC:C_RC + 64] = rc.reshape(1, 64)
        m = dict(shared)
        m["x"] = np.ascontiguousarray(x[b, :S])
        m["cvec"] = cvec
        in_maps.append(m)
    return in_maps


def kernel(**inputs):
    in_maps = _host_inputs(inputs, 16)
    nc = build(16)
    res = run_bass_kernel_spmd(nc, in_maps, core_ids=list(range(len(in_maps))))
    return np.stack([np.asarray(r["out"], dtype=np.float32) for r in res.results], axis=0)
```

```python
import numpy as np
import concourse.bass as bass
import concourse.mybir as mybir
from concourse.bass_utils import run_bass_kernel_spmd

F32 = mybir.dt.float32
BF16 = mybir.dt.bfloat16
I32 = mybir.dt.int32
AF = mybir.ActivationFunctionType
ALU = mybir.AluOpType

D = 1024
SEQ = 4096
T = 256
NJ = 32
EPS = 1e-6
PE, ACT, DVE, POOL, SP = "pe", "act", "dve", "pool", "sp"
ENGS = (PE, ACT, DVE, POOL, SP)

C_C, C_N1, C_N2, C_LG, C_LB, C_BP, C_PS, C_RC = 0, 8, 16, 24, 28, 32, 36, 40
NCV = 40 + 64


class Buf:
    __slots__ = ("name", "w", "r")

    def __init__(self, name):
        self.name = name
        self.w = None
        self.r = []


class Sched:
    def __init__(self):
        self.ops = {e: [] for e in ENGS}
        self.count = {}

    def _deps(self, eng, reads, writes):
        deps = []
        for b in reads:
            if b.w is not None:
                deps.append((b.w, "raw"))
        for b in writes:
            if b.w is not None:
                deps.append((b.w, "waw"))
            for r in b.r:
                deps.append((r, "war"))
        out = []
        for (ev, kind) in deps:
            if ev[0] == eng:
                if eng == PE:
                    continue
            out.append(ev)
        return out

    def _record(self, eng, fn, key, amt, reads, writes):
        deps = self._deps(eng, reads, writes)
        self.count[key] = self.count.get(key, 0) + amt
        ev = (key, self.count[key])
        self.ops[eng].append((deps, fn, key, amt))
        for b in reads:
            b.r.append(ev)
        for b in writes:
            b.w = ev
            b.r = []
        return ev

    def op(self, eng, fn, reads=(), writes=()):
        return self._record(eng, fn, eng, 1, reads, writes)

    def dma(self, queue, fn, key, reads=(), writes=()):
        return self._record(queue, fn, key, 16, reads, writes)

    def wait(self, eng, events):
        self.ops[eng].append((list(events), None, None, 0))


def build(NT=16):
    S = NT * T
    nc = bass.Bass("TRN2", target_bir_lowering=False)
    dt_in = lambda name, shape: nc.dram_tensor(name, shape, F32, kind="ExternalInput").ap()
    x_d = dt_in("x", [S, D])
    cvec_d = dt_in("cvec", [128, NCV])
    n1pb_d = dt_in("n1pb", [128, D])
    n2pb_d = dt_in("n2pb", [128, D])
    bspb_d = dt_in("bspb", [128, 512])
    mask_d = dt_in("mask", [128, 512])
    wspT_d = dt_in("wspT", [128, 512])
    ident_d = dt_in("ident", [128, 128])
    wpool_d = dt_in("wpool", [128, 512])
    bada_d = dt_in("bada", [1, 6144])
    wada_d = dt_in("wada", [128, 8 * 6144])
    win_d = dt_in("w_in", [128, 8 * 1536])
    wout_d = dt_in("w_out", [128, 8 * 1024])
    fc1_d = dt_in("w_fc1", [128, 8 * 4096])
    fc2_d = dt_in("w_fc2", [4096, D])
    out_d = nc.dram_tensor("out", [S, D], F32, kind="ExternalOutput").ap()
    fc2s_d = nc.dram_tensor("fc2s", [4096, D], BF16, kind="Internal").ap()

    sc = Sched()
    sems = {}

    import contextlib
    with contextlib.ExitStack() as _st:
        w_in_sb = _st.enter_context(nc.sbuf_tensor("w_in_sb", [128, 8, 1536], BF16))
        w_out_sb = _st.enter_context(nc.sbuf_tensor("w_out_sb", [128, 8, 1024], BF16))
        fc1_sb = _st.enter_context(nc.sbuf_tensor("fc1_sb", [128, 8, 4096], BF16))
        fc2buf = _st.enter_context(nc.sbuf_tensor("fc2buf", [128, 3, 2, 1024], BF16))
        wsp_sb = _st.enter_context(nc.sbuf_tensor("wsp_sb", [128, 4, 128], BF16))
        wpool_sb = _st.enter_context(nc.sbuf_tensor("wpool_sb", [128, 4, 128], BF16))
        ident = _st.enter_context(nc.sbuf_tensor("ident_sb", [128, 128], BF16))
        ones = _st.enter_context(nc.sbuf_tensor("ones_sb", [128, 128], F32))
        gp1 = _st.enter_context(nc.sbuf_tensor("gp1", [128, D], F32))
        gp2 = _st.enter_context(nc.sbuf_tensor("gp2", [128, D], F32))
        Bt = _st.enter_context(nc.sbuf_tensor("Bt", [128, 4, 128], F32))
        cv = _st.enter_context(nc.sbuf_tensor("cv", [128, NCV], F32))
        mc = _st.enter_context(nc.sbuf_tensor("mc", [128, 64], F32))
        sm = _st.enter_context(nc.sbuf_tensor("sm", [128, 96], F32))
        xb = _st.enter_context(nc.sbuf_tensor("xb", [128, 3, 2048], F32))
        hbf = _st.enter_context(nc.sbuf_tensor("hbf", [128, 2, 1024], BF16))
        hP = _st.enter_context(nc.sbuf_tensor("hP", [128, 2, 8, T], BF16))
        junk = _st.enter_context(nc.sbuf_tensor("junk", [128, 1024], BF16))
        ubf = _st.enter_context(nc.sbuf_tensor("ubf", [128, 4, T], BF16))
        vg = _st.enter_context(nc.sbuf_tensor("vg", [128, 2, 512], F32))
        vn = _st.enter_context(nc.sbuf_tensor("vn", [128, 2, 512], BF16))
        Z = _st.enter_context(nc.sbuf_tensor("Z", [128, 4, 272], F32))
        pt = _st.enter_context(nc.sbuf_tensor("pt", [128, 2, 272], F32))
        diff = _st.enter_context(nc.sbuf_tensor("diff", [128, 4, T], BF16))
        tmpS = _st.enter_context(nc.sbuf_tensor("tmpS", [128, 4, 128], F32))
        yT = _st.enter_context(nc.sbuf_tensor("yT", [128, 8, T], BF16))
        tmp = _st.enter_context(nc.sbuf_tensor("tmp", [128, 2, 512], F32))
        rl = _st.enter_context(nc.sbuf_tensor("rl", [128, 2, 512], F32))
        hid = _st.enter_context(nc.sbuf_tensor("hid", [128, 3, 512], BF16))
        ps = _st.enter_context(nc.psum_tensor("ps", [128, 8, 512], F32))
        B = {}

        def buf(name):
            if name not in B:
                B[name] = Buf(name)
            return B[name]

        def bank(b):
            return ps[:, b, :]

        def bank_bf(b):
            return ps[:, b, :].bitcast(BF16)

        def Bps(b):
            return buf("ps%d" % b)

        def stg4(ti):
            return tmp[:, ti, :] if ti < 2 else vg[:, ti - 2, :]

        def stg4_buf(ti):
            return buf("tmp%d" % ti) if ti < 2 else buf("vg%d" % (ti - 2))

        def xs(slot, s):
            return xb[:, slot, s * 1024:(s + 1) * 1024]

        sm_next = [0]

        def smcol(n):
            a = sm_next[0]
            sm_next[0] += n
            assert sm_next[0] <= 96
            return a

        def rstd_chain(src_ap, src_bufs, n, inv_d, tag):
            c0 = smcol(4 * n)
            vv = sm[:, c0:c0 + n]
            r = sm[:, c0 + n:c0 + 2 * n]
            t = sm[:, c0 + 2 * n:c0 + 3 * n]
            u = sm[:, c0 + 3 * n:c0 + 4 * n]
            bv, br, bt_, bu = (buf(tag + "_vv"), buf(tag + "_r"), buf(tag + "_t"), buf(tag + "_u"))

            def chain():
                sc.op(DVE, lambda e: e.tensor_scalar(out=vv, in0=src_ap, scalar1=inv_d, scalar2=EPS,
                                                     op0=ALU.mult, op1=ALU.add),
                      reads=src_bufs, writes=[bv])
                sc.op(DVE, lambda e: e.tensor_scalar(out=r.bitcast(I32), in0=vv.bitcast(I32), scalar1=-0.5,
                                                     scalar2=1597463007.0, op0=ALU.mult, op1=ALU.add),
                      reads=[bv], writes=[br])
                for _ in range(2):
                    sc.op(DVE, lambda e: e.tensor_tensor(out=t, in0=r, in1=r, op=ALU.mult),
                          reads=[br], writes=[bt_])
                    sc.op(DVE, lambda e: e.scalar_tensor_tensor(out=u, in0=t, scalar=-0.5, in1=vv,
                                                                op0=ALU.mult, op1=ALU.mult),
                          reads=[bt_, bv], writes=[bu])
                    sc.op(DVE, lambda e: e.scalar_tensor_tensor(out=r, in0=u, scalar=1.5, in1=r,
                                                                op0=ALU.add, op1=ALU.mult),
                          reads=[bu, br], writes=[br])
            return chain, r, br

        sc.op(POOL, lambda e: e.memset(ones[:], 1.0), writes=[buf("ones")])
        sc.op(POOL, lambda e: e.memset(Z[:, :, 0:16], 0.0), writes=[buf("Z")])

        sc.dma(SP, lambda e: e.dma_start(out=cv[:], in_=cvec_d[:, :]), "c_cv", writes=[buf("cv")])
        sc.dma(SP, lambda e: e.dma_start(out=gp1[:], in_=n1pb_d[:, :]), "c_g1", writes=[buf("gp1")])
        sc.dma(SP, lambda e: e.dma_start(out=gp2[:], in_=n2pb_d[:, :]), "c_g2", writes=[buf("gp2")])
        sc.dma(SP, lambda e: e.dma_start(out=Bt[:].rearrange("p h t -> p (h t)"), in_=bspb_d[:, :]), "c_bt",
               writes=[buf("Bt")])
        sc.dma(SP, lambda e: e.dma_start(out=vg[:, 0, :], in_=wspT_d[:, :]), "c_ws", writes=[buf("vg0")])
        sc.dma(SP, lambda e: e.dma_start(out=vg[:, 1, :], in_=mask_d[:, :]), "c_mk", writes=[buf("vg1")])
        sc.dma(POOL, lambda e: e.dma_start(out=ident[:], in_=ident_d[:, :]), "c_id", writes=[buf("ident")])
        sc.dma(POOL, lambda e: e.dma_start(out=wpool_sb[:].rearrange("p g d -> p (g d)"), in_=wpool_d[:, :]),
               "c_wp", writes=[buf("wpool")])
        def x_load(i):
            slot = i % 3
            src = x_d[i * T:(i + 1) * T, :].rearrange("(s p) d -> p s d", p=128)
            dst = xb[:, slot, :].rearrange("p (s d) -> p s d", s=2)
            sc.dma(SP, lambda e: e.dma_start(out=dst, in_=src), "xl%d" % slot,
                   writes=[buf("xb%d_0" % slot), buf("xb%d_1" % slot)])
        win_v = win_d.rearrange("p (k n) -> p k n", k=8)
        for hh in range(2):
            sc.dma(POOL, lambda e, hh=hh: e.dma_start(out=w_in_sb[:, hh * 4:(hh + 1) * 4, :],
                                                      in_=win_v[:, hh * 4:(hh + 1) * 4, :]),
                   "w_in%d" % hh, writes=[buf("w_in%d" % hh)])
        W_IN = [buf("w_in0"), buf("w_in1")]
        c_th = smcol(8); c_hf = smcol(8); c_sc = smcol(8)
        sc.op(ACT, lambda e: e.activation(out=sm[:, c_th:c_th + 8], in_=cv[:, C_C:C_C + 8], func=AF.Tanh, scale=0.5),
              reads=[buf("cv")], writes=[buf("s_th")])
        sc.op(DVE, lambda e: e.tensor_scalar(out=sm[:, c_hf:c_hf + 8], in0=sm[:, c_th:c_th + 8], scalar1=1.0,
                                             scalar2=0.5, op0=ALU.add, op1=ALU.mult),
              reads=[buf("s_th")], writes=[buf("s_hf")])
        sc.op(DVE, lambda e: e.tensor_tensor(out=sm[:, c_sc:c_sc + 8], in0=sm[:, c_hf:c_hf + 8],
                                             in1=cv[:, C_C:C_C + 8], op=ALU.mult),
              reads=[buf("s_hf"), buf("cv")], writes=[buf("s_sc")])
        scv = sm[:, c_sc:c_sc + 8]

        wada_v = wada_d.rearrange("p (b k c) -> p b k c", b=24, k=8)
        CH_COL = {0: 0, 1: 8, 3: 16, 4: 24}
        for b in range(24):
            st = b % 3
            stg = xb[:, st, :].rearrange("p (k c) -> p k c", c=256)
            stg_bufs = [buf("xb%d_0" % st), buf("xb%d_1" % st)]
            sc.dma(SP, lambda e, b=b, stg=stg: e.dma_start(out=stg, in_=wada_v[:, b, :, :]),
                   "wa%d" % (b % 3), writes=stg_bufs)
            sc.dma(SP, lambda e, b=b: e.dma_start(out=rl[0:1, b % 2, 0:256], in_=bada_d[0:1, b * 256:(b + 1) * 256]),
                   "ba%d" % (b % 2), writes=[buf("rl%d" % (b % 2))])
            pr = 4 + b % 2

            accv = rl[:, b % 2, 256:512]

            mb_ = buf("macc%d" % (b % 2))
            sc.op(DVE, lambda e, stg=stg, accv=accv: e.tensor_scalar(out=accv, in0=stg[:, 0, :], scalar1=scv[:, 0:1],
                                                                 scalar2=None, op0=ALU.mult),
                  reads=stg_bufs + [buf("s_sc")], writes=[mb_])
            for k in range(1, 8):
                sc.op(DVE, lambda e, stg=stg, accv=accv, k=k: e.scalar_tensor_tensor(
                    out=accv, in0=stg[:, k, :], scalar=scv[:, k:k + 1], in1=accv, op0=ALU.mult, op1=ALU.add),
                    reads=stg_bufs + [buf("s_sc"), mb_], writes=[mb_])

            def mm_mod(e, b=b, pr=pr, accv=accv):
                e.matmul(ps[0:1, pr, 0:256], lhsT=ones[:, 0:1], rhs=accv, start=True, stop=False)
                return e.matmul(ps[0:1, pr, 0:256], lhsT=ones[0:1, 0:1], rhs=rl[0:1, b % 2, 0:256], start=False, stop=True)
            sc.op(PE, mm_mod, reads=[buf("macc%d" % (b % 2)), buf("ones"), buf("rl%d" % (b % 2))],
                  writes=[Bps(pr), buf("modgate%d" % b)])
            sc.op(DVE, lambda e, b=b, pr=pr: e.tensor_copy(out=tmp[0:1, b % 2, 0:256], in_=ps[0:1, pr, 0:256]),
                  reads=[Bps(pr)], writes=[buf("tmp%d" % (b % 2))])
            chunk = b // 4
            if chunk in (2, 5):
                gp = gp1 if chunk == 2 else gp2
                gpb = buf("gp1") if chunk == 2 else buf("gp2")
                pb = 6 + b % 2
                cols = slice((b % 4) * 256, (b % 4) * 256 + 256)
                sc.op(PE, lambda e, b=b, pb=pb: e.matmul(ps[:, pb, 0:256], lhsT=ones[0:1, :], rhs=tmp[0:1, b % 2, 0:256],
                                                        start=True, stop=True),
                      reads=[buf("tmp%d" % (b % 2)), buf("ones")], writes=[Bps(pb)])
                sc.op(DVE, lambda e, gp=gp, pb=pb, cols=cols: e.tensor_tensor(out=gp[:, cols], in0=ps[:, pb, 0:256],
                                                                          in1=gp[:, cols], op=ALU.mult),
                      reads=[Bps(pb), gpb], writes=[gpb])
            else:
                col0 = CH_COL[chunk] + (b % 4) * 2

                def mm_col(e, b=b, col0=col0):
                    ins = None
                    for q in range(2):
                        ins = e.matmul(ps[:, 0, col0 + q:col0 + q + 1], lhsT=tmp[0:1, b % 2, q * 128:(q + 1) * 128],
                                       rhs=ones[0:1, 0:1], start=True, stop=True)
                    return ins
                sc.op(PE, mm_col, reads=[buf("tmp%d" % (b % 2)), buf("ones")], writes=[Bps(0)])
        sc.dma(POOL, lambda e: e.dma_start(out=w_out_sb[:], in_=wout_d.rearrange("p (k n) -> p k n", k=8)),
               "w_out", reads=[buf("modgate8")], writes=[buf("w_out")])
        fc1_v = fc1_d.rearrange("p (k n) -> p k n", k=8)
        for q in range(4):
            sc.dma(POOL, lambda e, q=q: e.dma_start(out=fc1_sb[:, 2 * q:2 * q + 2, :], in_=fc1_v[:, 2 * q:2 * q + 2, :]),
                   "fc1_%d" % q, reads=[buf("modgate%d" % (12 + 4 * q if q < 3 else 23))], writes=[buf("fc1_%d" % q)])
        FC1 = [buf("fc1_%d" % q) for q in range(4)]

        x_load(0)
        sc.op(DVE, lambda e: e.tensor_copy(out=mc[:, 0:32], in_=ps[:, 0, 0:32]), reads=[Bps(0)], writes=[buf("mc_raw")])
        sc.op(DVE, lambda e: e.scalar_tensor_tensor(out=mc[:, 32:40], in0=mc[:, 8:16], scalar=1.0,
                                                    in1=cv[:, C_N1:C_N1 + 8], op0=ALU.add, op1=ALU.mult),
              reads=[buf("mc_raw"), buf("cv")], writes=[buf("mc_g1")])
        sc.op(DVE, lambda e: e.scalar_tensor_tensor(out=mc[:, 40:48], in0=mc[:, 24:32], scalar=1.0,
                                                    in1=cv[:, C_N2:C_N2 + 8], op0=ALU.add, op1=ALU.mult),
              reads=[buf("mc_raw"), buf("cv")], writes=[buf("mc_g2")])
        MODB = [buf("mc_raw"), buf("mc_g1"), buf("mc_g2")]
        G1, SH1, G2, SH2 = 32, 0, 40, 16

        sc.op(DVE, lambda e: e.tensor_tensor(out=vg[:, 0, :], in0=vg[:, 0, :], in1=vg[:, 1, :], op=ALU.mult),
              reads=[buf("vg0"), buf("vg1")], writes=[buf("vg0")])
        sc.op(ACT, lambda e: e.activation(out=wsp_sb[:].rearrange("p h t -> p (h t)"), in_=vg[:, 0, :], func=AF.Identity),
              reads=[buf("vg0")], writes=[buf("wsp")])
        sc.op(PE, lambda e: e.matmul(ps[:, 1, :], lhsT=ones[:, :], rhs=vg[:, 0, :], start=True, stop=True),
              reads=[buf("vg0"), buf("ones")], writes=[Bps(1)])

        def bt_fix(e):
            ins = None
            for h in range(4):
                ins = e.scalar_tensor_tensor(out=Bt[:, h, :], in0=ps[:, 1, h * 128:(h + 1) * 128],
                                             scalar=cv[:, C_LB + h:C_LB + h + 1], in1=Bt[:, h, :],
                                             op0=ALU.mult, op1=ALU.add)
            return ins
        sc.op(DVE, bt_fix, reads=[Bps(1), buf("cv"), buf("Bt")], writes=[buf("Bt")])

        c_ss1 = smcol(2)
        ch1, r1, br1 = rstd_chain(sm[:, c_ss1:c_ss1 + 2], [buf("ss1_0"), buf("ss1_1")], 2, 1.0 / D, "r1")
        c_vs = smcol(2); c_vq = smcol(2); c_mean = smcol(2); c_msq = smcol(2); c_var = smcol(2)
        chv, rv, brv = rstd_chain(sm[:, c_var:c_var + 2], [buf("var")], 2, 1.0, "rv")
        c_ssm = smcol(2); c_ssms = smcol(1)
        chm, rm, brm = rstd_chain(sm[:, c_ssms:c_ssms + 1], [buf("ssms")], 1, 1.0 / D, "rm")
        c_ss2 = smcol(2)
        ch2, r2, br2 = rstd_chain(sm[:, c_ss2:c_ss2 + 2], [buf("ss2_0"), buf("ss2_1")], 2, 1.0 / D, "r2")
        c_ssf = smcol(4); c_ssfs = smcol(2)
        chf, rf, brf = rstd_chain(sm[:, c_ssfs:c_ssfs + 2], [buf("ssfs")], 2, 1.0 / D, "rf")

        MB = (6, 7)
        FB = (4, 5)

        def norm_A(slot, ss_col, ss_name, chain, r_ap, r_buf):
            for s in range(2):
                sc.op(ACT, lambda e, s=s: e.activation(out=junk[:], in_=xs(slot, s), func=AF.Square,
                                                       accum_out=sm[:, ss_col + s:ss_col + s + 1]),
                      reads=[buf("xb%d_%d" % (slot, s))], writes=[buf("junk"), buf("%s_%d" % (ss_name, s))])
            chain()

        def norm_A2(slot, r_ap, r_buf):
            sc.op(ACT, lambda e: e.activation(out=hbf[:, 0, :], in_=xs(slot, 0), func=AF.Identity, scale=r_ap[:, 0:1]),
                  reads=[buf("xb%d_0" % slot), r_buf], writes=[buf("hbf_0")])
            sc.op(DVE, lambda e: e.tensor_scalar(out=hbf[:, 1, :], in0=xs(slot, 1), scalar1=r_ap[:, 1:2], scalar2=None,
                                                 op0=ALU.mult),
                  reads=[buf("xb%d_1" % slot), r_buf], writes=[buf("hbf_1")])

        def norm_B(dstT, dstT_buf, gcol, shcol):
            for s in range(2):
                mb = MB[s]

                def tr(e, s=s, mb=mb):
                    ins = None
                    for k in range(8):
                        ins = e.transpose(out=bank_bf(mb)[:, k * 128:(k + 1) * 128],
                                          in_=hbf[:, s, k * 128:(k + 1) * 128], identity=ident[:])
                    return ins
                sc.op(PE, tr, reads=[buf("hbf_%d" % s), buf("ident")], writes=[Bps(mb)])

                if s == 0:
                    def ev(e, s=s, mb=mb):
                        ins = None
                        for k in range(8):
                            ins = e.activation(out=dstT[:, k, s * 128:(s + 1) * 128],
                                               in_=bank_bf(mb)[:, k * 128:(k + 1) * 128], func=AF.Identity,
                                               scale=mc[:, gcol + k:gcol + k + 1], bias=mc[:, shcol + k:shcol + k + 1])
                        return ins
                    sc.op(ACT, ev, reads=[Bps(mb)] + MODB, writes=[buf(dstT_buf + "_%d" % s)])
                else:
                    def ev(e, s=s, mb=mb):
                        ins = None
                        for k in range(8):
                            ins = e.tensor_scalar(out=dstT[:, k, s * 128:(s + 1) * 128],
                                                  in0=bank_bf(mb)[:, k * 128:(k + 1) * 128],
                                                  scalar1=mc[:, gcol + k:gcol + k + 1],
                                                  scalar2=mc[:, shcol + k:shcol + k + 1], op0=ALU.mult, op1=ALU.add)
                        return ins
                    sc.op(DVE, ev, reads=[Bps(mb)] + MODB, writes=[buf(dstT_buf + "_%d" % s)])

        def mixer(i):
            slot = i % 3
            XB = [buf("xb%d_0" % slot), buf("xb%d_1" % slot)]
            hT = hP[:, i % 2]
            HT = [buf("hP%d_0" % (i % 2)), buf("hP%d_1" % (i % 2))]
            norm_A(slot, c_ss1, "ss1", ch1, r1, br1)
            yield
            yield
            norm_A2(slot, r1, br1)
            yield
            norm_B(hT, "hP%d" % (i % 2), G1, SH1)
            yield
            if i > 0:
                sc.op(DVE, lambda e: e.tensor_copy(out=Z[:, :, 0:16], in_=Z[:, :, 256:272]),
                      reads=[buf("Z")], writes=[buf("Z")])
            for gp_ in range(2):
                mb = MB[gp_]

                def mm_z(e, gp_=gp_, mb=mb):
                    ins = None
                    for gg in range(2):
                        g = gp_ * 2 + gg
                        for k in range(8):
                            ins = e.matmul(ps[:, mb, gg * T:(gg + 1) * T],
                                           lhsT=w_in_sb[:, k, 1024 + g * 128:1024 + (g + 1) * 128],
                                           rhs=hT[:, k, :], start=(k == 0), stop=(k == 7))
                    return ins
                sc.op(PE, mm_z, reads=HT + W_IN, writes=[Bps(mb)])
                sc.op(ACT, lambda e, gp_=gp_, mb=mb: e.activation(
                    out=Z[:, 2 * gp_:2 * gp_ + 2, 16:272], in_=ps[:, mb, :].rearrange("p (g t) -> p g t", g=2),
                    func=AF.Identity), reads=[Bps(mb)], writes=[buf("Z")])
            yield
            def pooling(g):
                m = g + 1
                w = 1 << m
                src = Z[:, g, :]
                src_b = buf("Z")
                for k in range(m):
                    lo = (1 << (k + 1)) - 1
                    sh = 1 << k
                    dst = pt[:, k % 2, :]
                    dst_b = buf("pt%d" % (k % 2))
                    sc.op(DVE, lambda e, src=src, dst=dst, lo=lo, sh=sh: e.tensor_tensor(
                        out=dst[:, lo:272], in0=src[:, lo:272], in1=src[:, lo - sh:272 - sh], op=ALU.add),
                        reads=[src_b], writes=[dst_b])
                    src, src_b = dst, dst_b
                sc.op(DVE, lambda e, src=src, g=g, w=w: e.scalar_tensor_tensor(
                    out=diff[:, g, :], in0=src[:, 16:272], scalar=1.0 / w, in1=Z[:, g, 16:272],
                    op0=ALU.mult, op1=ALU.subtract), reads=[src_b, buf("Z")], writes=[buf("diff")])
                if i == 0:
                    oth = pt[:, (m % 2), 0:16]
                    oth_b = buf("pt%d" % (m % 2))
                    sc.op(DVE, lambda e, src=src, g=g, oth=oth: e.tensor_tensor(
                        out=oth, in0=src[:, 16:32], in1=cv[:, C_RC + 16 * g:C_RC + 16 * g + 16], op=ALU.mult),
                        reads=[src_b, buf("cv")], writes=[oth_b])
                    sc.op(DVE, lambda e, g=g, oth=oth: e.tensor_tensor(
                        out=diff[:, g, 0:16], in0=oth, in1=Z[:, g, 16:32], op=ALU.subtract),
                        reads=[oth_b, buf("Z"), buf("diff")], writes=[buf("diff")])
            for cp in range(2):
                mb = MB[cp]

                def mm_u(e, cp=cp, mb=mb):
                    ins = None
                    for cc in range(2):
                        c = cp * 2 + cc
                        for k in range(8):
                            ins = e.matmul(ps[:, mb, cc * T:(cc + 1) * T], lhsT=w_in_sb[:, k, c * 128:(c + 1) * 128],
                                           rhs=hT[:, k, :], start=(k == 0), stop=(k == 7))
                    return ins
                sc.op(PE, mm_u, reads=HT + W_IN, writes=[Bps(mb)])
                sc.op(ACT, lambda e, cp=cp, mb=mb: e.activation(
                    out=ubf[:, 2 * cp:2 * cp + 2, :].rearrange("p c t -> p (c t)"), in_=ps[:, mb, :],
                    func=AF.Gelu_apprx_tanh), reads=[Bps(mb)], writes=[buf("ubf%d" % cp)])
            pooling(0)
            pooling(1)
            for s in range(2):
                mb = MB[s]

                def mm_v(e, s=s, mb=mb):
                    ins = None
                    for k in range(8):
                        ins = e.matmul(ps[:, mb, :], lhsT=hT[:, k, s * 128:(s + 1) * 128], rhs=w_in_sb[:, k, 512:1024],
                                       start=(k == 0), stop=(k == 7))
                    return ins
                sc.op(PE, mm_v, reads=HT + W_IN, writes=[Bps(mb)])
                sc.op(ACT, lambda e, s=s, mb=mb: e.activation(out=vg[:, s, :], in_=ps[:, mb, :], func=AF.Gelu_apprx_tanh,
                                                              accum_out=sm[:, c_vs + s:c_vs + s + 1]),
                      reads=[Bps(mb)], writes=[buf("vg%d" % s), buf("vs%d" % s)])
                sc.op(ACT, lambda e, s=s: e.activation(out=junk[:, 0:512], in_=vg[:, s, :], func=AF.Square,
                                                       accum_out=sm[:, c_vq + s:c_vq + s + 1]),
                      reads=[buf("vg%d" % s)], writes=[buf("junk"), buf("vq%d" % s)])
            sc.op(DVE, lambda e: e.tensor_scalar(out=sm[:, c_mean:c_mean + 2], in0=sm[:, c_vs:c_vs + 2],
                                                 scalar1=1.0 / 512, scalar2=None, op0=ALU.mult),
                  reads=[buf("vs0"), buf("vs1")], writes=[buf("mean")])
            sc.op(DVE, lambda e: e.tensor_tensor(out=sm[:, c_msq:c_msq + 2], in0=sm[:, c_mean:c_mean + 2],
                                                 in1=sm[:, c_mean:c_mean + 2], op=ALU.mult),
                  reads=[buf("mean")], writes=[buf("msq")])
            sc.op(DVE, lambda e: e.scalar_tensor_tensor(out=sm[:, c_var:c_var + 2], in0=sm[:, c_vq:c_vq + 2],
                                                        scalar=1.0 / 512, in1=sm[:, c_msq:c_msq + 2],
                                                        op0=ALU.mult, op1=ALU.subtract),
                  reads=[buf("vq0"), buf("vq1"), buf("msq")], writes=[buf("var")])
            chv()
            for s in range(2):
                sc.op(DVE, lambda e, s=s: e.tensor_scalar(out=vn[:, s, :], in0=vg[:, s, :],
                                                           scalar1=sm[:, c_mean + s:c_mean + s + 1],
                                                           scalar2=rv[:, s:s + 1], op0=ALU.subtract, op1=ALU.mult),
                      reads=[buf("vg%d" % s), buf("mean"), brv], writes=[buf("vn%d" % s)])
            yield
            pooling(2)
            pooling(3)
            yield
            for s in range(2):
                mb = MB[s]

                def mm_s(e, s=s, mb=mb):
                    ins = None
                    for h in range(4):
                        ins = e.matmul(ps[:, mb, h * 128:(h + 1) * 128], lhsT=vn[:, s, h * 128:(h + 1) * 128],
                                       rhs=wsp_sb[:, h, :], start=True, stop=True)
                    return ins
                sc.op(PE, mm_s, reads=[buf("vn%d" % s), buf("wsp")], writes=[Bps(mb)])

                def ev_s(e, s=s, mb=mb):
                    ins = None
                    for h in range(4):
                        ins = e.scalar_tensor_tensor(out=tmpS[:, h, :], in0=ps[:, mb, h * 128:(h + 1) * 128],
                                                     scalar=cv[:, C_LG + h:C_LG + h + 1], in1=Bt[:, h, :],
                                                     op0=ALU.mult, op1=ALU.add)
                    return ins
                sc.op(DVE, ev_s, reads=[Bps(mb), buf("cv"), buf("Bt")], writes=[buf("tmpS")])
                sc.op(POOL, lambda e, s=s: e.tensor_tensor(out=yT[:, 0:4, s * 128:(s + 1) * 128], in0=tmpS[:],
                                                           in1=ubf[:, :, s * 128:(s + 1) * 128], op=ALU.mult),
                      reads=[buf("tmpS"), buf("ubf0"), buf("ubf1")], writes=[buf("yTa%d" % s)])
            yield
            for gp_ in range(2):
                mb = MB[gp_]

                def mm_p(e, gp_=gp_, mb=mb):
                    ins = None
                    for gg in range(2):
                        g = gp_ * 2 + gg
                        ins = e.matmul(ps[:, mb, gg * T:(gg + 1) * T], lhsT=wpool_sb[:, g, :], rhs=diff[:, g, :],
                                       start=True, stop=True)
                    return ins
                sc.op(PE, mm_p, reads=[buf("diff"), buf("wpool")], writes=[Bps(mb)])

                def ev_p(e, gp_=gp_, mb=mb):
                    ins = None
                    for gg in range(2):
                        g = gp_ * 2 + gg
                        ins = e.tensor_scalar(out=yT[:, 4 + g, :], in0=ps[:, mb, gg * T:(gg + 1) * T],
                                              scalar1=cv[:, C_BP + g:C_BP + g + 1], scalar2=cv[:, C_PS + g:C_PS + g + 1],
                                              op0=ALU.add, op1=ALU.mult)
                    return ins
                sc.op(DVE, ev_p, reads=[Bps(mb), buf("cv")], writes=[buf("yTb%d" % gp_)])
            yield
            YT = [buf("yTa0"), buf("yTa1"), buf("yTb0"), buf("yTb1")]
            for s in range(2):
                for hf in range(2):
                    mb = MB[hf]

                    def mm_o(e, s=s, hf=hf, mb=mb):
                        ins = None
                        for k in range(8):
                            ins = e.matmul(ps[:, mb, :], lhsT=yT[:, k, s * 128:(s + 1) * 128],
                                           rhs=w_out_sb[:, k, hf * 512:(hf + 1) * 512], start=(k == 0), stop=(k == 7))
                        return ins
                    sc.op(PE, mm_o, reads=YT + [buf("w_out")], writes=[Bps(mb)])
                    sc.op(ACT, lambda e, hf=hf, mb=mb: e.activation(out=junk[:, 0:512], in_=ps[:, mb, :], func=AF.Square,
                                                                    accum_out=sm[:, c_ssm + hf:c_ssm + hf + 1]),
                          reads=[Bps(mb)], writes=[buf("junk"), buf("ssm%d" % hf)])
                for hf in range(2):
                    mb = MB[hf]
                    ti = 2 * s + hf
                    sc.op(DVE, lambda e, hf=hf, mb=mb, ti=ti: e.tensor_tensor(
                        out=stg4(ti), in0=ps[:, mb, :], in1=gp1[:, hf * 512:(hf + 1) * 512], op=ALU.mult),
                        reads=[Bps(mb), buf("gp1"), buf("ssm%d" % hf)], writes=[stg4_buf(ti)])
                sc.op(DVE, lambda e: e.tensor_tensor(out=sm[:, c_ssms:c_ssms + 1], in0=sm[:, c_ssm:c_ssm + 1],
                                                     in1=sm[:, c_ssm + 1:c_ssm + 2], op=ALU.add),
                      reads=[buf("ssm0"), buf("ssm1")], writes=[buf("ssms")])
                chm()
                for hf in range(2):
                    ti = 2 * s + hf
                    sc.op(DVE, lambda e, s=s, hf=hf, ti=ti: e.scalar_tensor_tensor(
                        out=xs(slot, s)[:, hf * 512:(hf + 1) * 512], in0=stg4(ti), scalar=rm[:, 0:1],
                        in1=xs(slot, s)[:, hf * 512:(hf + 1) * 512], op0=ALU.mult, op1=ALU.add),
                        reads=[stg4_buf(ti), brm, XB[s]], writes=[XB[s]])
                yield
            yield
            norm_A(slot, c_ss2, "ss2", ch2, r2, br2)
            yield
            norm_A2(slot, r2, br2)
            yield
            norm_B(hT, "hP%d" % (i % 2), G2, SH2)
            yield

        def slab_load(G):
            i, q = divmod(G, 16)
            if i >= NT:
                return
            sl = G % 3
            rows = slice(q * 256, (q + 1) * 256)
            if i == 0:
                sc.dma(POOL, lambda e: e.dma_start(out=fc2buf[:, sl, :, :],
                                                   in_=fc2_d[rows, :].rearrange("(j p) n -> p j n", p=128)),
                       "f2p_%d" % sl, writes=[buf("f2b%d" % sl)])
                sc.dma(SP, lambda e: e.dma_start(out=fc2s_d[rows, :].rearrange("(j p) n -> p j n", p=128),
                                                 in_=fc2buf[:, sl, :, :]),
                       "f2w%d" % sl, reads=[buf("f2b%d" % sl)], writes=[buf("f2s%d" % q)])
            else:
                sc.dma(SP, lambda e: e.dma_start(out=fc2buf[:, sl, :, :],
                                                 in_=fc2s_d[rows, :].rearrange("(j p) n -> p j n", p=128)),
                       "f2_%d" % sl, reads=[buf("f2s%d" % q)], writes=[buf("f2b%d" % sl)])

        def ffn(i):
            slot = i % 3
            XB = [buf("xb%d_0" % slot), buf("xb%d_1" % slot)]
            h2T = hP[:, i % 2]
            H2T = [buf("hP%d_0" % (i % 2)), buf("hP%d_1" % (i % 2))]
            if i == 0:
                slab_load(0)
                slab_load(1)
                slab_load(2)

            def fc1(jp):
                fb = FB[jp % 2]

                def mm(e):
                    ins = None
                    for jj in range(2):
                        j = jp * 2 + jj
                        for k in range(8):
                            ins = e.matmul(ps[:, fb, jj * T:(jj + 1) * T], lhsT=fc1_sb[:, k, j * 128:(j + 1) * 128],
                                           rhs=h2T[:, k, :], start=(k == 0), stop=(k == 7))
                    return ins
                sc.op(PE, mm, reads=H2T + FC1, writes=[Bps(fb)])
                sc.op(ACT, lambda e: e.activation(out=rl[:, jp % 2, :], in_=ps[:, fb, :], func=AF.Relu),
                      reads=[Bps(fb)], writes=[buf("rl%d" % (jp % 2))])
                sc.op(POOL, lambda e: e.tensor_tensor(out=hid[:, jp % 3, :], in0=rl[:, jp % 2, :], in1=rl[:, jp % 2, :],
                                                      op=ALU.mult),
                      reads=[buf("rl%d" % (jp % 2))], writes=[buf("hid%d" % (jp % 3))])

            def fc2(jp):
                sl = (16 * i + jp) % 3

                def mm(e):
                    ins = None
                    for jj in range(2):
                        j = jp * 2 + jj
                        for s in range(2):
                            for hf in range(2):
                                ins = e.matmul(ps[:, 2 * s + hf, :],
                                               lhsT=hid[:, jp % 3, jj * T + s * 128:jj * T + (s + 1) * 128],
                                               rhs=fc2buf[:, sl, jj, hf * 512:(hf + 1) * 512],
                                               start=(j == 0), stop=(j == NJ - 1))
                    return ins
                sc.op(PE, mm, reads=[buf("hid%d" % (jp % 3)), buf("f2b%d" % sl)], writes=[Bps(b) for b in range(4)])

            for jp in range(16):
                fc1(jp)
                if jp >= 2:
                    fc2(jp - 2)
                    slab_load(16 * i + jp + 1)
                yield
            fc2(14)
            slab_load(16 * i + 17)
            fc2(15)
            slab_load(16 * i + 18)
            for s in range(2):
                for hf in range(2):
                    b_ = 2 * s + hf
                    sc.op(ACT, lambda e, b_=b_: e.activation(out=junk[:, 0:512], in_=ps[:, b_, :], func=AF.Square,
                                                             accum_out=sm[:, c_ssf + b_:c_ssf + b_ + 1]),
                          reads=[Bps(b_)], writes=[buf("junk"), buf("ssf%d" % b_)])
            for s in range(2):
                for hf in range(2):
                    b_ = 2 * s + hf
                    sc.op(DVE, lambda e, hf=hf, b_=b_: e.tensor_tensor(
                        out=stg4(b_), in0=ps[:, b_, :], in1=gp2[:, hf * 512:(hf + 1) * 512], op=ALU.mult),
                        reads=[Bps(b_), buf("gp2"), buf("ssf%d" % b_)], writes=[stg4_buf(b_)])
            sc.op(DVE, lambda e: e.tensor_tensor(out=sm[:, c_ssfs:c_ssfs + 2],
                                                 in0=sm[:, c_ssf:c_ssf + 4].rearrange("p (s h) -> p s h", h=2)[:, :, 0],
                                                 in1=sm[:, c_ssf:c_ssf + 4].rearrange("p (s h) -> p s h", h=2)[:, :, 1],
                                                 op=ALU.add),
                  reads=[buf("ssf%d" % b_) for b_ in range(4)], writes=[buf("ssfs")])
            chf()
            for s in range(2):
                for hf in range(2):
                    b_ = 2 * s + hf
                    sc.op(DVE, lambda e, s=s, hf=hf, b_=b_: e.scalar_tensor_tensor(
                        out=xs(slot, s)[:, hf * 512:(hf + 1) * 512], in0=stg4(b_), scalar=rf[:, s:s + 1],
                        in1=xs(slot, s)[:, hf * 512:(hf + 1) * 512], op0=ALU.mult, op1=ALU.add),
                        reads=[stg4_buf(b_), brf, XB[s]], writes=[XB[s]])
            dst = out_d[i * T:(i + 1) * T, :].rearrange("(s p) d -> p s d", p=128)
            srcv = xb[:, slot, :].rearrange("p (s d) -> p s d", s=2)
            ev = sc.dma(SP, lambda e: e.dma_start(out=dst, in_=srcv), "xs%d" % slot, reads=XB)
            stores.append(ev)
            yield

        stores = []
        if NT > 1:
            x_load(1)
        for _ in mixer(0):
            pass
        for i in range(NT):
            gm = mixer(i + 1) if i + 1 < NT else None
            step = 0
            for _ in ffn(i):
                step += 1
                if step == 4 and i + 2 < NT:
                    x_load(i + 2)
                if gm is not None and step >= 2:
                    try:
                        next(gm)
                    except StopIteration:
                        gm = None
            if gm is not None:
                for _ in gm:
                    pass
        sc.wait(SP, stores)

        all_keys = set()
        for e_ in ENGS:
            for (deps, fn, key, amt) in sc.ops[e_]:
                if key is not None:
                    all_keys.add(key)
        with contextlib.ExitStack() as stack:
            for k_ in sorted(all_keys):
                sems[k_] = stack.enter_context(nc.semaphore("s_" + k_))
            block = stack.enter_context(nc.Block())

            def run(eng_name, eng):
                waited = {}
                for (deps, fn, key, amt) in sc.ops[eng_name]:
                    for (k_, v_) in deps:
                        if waited.get(k_, 0) >= v_:
                            continue
                        eng.wait_ge(sems[k_], v_)
                        waited[k_] = v_
                    if fn is None:
                        continue
                    ins = fn(eng)
                    ins.then_inc(sems[key], amt)

            @block.tensor
            def _(e):
                run(PE, e)

            @block.scalar
            def _(e):
                run(ACT, e)

            @block.vector
            def _(e):
                run(DVE, e)

            @block.gpsimd
            def _(e):
                run(POOL, e)

            @block.sync
            def _(e):
                run(SP, e)
    return nc


def _host_inputs(inputs, NT=16):
    f = lambda a: np.ascontiguousarray(np.asarray(a, dtype=np.float32))
    S = NT * T
    x = f(inputs["x"])
    c = f(inputs["c"])
    pmaj = lambda w: np.ascontiguousarray(f(w).reshape(8, 128, -1).transpose(1, 0, 2).reshape(128, -1))
    col = lambda v, n: np.ascontiguousarray(f(v).reshape(n, 128).T)
    rc = np.zeros((4, 16), np.float32)
    for g in range(4):
        w = 2 << g
        for t in range(16):
            rc[g, t] = 1.0 / min(t + 1, w)
    shared = {
        "n1pb": np.ascontiguousarray(np.broadcast_to(f(inputs["norm1_post"])[None, :], (128, D))),
        "n2pb": np.ascontiguousarray(np.broadcast_to(f(inputs["norm2_post"])[None, :], (128, D))),
        "bspb": np.ascontiguousarray(np.broadcast_to(f(inputs["b_spatial"]).reshape(1, 512), (128, 512))),
        "mask": np.ascontiguousarray(np.tile(np.triu(np.ones((128, 128), np.float32)), (1, 4))),
        "wspT": np.ascontiguousarray(f(inputs["w_spatial"]).transpose(2, 0, 1).reshape(128, 512)),
        "ident": np.eye(128, dtype=np.float32),
        "wpool": np.ascontiguousarray(f(inputs["w_pool"]).transpose(1, 0, 2).reshape(128, 512)),
        "bada": f(inputs["b_ada"]).reshape(1, 6144),
        "wada": np.ascontiguousarray(f(inputs["w_ada"]).reshape(8, 128, 24, 256).transpose(1, 2, 0, 3).reshape(128, -1)),
        "w_in": pmaj(inputs["w_in"]),
        "w_out": pmaj(inputs["w_out"]),
        "w_fc1": pmaj(inputs["w_fc1"]),
        "w_fc2": f(inputs["w_fc2"]),
    }
    in_maps = []
    for b in range(x.shape[0]):
        cvec = np.zeros((128, NCV), np.float32)
        cvec[:, C_C:C_C + 8] = col(c[b], 8)
        cvec[:, C_N1:C_N1 + 8] = col(inputs["norm1_pre"], 8)
        cvec[:, C_N2:C_N2 + 8] = col(inputs["norm2_pre"], 8)
        cvec[:, C_LG:C_LG + 4] = col(inputs["ln_v_gain"], 4)
        cvec[:, C_LB:C_LB + 4] = col(inputs["ln_v_bias"], 4)
        cvec[:, C_BP:C_BP + 4] = col(np.asarray(inputs["b_pool"]).reshape(-1), 4)
        cvec[:, C_PS:C_PS + 4] = col(inputs["pool_scale"], 4)
        cvec[:, C_RC:C_RC + 64] = rc.reshape(1, 64)
        m = dict(shared)
        m["x"] = np.ascontiguousarray(x[b, :S])
        m["cvec"] = cvec
        in_maps.append(m)
    return in_maps


def kernel(**inputs):
    in_maps = _host_inputs(inputs, 16)
    nc = build(16)
    res = run_bass_kernel_spmd(nc, in_maps, core_ids=list(range(len(in_maps))))
    return np.stack([np.asarray(r["out"], dtype=np.float32) for r in res.results], axis=0)
```

```python
import numpy as np
import concourse.bass as bass
import concourse.mybir as mybir
from concourse.bass_utils import run_bass_kernel_spmd

F32 = mybir.dt.float32
BF16 = mybir.dt.bfloat16
I32 = mybir.dt.int32
AF = mybir.ActivationFunctionType
ALU = mybir.AluOpType

D = 1024
SEQ = 4096
T = 256
NJ = 32
EPS = 1e-6
PE, ACT, DVE, POOL, SP = "pe", "act", "dve", "pool", "sp"
ENGS = (PE, ACT, DVE, POOL, SP)

C_C, C_N1, C_N2, C_LG, C_LB, C_BP, C_PS, C_RC = 0, 8, 16, 24, 28, 32, 36, 40
NCV = 40 + 64


class Buf:
    __slots__ = ("name", "w", "r")

    def __init__(self, name):
        self.name = name
        self.w = None
        self.r = []


class Sched:
    def __init__(self):
        self.ops = {e: [] for e in ENGS}
        self.count = {}

    def _deps(self, eng, reads, writes):
        deps = []
        for b in reads:
            if b.w is not None:
                deps.append((b.w, "raw"))
        for b in writes:
            if b.w is not None:
                deps.append((b.w, "waw"))
            for r in b.r:
                deps.append((r, "war"))
        out = []
        for (ev, kind) in deps:
            if ev[0] == eng:
                if eng == PE:
                    continue
            out.append(ev)
        return out

    def _record(self, eng, fn, key, amt, reads, writes):
        deps = self._deps(eng, reads, writes)
        self.count[key] = self.count.get(key, 0) + amt
        ev = (key, self.count[key])
        self.ops[eng].append((deps, fn, key, amt))
        for b in reads:
            b.r.append(ev)
        for b in writes:
            b.w = ev
            b.r = []
        return ev

    def op(self, eng, fn, reads=(), writes=()):
        return self._record(eng, fn, eng, 1, reads, writes)

    def dma(self, queue, fn, key, reads=(), writes=()):
        return self._record(queue, fn, key, 16, reads, writes)

    def wait(self, eng, events):
        self.ops[eng].append((list(events), None, None, 0))


def build(NT=16):
    S = NT * T
    nc = bass.Bass("TRN2", target_bir_lowering=False)
    dt_in = lambda name, shape: nc.dram_tensor(name, shape, F32, kind="ExternalInput").ap()
    x_d = dt_in("x", [S, D])
    cvec_d = dt_in("cvec", [128, NCV])
    n1pb_d = dt_in("n1pb", [128, D])
    n2pb_d = dt_in("n2pb", [128, D])
    bspb_d = dt_in("bspb", [128, 512])
    mask_d = dt_in("mask", [128, 512])
    wspT_d = dt_in("wspT", [128, 512])
    ident_d = dt_in("ident", [128, 128])
    wpool_d = dt_in("wpool", [128, 512])
    bada_d = dt_in("bada", [1, 6144])
    wada_d = dt_in("wada", [128, 8 * 6144])
    win_d = dt_in("w_in", [128, 8 * 1536])
    wout_d = dt_in("w_out", [128, 8 * 1024])
    fc1_d = dt_in("w_fc1", [128, 8 * 4096])
    fc2_d = dt_in("w_fc2", [4096, D])
    out_d = nc.dram_tensor("out", [S, D], F32, kind="ExternalOutput").ap()
    fc2s_d = nc.dram_tensor("fc2s", [4096, D], BF16, kind="Internal").ap()

    sc = Sched()
    sems = {}

    import contextlib
    with contextlib.ExitStack() as _st:
        w_in_sb = _st.enter_context(nc.sbuf_tensor("w_in_sb", [128, 8, 1536], BF16))
        w_out_sb = _st.enter_context(nc.sbuf_tensor("w_out_sb", [128, 8, 1024], BF16))
        fc1_sb = _st.enter_context(nc.sbuf_tensor("fc1_sb", [128, 8, 4096], BF16))
        fc2buf = _st.enter_context(nc.sbuf_tensor("fc2buf", [128, 3, 2, 1024], BF16))
        wsp_sb = _st.enter_context(nc.sbuf_tensor("wsp_sb", [128, 4, 128], BF16))
        wpool_sb = _st.enter_context(nc.sbuf_tensor("wpool_sb", [128, 4, 128], BF16))
        ident = _st.enter_context(nc.sbuf_tensor("ident_sb", [128, 128], BF16))
        ones = _st.enter_context(nc.sbuf_tensor("ones_sb", [128, 128], F32))
        gp1 = _st.enter_context(nc.sbuf_tensor("gp1", [128, D], F32))
        gp2 = _st.enter_context(nc.sbuf_tensor("gp2", [128, D], F32))
        Bt = _st.enter_context(nc.sbuf_tensor("Bt", [128, 4, 128], F32))
        cv = _st.enter_context(nc.sbuf_tensor("cv", [128, NCV], F32))
        mc = _st.enter_context(nc.sbuf_tensor("mc", [128, 64], F32))
        sm = _st.enter_context(nc.sbuf_tensor("sm", [128, 96], F32))
        xb = _st.enter_context(nc.sbuf_tensor("xb", [128, 3, 2048], F32))
        hbf = _st.enter_context(nc.sbuf_tensor("hbf", [128, 2, 1024], BF16))
        hP = _st.enter_context(nc.sbuf_tensor("hP", [128, 2, 8, T], BF16))
        junk = _st.enter_context(nc.sbuf_tensor("junk", [128, 1024], BF16))
        ubf = _st.enter_context(nc.sbuf_tensor("ubf", [128, 4, T], BF16))
        vg = _st.enter_context(nc.sbuf_tensor("vg", [128, 2, 512], F32))
        vn = _st.enter_context(nc.sbuf_tensor("vn", [128, 2, 512], BF16))
        Z = _st.enter_context(nc.sbuf_tensor("Z", [128, 4, 272], F32))
        pt = _st.enter_context(nc.sbuf_tensor("pt", [128, 2, 272], F32))
        diff = _st.enter_context(nc.sbuf_tensor("diff", [128, 4, T], BF16))
        tmpS = _st.enter_context(nc.sbuf_tensor("tmpS", [128, 4, 128], F32))
        yT = _st.enter_context(nc.sbuf_tensor("yT", [128, 8, T], BF16))
        tmp = _st.enter_context(nc.sbuf_tensor("tmp", [128, 2, 512], F32))
        rl = _st.enter_context(nc.sbuf_tensor("rl", [128, 2, 512], F32))
        hid = _st.enter_context(nc.sbuf_tensor("hid", [128, 3, 512], BF16))
        ps = _st.enter_context(nc.psum_tensor("ps", [128, 8, 512], F32))
        B = {}

        def buf(name):
            if name not in B:
                B[name] = Buf(name)
            return B[name]

        def bank(b):
            return ps[:, b, :]

        def bank_bf(b):
            return ps[:, b, :].bitcast(BF16)

        def Bps(b):
            return buf("ps%d" % b)

        def stg4(ti):
            return tmp[:, ti, :] if ti < 2 else vg[:, ti - 2, :]

        def stg4_buf(ti):
            return buf("tmp%d" % ti) if ti < 2 else buf("vg%d" % (ti - 2))

        def xs(slot, s):
            return xb[:, slot, s * 1024:(s + 1) * 1024]

        sm_next = [0]

        def smcol(n):
            a = sm_next[0]
            sm_next[0] += n
            assert sm_next[0] <= 96
            return a

        def rstd_chain(src_ap, src_bufs, n, inv_d, tag):
            c0 = smcol(4 * n)
            vv = sm[:, c0:c0 + n]
            r = sm[:, c0 + n:c0 + 2 * n]
            t = sm[:, c0 + 2 * n:c0 + 3 * n]
            u = sm[:, c0 + 3 * n:c0 + 4 * n]
            bv, br, bt_, bu = (buf(tag + "_vv"), buf(tag + "_r"), buf(tag + "_t"), buf(tag + "_u"))

            def chain():
                sc.op(DVE, lambda e: e.tensor_scalar(out=vv, in0=src_ap, scalar1=inv_d, scalar2=EPS,
                                                     op0=ALU.mult, op1=ALU.add),
                      reads=src_bufs, writes=[bv])
                sc.op(DVE, lambda e: e.tensor_scalar(out=r.bitcast(I32), in0=vv.bitcast(I32), scalar1=-0.5,
                                                     scalar2=1597463007.0, op0=ALU.mult, op1=ALU.add),
                      reads=[bv], writes=[br])
                for _ in range(2):
                    sc.op(DVE, lambda e: e.tensor_tensor(out=t, in0=r, in1=r, op=ALU.mult),
                          reads=[br], writes=[bt_])
                    sc.op(DVE, lambda e: e.scalar_tensor_tensor(out=u, in0=t, scalar=-0.5, in1=vv,
                                                                op0=ALU.mult, op1=ALU.mult),
                          reads=[bt_, bv], writes=[bu])
                    sc.op(DVE, lambda e: e.scalar_tensor_tensor(out=r, in0=u, scalar=1.5, in1=r,
                                                                op0=ALU.add, op1=ALU.mult),
                          reads=[bu, br], writes=[br])
            return chain, r, br

        sc.op(POOL, lambda e: e.memset(ones[:], 1.0), writes=[buf("ones")])
        sc.op(POOL, lambda e: e.memset(Z[:, :, 0:16], 0.0), writes=[buf("Z")])

        sc.dma(SP, lambda e: e.dma_start(out=cv[:], in_=cvec_d[:, :]), "c_cv", writes=[buf("cv")])
        sc.dma(SP, lambda e: e.dma_start(out=gp1[:], in_=n1pb_d[:, :]), "c_g1", writes=[buf("gp1")])
        sc.dma(SP, lambda e: e.dma_start(out=gp2[:], in_=n2pb_d[:, :]), "c_g2", writes=[buf("gp2")])
        sc.dma(SP, lambda e: e.dma_start(out=Bt[:].rearrange("p h t -> p (h t)"), in_=bspb_d[:, :]), "c_bt",
               writes=[buf("Bt")])
        sc.dma(SP, lambda e: e.dma_start(out=vg[:, 0, :], in_=wspT_d[:, :]), "c_ws", writes=[buf("vg0")])
        sc.dma(SP, lambda e: e.dma_start(out=vg[:, 1, :], in_=mask_d[:, :]), "c_mk", writes=[buf("vg1")])
        sc.dma(POOL, lambda e: e.dma_start(out=ident[:], in_=ident_d[:, :]), "c_id", writes=[buf("ident")])
        sc.dma(POOL, lambda e: e.dma_start(out=wpool_sb[:].rearrange("p g d -> p (g d)"), in_=wpool_d[:, :]),
               "c_wp", writes=[buf("wpool")])
        def x_load(i):
            slot = i % 3
            src = x_d[i * T:(i + 1) * T, :].rearrange("(s p) d -> p s d", p=128)
            dst = xb[:, slot, :].rearrange("p (s d) -> p s d", s=2)
            sc.dma(SP, lambda e: e.dma_start(out=dst, in_=src), "xl%d" % slot,
                   writes=[buf("xb%d_0" % slot), buf("xb%d_1" % slot)])
        win_v = win_d.rearrange("p (k n) -> p k n", k=8)
        for hh in range(2):
            sc.dma(POOL, lambda e, hh=hh: e.dma_start(out=w_in_sb[:, hh * 4:(hh + 1) * 4, :],
                                                      in_=win_v[:, hh * 4:(hh + 1) * 4, :]),
                   "w_in%d" % hh, writes=[buf("w_in%d" % hh)])
        W_IN = [buf("w_in0"), buf("w_in1")]
        c_th = smcol(8); c_hf = smcol(8); c_sc = smcol(8)
        sc.op(ACT, lambda e: e.activation(out=sm[:, c_th:c_th + 8], in_=cv[:, C_C:C_C + 8], func=AF.Tanh, scale=0.5),
              reads=[buf("cv")], writes=[buf("s_th")])
        sc.op(DVE, lambda e: e.tensor_scalar(out=sm[:, c_hf:c_hf + 8], in0=sm[:, c_th:c_th + 8], scalar1=1.0,
                                             scalar2=0.5, op0=ALU.add, op1=ALU.mult),
              reads=[buf("s_th")], writes=[buf("s_hf")])
        sc.op(DVE, lambda e: e.tensor_tensor(out=sm[:, c_sc:c_sc + 8], in0=sm[:, c_hf:c_hf + 8],
                                             in1=cv[:, C_C:C_C + 8], op=ALU.mult),
              reads=[buf("s_hf"), buf("cv")], writes=[buf("s_sc")])
        scv = sm[:, c_sc:c_sc + 8]

        wada_v = wada_d.rearrange("p (b k c) -> p b k c", b=24, k=8)
        CH_COL = {0: 0, 1: 8, 3: 16, 4: 24}
        for b in range(24):
            st = b % 3
            stg = xb[:, st, :].rearrange("p (k c) -> p k c", c=256)
            stg_bufs = [buf("xb%d_0" % st), buf("xb%d_1" % st)]
            sc.dma(SP, lambda e, b=b, stg=stg: e.dma_start(out=stg, in_=wada_v[:, b, :, :]),
                   "wa%d" % (b % 3), writes=stg_bufs)
            sc.dma(SP, lambda e, b=b: e.dma_start(out=rl[0:1, b % 2, 0:256], in_=bada_d[0:1, b * 256:(b + 1) * 256]),
                   "ba%d" % (b % 2), writes=[buf("rl%d" % (b % 2))])
            pr = 4 + b % 2

            accv = rl[:, b % 2, 256:512]

            mb_ = buf("macc%d" % (b % 2))

            def act_prod(e, stg=stg):
                ins = None
                for k in range(8):
                    ins = e.activation(out=stg[:, k, :], in_=stg[:, k, :], func=AF.Identity, scale=scv[:, k:k + 1])
                return ins
            sc.op(ACT, act_prod, reads=stg_bufs + [buf("s_sc")], writes=stg_bufs)
            sc.op(DVE, lambda e, stg=stg: e.tensor_tensor(out=stg[:, 0:4, :], in0=stg[:, 0:4, :], in1=stg[:, 4:8, :],
                                                          op=ALU.add), reads=stg_bufs, writes=stg_bufs)
            sc.op(DVE, lambda e, stg=stg: e.tensor_tensor(out=stg[:, 0:2, :], in0=stg[:, 0:2, :], in1=stg[:, 2:4, :],
                                                          op=ALU.add), reads=stg_bufs, writes=stg_bufs)
            sc.op(DVE, lambda e, stg=stg, accv=accv: e.tensor_tensor(out=accv, in0=stg[:, 0, :], in1=stg[:, 1, :],
                                                                     op=ALU.add), reads=stg_bufs, writes=[mb_])

            def mm_mod(e, b=b, pr=pr, accv=accv):
                e.matmul(ps[0:1, pr, 0:256], lhsT=ones[:, 0:1], rhs=accv, start=True, stop=False)
                return e.matmul(ps[0:1, pr, 0:256], lhsT=ones[0:1, 0:1], rhs=rl[0:1, b % 2, 0:256], start=False, stop=True)
            sc.op(PE, mm_mod, reads=[buf("macc%d" % (b % 2)), buf("ones"), buf("rl%d" % (b % 2))],
                  writes=[Bps(pr), buf("modgate%d" % b)])
            sc.op(DVE, lambda e, b=b, pr=pr: e.tensor_copy(out=tmp[0:1, b % 2, 0:256], in_=ps[0:1, pr, 0:256]),
                  reads=[Bps(pr)], writes=[buf("tmp%d" % (b % 2))])
            chunk = b // 4
            if chunk in (2, 5):
                gp = gp1 if chunk == 2 else gp2
                gpb = buf("gp1") if chunk == 2 else buf("gp2")
                pb = 6 + b % 2
                cols = slice((b % 4) * 256, (b % 4) * 256 + 256)
                sc.op(PE, lambda e, b=b, pb=pb: e.matmul(ps[:, pb, 0:256], lhsT=ones[0:1, :], rhs=tmp[0:1, b % 2, 0:256],
                                                        start=True, stop=True),
                      reads=[buf("tmp%d" % (b % 2)), buf("ones")], writes=[Bps(pb)])
                sc.op(DVE, lambda e, gp=gp, pb=pb, cols=cols: e.tensor_tensor(out=gp[:, cols], in0=ps[:, pb, 0:256],
                                                                          in1=gp[:, cols], op=ALU.mult),
                      reads=[Bps(pb), gpb], writes=[gpb])
            else:
                col0 = CH_COL[chunk] + (b % 4) * 2

                def mm_col(e, b=b, col0=col0):
                    ins = None
                    for q in range(2):
                        ins = e.matmul(ps[:, 0, col0 + q:col0 + q + 1], lhsT=tmp[0:1, b % 2, q * 128:(q + 1) * 128],
                                       rhs=ones[0:1, 0:1], start=True, stop=True)
                    return ins
                sc.op(PE, mm_col, reads=[buf("tmp%d" % (b % 2)), buf("ones")], writes=[Bps(0)])
        sc.dma(POOL, lambda e: e.dma_start(out=w_out_sb[:], in_=wout_d.rearrange("p (k n) -> p k n", k=8)),
               "w_out", reads=[buf("modgate8")], writes=[buf("w_out")])
        fc1_v = fc1_d.rearrange("p (k n) -> p k n", k=8)
        for q in range(4):
            sc.dma(POOL, lambda e, q=q: e.dma_start(out=fc1_sb[:, 2 * q:2 * q + 2, :], in_=fc1_v[:, 2 * q:2 * q + 2, :]),
                   "fc1_%d" % q, reads=[buf("modgate%d" % (12 + 4 * q if q < 3 else 23))], writes=[buf("fc1_%d" % q)])
        FC1 = [buf("fc1_%d" % q) for q in range(4)]

        x_load(0)
        sc.op(DVE, lambda e: e.tensor_copy(out=mc[:, 0:32], in_=ps[:, 0, 0:32]), reads=[Bps(0)], writes=[buf("mc_raw")])
        sc.op(DVE, lambda e: e.scalar_tensor_tensor(out=mc[:, 32:40], in0=mc[:, 8:16], scalar=1.0,
                                                    in1=cv[:, C_N1:C_N1 + 8], op0=ALU.add, op1=ALU.mult),
              reads=[buf("mc_raw"), buf("cv")], writes=[buf("mc_g1")])
        sc.op(DVE, lambda e: e.scalar_tensor_tensor(out=mc[:, 40:48], in0=mc[:, 24:32], scalar=1.0,
                                                    in1=cv[:, C_N2:C_N2 + 8], op0=ALU.add, op1=ALU.mult),
              reads=[buf("mc_raw"), buf("cv")], writes=[buf("mc_g2")])
        MODB = [buf("mc_raw"), buf("mc_g1"), buf("mc_g2")]
        G1, SH1, G2, SH2 = 32, 0, 40, 16

        sc.op(DVE, lambda e: e.tensor_tensor(out=vg[:, 0, :], in0=vg[:, 0, :], in1=vg[:, 1, :], op=ALU.mult),
              reads=[buf("vg0"), buf("vg1")], writes=[buf("vg0")])
        sc.op(ACT, lambda e: e.activation(out=wsp_sb[:].rearrange("p h t -> p (h t)"), in_=vg[:, 0, :], func=AF.Identity),
              reads=[buf("vg0")], writes=[buf("wsp")])
        sc.op(PE, lambda e: e.matmul(ps[:, 1, :], lhsT=ones[:, :], rhs=vg[:, 0, :], start=True, stop=True),
              reads=[buf("vg0"), buf("ones")], writes=[Bps(1)])

        def bt_fix(e):
            ins = None
            for h in range(4):
                ins = e.scalar_tensor_tensor(out=Bt[:, h, :], in0=ps[:, 1, h * 128:(h + 1) * 128],
                                             scalar=cv[:, C_LB + h:C_LB + h + 1], in1=Bt[:, h, :],
                                             op0=ALU.mult, op1=ALU.add)
            return ins
        sc.op(DVE, bt_fix, reads=[Bps(1), buf("cv"), buf("Bt")], writes=[buf("Bt")])

        c_ss1 = smcol(2)
        ch1, r1, br1 = rstd_chain(sm[:, c_ss1:c_ss1 + 2], [buf("ss1_0"), buf("ss1_1")], 2, 1.0 / D, "r1")
        c_vs = smcol(2); c_vq = smcol(2); c_mean = smcol(2); c_msq = smcol(2); c_var = smcol(2)
        chv, rv, brv = rstd_chain(sm[:, c_var:c_var + 2], [buf("var")], 2, 1.0, "rv")
        c_ssm = smcol(2); c_ssms = smcol(1)
        chm, rm, brm = rstd_chain(sm[:, c_ssms:c_ssms + 1], [buf("ssms")], 1, 1.0 / D, "rm")
        c_ss2 = smcol(2)
        ch2, r2, br2 = rstd_chain(sm[:, c_ss2:c_ss2 + 2], [buf("ss2_0"), buf("ss2_1")], 2, 1.0 / D, "r2")
        c_ssf = smcol(4); c_ssfs = smcol(2)
        chf, rf, brf = rstd_chain(sm[:, c_ssfs:c_ssfs + 2], [buf("ssfs")], 2, 1.0 / D, "rf")

        MB = (6, 7)
        FB = (4, 5)

        def norm_A(slot, ss_col, ss_name, chain, r_ap, r_buf):
            for s in range(2):
                sc.op(ACT, lambda e, s=s: e.activation(out=junk[:], in_=xs(slot, s), func=AF.Square,
                                                       accum_out=sm[:, ss_col + s:ss_col + s + 1]),
                      reads=[buf("xb%d_%d" % (slot, s))], writes=[buf("junk"), buf("%s_%d" % (ss_name, s))])
            chain()

        def norm_A2(slot, r_ap, r_buf):
            sc.op(ACT, lambda e: e.activation(out=hbf[:, 0, :], in_=xs(slot, 0), func=AF.Identity, scale=r_ap[:, 0:1]),
                  reads=[buf("xb%d_0" % slot), r_buf], writes=[buf("hbf_0")])
            sc.op(DVE, lambda e: e.tensor_scalar(out=hbf[:, 1, :], in0=xs(slot, 1), scalar1=r_ap[:, 1:2], scalar2=None,
                                                 op0=ALU.mult),
                  reads=[buf("xb%d_1" % slot), r_buf], writes=[buf("hbf_1")])

        def norm_B(dstT, dstT_buf, gcol, shcol):
            for s in range(2):
                mb = MB[s]

                def tr(e, s=s, mb=mb):
                    ins = None
                    for k in range(8):
                        ins = e.transpose(out=bank_bf(mb)[:, k * 128:(k + 1) * 128],
                                          in_=hbf[:, s, k * 128:(k + 1) * 128], identity=ident[:])
                    return ins
                sc.op(PE, tr, reads=[buf("hbf_%d" % s), buf("ident")], writes=[Bps(mb)])

                if s == 0:
                    def ev(e, s=s, mb=mb):
                        ins = None
                        for k in range(8):
                            ins = e.activation(out=dstT[:, k, s * 128:(s + 1) * 128],
                                               in_=bank_bf(mb)[:, k * 128:(k + 1) * 128], func=AF.Identity,
                                               scale=mc[:, gcol + k:gcol + k + 1], bias=mc[:, shcol + k:shcol + k + 1])
                        return ins
                    sc.op(ACT, ev, reads=[Bps(mb)] + MODB, writes=[buf(dstT_buf + "_%d" % s)])
                else:
                    def ev(e, s=s, mb=mb):
                        ins = None
                        for k in range(8):
                            ins = e.tensor_scalar(out=dstT[:, k, s * 128:(s + 1) * 128],
                                                  in0=bank_bf(mb)[:, k * 128:(k + 1) * 128],
                                                  scalar1=mc[:, gcol + k:gcol + k + 1],
                                                  scalar2=mc[:, shcol + k:shcol + k + 1], op0=ALU.mult, op1=ALU.add)
                        return ins
                    sc.op(DVE, ev, reads=[Bps(mb)] + MODB, writes=[buf(dstT_buf + "_%d" % s)])

        def mixer(i):
            slot = i % 3
            XB = [buf("xb%d_0" % slot), buf("xb%d_1" % slot)]
            hT = hP[:, i % 2]
            HT = [buf("hP%d_0" % (i % 2)), buf("hP%d_1" % (i % 2))]
            norm_A(slot, c_ss1, "ss1", ch1, r1, br1)
            yield
            yield
            norm_A2(slot, r1, br1)
            yield
            norm_B(hT, "hP%d" % (i % 2), G1, SH1)
            yield
            if i > 0:
                sc.op(DVE, lambda e: e.tensor_copy(out=Z[:, :, 0:16], in_=Z[:, :, 256:272]),
                      reads=[buf("Z")], writes=[buf("Z")])
            for gp_ in range(2):
                mb = MB[gp_]

                def mm_z(e, gp_=gp_, mb=mb):
                    ins = None
                    for gg in range(2):
                        g = gp_ * 2 + gg
                        for k in range(8):
                            ins = e.matmul(ps[:, mb, gg * T:(gg + 1) * T],
                                           lhsT=w_in_sb[:, k, 1024 + g * 128:1024 + (g + 1) * 128],
                                           rhs=hT[:, k, :], start=(k == 0), stop=(k == 7))
                    return ins
                sc.op(PE, mm_z, reads=HT + W_IN, writes=[Bps(mb)])
                sc.op(ACT, lambda e, gp_=gp_, mb=mb: e.activation(
                    out=Z[:, 2 * gp_:2 * gp_ + 2, 16:272], in_=ps[:, mb, :].rearrange("p (g t) -> p g t", g=2),
                    func=AF.Identity), reads=[Bps(mb)], writes=[buf("Z")])
            yield
            def pooling(g):
                m = g + 1
                w = 1 << m
                src = Z[:, g, :]
                src_b = buf("Z")
                for k in range(m):
                    lo = (1 << (k + 1)) - 1
                    sh = 1 << k
                    dst = pt[:, k % 2, :]
                    dst_b = buf("pt%d" % (k % 2))
                    sc.op(DVE, lambda e, src=src, dst=dst, lo=lo, sh=sh: e.tensor_tensor(
                        out=dst[:, lo:272], in0=src[:, lo:272], in1=src[:, lo - sh:272 - sh], op=ALU.add),
                        reads=[src_b], writes=[dst_b])
                    src, src_b = dst, dst_b
                sc.op(DVE, lambda e, src=src, g=g, w=w: e.scalar_tensor_tensor(
                    out=diff[:, g, :], in0=src[:, 16:272], scalar=1.0 / w, in1=Z[:, g, 16:272],
                    op0=ALU.mult, op1=ALU.subtract), reads=[src_b, buf("Z")], writes=[buf("diff")])
                if i == 0:
                    oth = pt[:, (m % 2), 0:16]
                    oth_b = buf("pt%d" % (m % 2))
                    sc.op(DVE, lambda e, src=src, g=g, oth=oth: e.tensor_tensor(
                        out=oth, in0=src[:, 16:32], in1=cv[:, C_RC + 16 * g:C_RC + 16 * g + 16], op=ALU.mult),
                        reads=[src_b, buf("cv")], writes=[oth_b])
                    sc.op(DVE, lambda e, g=g, oth=oth: e.tensor_tensor(
                        out=diff[:, g, 0:16], in0=oth, in1=Z[:, g, 16:32], op=ALU.subtract),
                        reads=[oth_b, buf("Z"), buf("diff")], writes=[buf("diff")])
            for cp in range(2):
                mb = MB[cp]

                def mm_u(e, cp=cp, mb=mb):
                    ins = None
                    for cc in range(2):
                        c = cp * 2 + cc
                        for k in range(8):
                            ins = e.matmul(ps[:, mb, cc * T:(cc + 1) * T], lhsT=w_in_sb[:, k, c * 128:(c + 1) * 128],
                                           rhs=hT[:, k, :], start=(k == 0), stop=(k == 7))
                    return ins
                sc.op(PE, mm_u, reads=HT + W_IN, writes=[Bps(mb)])
                sc.op(ACT, lambda e, cp=cp, mb=mb: e.activation(
                    out=ubf[:, 2 * cp:2 * cp + 2, :].rearrange("p c t -> p (c t)"), in_=ps[:, mb, :],
                    func=AF.Gelu_apprx_tanh), reads=[Bps(mb)], writes=[buf("ubf%d" % cp)])
            pooling(0)
            pooling(1)
            for s in range(2):
                mb = MB[s]

                def mm_v(e, s=s, mb=mb):
                    ins = None
                    for k in range(8):
                        ins = e.matmul(ps[:, mb, :], lhsT=hT[:, k, s * 128:(s + 1) * 128], rhs=w_in_sb[:, k, 512:1024],
                                       start=(k == 0), stop=(k == 7))
                    return ins
                sc.op(PE, mm_v, reads=HT + W_IN, writes=[Bps(mb)])
                sc.op(ACT, lambda e, s=s, mb=mb: e.activation(out=vg[:, s, :], in_=ps[:, mb, :], func=AF.Gelu_apprx_tanh,
                                                              accum_out=sm[:, c_vs + s:c_vs + s + 1]),
                      reads=[Bps(mb)], writes=[buf("vg%d" % s), buf("vs%d" % s)])
                sc.op(ACT, lambda e, s=s: e.activation(out=junk[:, 0:512], in_=vg[:, s, :], func=AF.Square,
                                                       accum_out=sm[:, c_vq + s:c_vq + s + 1]),
                      reads=[buf("vg%d" % s)], writes=[buf("junk"), buf("vq%d" % s)])
            sc.op(DVE, lambda e: e.tensor_scalar(out=sm[:, c_mean:c_mean + 2], in0=sm[:, c_vs:c_vs + 2],
                                                 scalar1=1.0 / 512, scalar2=None, op0=ALU.mult),
                  reads=[buf("vs0"), buf("vs1")], writes=[buf("mean")])
            sc.op(DVE, lambda e: e.tensor_tensor(out=sm[:, c_msq:c_msq + 2], in0=sm[:, c_mean:c_mean + 2],
                                                 in1=sm[:, c_mean:c_mean + 2], op=ALU.mult),
                  reads=[buf("mean")], writes=[buf("msq")])
            sc.op(DVE, lambda e: e.scalar_tensor_tensor(out=sm[:, c_var:c_var + 2], in0=sm[:, c_vq:c_vq + 2],
                                                        scalar=1.0 / 512, in1=sm[:, c_msq:c_msq + 2],
                                                        op0=ALU.mult, op1=ALU.subtract),
                  reads=[buf("vq0"), buf("vq1"), buf("msq")], writes=[buf("var")])
            chv()
            for s in range(2):
                sc.op(DVE, lambda e, s=s: e.tensor_scalar(out=vn[:, s, :], in0=vg[:, s, :],
                                                           scalar1=sm[:, c_mean + s:c_mean + s + 1],
                                                           scalar2=rv[:, s:s + 1], op0=ALU.subtract, op1=ALU.mult),
                      reads=[buf("vg%d" % s), buf("mean"), brv], writes=[buf("vn%d" % s)])
            yield
            pooling(2)
            pooling(3)
            yield
            for s in range(2):
                mb = MB[s]

                def mm_s(e, s=s, mb=mb):
                    ins = None
                    for h in range(4):
                        ins = e.matmul(ps[:, mb, h * 128:(h + 1) * 128], lhsT=vn[:, s, h * 128:(h + 1) * 128],
                                       rhs=wsp_sb[:, h, :], start=True, stop=True)
                    return ins
                sc.op(PE, mm_s, reads=[buf("vn%d" % s), buf("wsp")], writes=[Bps(mb)])

                def ev_s(e, s=s, mb=mb):
                    ins = None
                    for h in range(4):
                        ins = e.scalar_tensor_tensor(out=tmpS[:, h, :], in0=ps[:, mb, h * 128:(h + 1) * 128],
                                                     scalar=cv[:, C_LG + h:C_LG + h + 1], in1=Bt[:, h, :],
                                                     op0=ALU.mult, op1=ALU.add)
                    return ins
                sc.op(DVE, ev_s, reads=[Bps(mb), buf("cv"), buf("Bt")], writes=[buf("tmpS")])
                sc.op(POOL, lambda e, s=s: e.tensor_tensor(out=yT[:, 0:4, s * 128:(s + 1) * 128], in0=tmpS[:],
                                                           in1=ubf[:, :, s * 128:(s + 1) * 128], op=ALU.mult),
                      reads=[buf("tmpS"), buf("ubf0"), buf("ubf1")], writes=[buf("yTa%d" % s)])
            yield
            for gp_ in range(2):
                mb = MB[gp_]

                def mm_p(e, gp_=gp_, mb=mb):
                    ins = None
                    for gg in range(2):
                        g = gp_ * 2 + gg
                        ins = e.matmul(ps[:, mb, gg * T:(gg + 1) * T], lhsT=wpool_sb[:, g, :], rhs=diff[:, g, :],
                                       start=True, stop=True)
                    return ins
                sc.op(PE, mm_p, reads=[buf("diff"), buf("wpool")], writes=[Bps(mb)])

                def ev_p(e, gp_=gp_, mb=mb):
                    ins = None
                    for gg in range(2):
                        g = gp_ * 2 + gg
                        ins = e.tensor_scalar(out=yT[:, 4 + g, :], in0=ps[:, mb, gg * T:(gg + 1) * T],
                                              scalar1=cv[:, C_BP + g:C_BP + g + 1], scalar2=cv[:, C_PS + g:C_PS + g + 1],
                                              op0=ALU.add, op1=ALU.mult)
                    return ins
                sc.op(DVE, ev_p, reads=[Bps(mb), buf("cv")], writes=[buf("yTb%d" % gp_)])
            yield
            YT = [buf("yTa0"), buf("yTa1"), buf("yTb0"), buf("yTb1")]
            for s in range(2):
                for hf in range(2):
                    mb = MB[hf]

                    def mm_o(e, s=s, hf=hf, mb=mb):
                        ins = None
                        for k in range(8):
                            ins = e.matmul(ps[:, mb, :], lhsT=yT[:, k, s * 128:(s + 1) * 128],
                                           rhs=w_out_sb[:, k, hf * 512:(hf + 1) * 512], start=(k == 0), stop=(k == 7))
                        return ins
                    sc.op(PE, mm_o, reads=YT + [buf("w_out")], writes=[Bps(mb)])
                    sc.op(ACT, lambda e, hf=hf, mb=mb: e.activation(out=junk[:, 0:512], in_=ps[:, mb, :], func=AF.Square,
                                                                    accum_out=sm[:, c_ssm + hf:c_ssm + hf + 1]),
                          reads=[Bps(mb)], writes=[buf("junk"), buf("ssm%d" % hf)])
                for hf in range(2):
                    mb = MB[hf]
                    ti = 2 * s + hf
                    sc.op(DVE, lambda e, hf=hf, mb=mb, ti=ti: e.tensor_tensor(
                        out=stg4(ti), in0=ps[:, mb, :], in1=gp1[:, hf * 512:(hf + 1) * 512], op=ALU.mult),
                        reads=[Bps(mb), buf("gp1"), buf("ssm%d" % hf)], writes=[stg4_buf(ti)])
                sc.op(DVE, lambda e: e.tensor_tensor(out=sm[:, c_ssms:c_ssms + 1], in0=sm[:, c_ssm:c_ssm + 1],
                                                     in1=sm[:, c_ssm + 1:c_ssm + 2], op=ALU.add),
                      reads=[buf("ssm0"), buf("ssm1")], writes=[buf("ssms")])
                chm()
                for hf in range(2):
                    ti = 2 * s + hf
                    sc.op(DVE, lambda e, s=s, hf=hf, ti=ti: e.scalar_tensor_tensor(
                        out=xs(slot, s)[:, hf * 512:(hf + 1) * 512], in0=stg4(ti), scalar=rm[:, 0:1],
                        in1=xs(slot, s)[:, hf * 512:(hf + 1) * 512], op0=ALU.mult, op1=ALU.add),
                        reads=[stg4_buf(ti), brm, XB[s]], writes=[XB[s]])
                yield
            yield
            norm_A(slot, c_ss2, "ss2", ch2, r2, br2)
            yield
            norm_A2(slot, r2, br2)
            yield
            norm_B(hT, "hP%d" % (i % 2), G2, SH2)
            yield

        def slab_load(G):
            i, q = divmod(G, 16)
            if i >= NT:
                return
            sl = G % 3
            rows = slice(q * 256, (q + 1) * 256)
            if i == 0:
                sc.dma(POOL, lambda e: e.dma_start(out=fc2buf[:, sl, :, :],
                                                   in_=fc2_d[rows, :].rearrange("(j p) n -> p j n", p=128)),
                       "f2p_%d" % sl, writes=[buf("f2b%d" % sl)])
                sc.dma(SP, lambda e: e.dma_start(out=fc2s_d[rows, :].rearrange("(j p) n -> p j n", p=128),
                                                 in_=fc2buf[:, sl, :, :]),
                       "f2w%d" % sl, reads=[buf("f2b%d" % sl)], writes=[buf("f2s%d" % q)])
            else:
                sc.dma(SP, lambda e: e.dma_start(out=fc2buf[:, sl, :, :],
                                                 in_=fc2s_d[rows, :].rearrange("(j p) n -> p j n", p=128)),
                       "f2_%d" % sl, reads=[buf("f2s%d" % q)], writes=[buf("f2b%d" % sl)])

        def ffn(i):
            slot = i % 3
            XB = [buf("xb%d_0" % slot), buf("xb%d_1" % slot)]
            h2T = hP[:, i % 2]
            H2T = [buf("hP%d_0" % (i % 2)), buf("hP%d_1" % (i % 2))]
            if i == 0:
                slab_load(0)
                slab_load(1)
                slab_load(2)

            def fc1(jp):
                fb = FB[jp % 2]

                def mm(e):
                    ins = None
                    for jj in range(2):
                        j = jp * 2 + jj
                        for k in range(8):
                            ins = e.matmul(ps[:, fb, jj * T:(jj + 1) * T], lhsT=fc1_sb[:, k, j * 128:(j + 1) * 128],
                                           rhs=h2T[:, k, :], start=(k == 0), stop=(k == 7))
                    return ins
                sc.op(PE, mm, reads=H2T + FC1, writes=[Bps(fb)])
                sc.op(ACT, lambda e: e.activation(out=rl[:, jp % 2, :], in_=ps[:, fb, :], func=AF.Relu),
                      reads=[Bps(fb)], writes=[buf("rl%d" % (jp % 2))])
                sc.op(POOL, lambda e: e.tensor_tensor(out=hid[:, jp % 3, :], in0=rl[:, jp % 2, :], in1=rl[:, jp % 2, :],
                                                      op=ALU.mult),
                      reads=[buf("rl%d" % (jp % 2))], writes=[buf("hid%d" % (jp % 3))])

            def fc2(jp):
                sl = (16 * i + jp) % 3

                def mm(e):
                    ins = None
                    for jj in range(2):
                        j = jp * 2 + jj
                        for s in range(2):
                            for hf in range(2):
                                ins = e.matmul(ps[:, 2 * s + hf, :],
                                               lhsT=hid[:, jp % 3, jj * T + s * 128:jj * T + (s + 1) * 128],
                                               rhs=fc2buf[:, sl, jj, hf * 512:(hf + 1) * 512],
                                               start=(j == 0), stop=(j == NJ - 1))
                    return ins
                sc.op(PE, mm, reads=[buf("hid%d" % (jp % 3)), buf("f2b%d" % sl)], writes=[Bps(b) for b in range(4)])

            for jp in range(16):
                fc1(jp)
                if jp >= 2:
                    fc2(jp - 2)
                    slab_load(16 * i + jp + 1)
                yield
            fc2(14)
            slab_load(16 * i + 17)
            fc2(15)
            slab_load(16 * i + 18)
            for s in range(2):
                for hf in range(2):
                    b_ = 2 * s + hf
                    sc.op(ACT, lambda e, b_=b_: e.activation(out=junk[:, 0:512], in_=ps[:, b_, :], func=AF.Square,
                                                             accum_out=sm[:, c_ssf + b_:c_ssf + b_ + 1]),
                          reads=[Bps(b_)], writes=[buf("junk"), buf("ssf%d" % b_)])
            for s in range(2):
                for hf in range(2):
                    b_ = 2 * s + hf
                    sc.op(DVE, lambda e, hf=hf, b_=b_: e.tensor_tensor(
                        out=stg4(b_), in0=ps[:, b_, :], in1=gp2[:, hf * 512:(hf + 1) * 512], op=ALU.mult),
                        reads=[Bps(b_), buf("gp2"), buf("ssf%d" % b_)], writes=[stg4_buf(b_)])
            sc.op(DVE, lambda e: e.tensor_tensor(out=sm[:, c_ssfs:c_ssfs + 2],
                                                 in0=sm[:, c_ssf:c_ssf + 4].rearrange("p (s h) -> p s h", h=2)[:, :, 0],
                                                 in1=sm[:, c_ssf:c_ssf + 4].rearrange("p (s h) -> p s h", h=2)[:, :, 1],
                                                 op=ALU.add),
                  reads=[buf("ssf%d" % b_) for b_ in range(4)], writes=[buf("ssfs")])
            chf()
            for s in range(2):
                for hf in range(2):
                    b_ = 2 * s + hf
                    sc.op(DVE, lambda e, s=s, hf=hf, b_=b_: e.scalar_tensor_tensor(
                        out=xs(slot, s)[:, hf * 512:(hf + 1) * 512], in0=stg4(b_), scalar=rf[:, s:s + 1],
                        in1=xs(slot, s)[:, hf * 512:(hf + 1) * 512], op0=ALU.mult, op1=ALU.add),
                        reads=[stg4_buf(b_), brf, XB[s]], writes=[XB[s]])
            dst = out_d[i * T:(i + 1) * T, :].rearrange("(s p) d -> p s d", p=128)
            srcv = xb[:, slot, :].rearrange("p (s d) -> p s d", s=2)
            ev = sc.dma(SP, lambda e: e.dma_start(out=dst, in_=srcv), "xs%d" % slot, reads=XB)
            stores.append(ev)
            yield

        stores = []
        if NT > 1:
            x_load(1)
        for _ in mixer(0):
            pass
        for i in range(NT):
            gm = mixer(i + 1) if i + 1 < NT else None
            step = 0
            for _ in ffn(i):
                step += 1
                if step == 4 and i + 2 < NT:
                    x_load(i + 2)
                if gm is not None and step >= 2:
                    try:
                        next(gm)
                    except StopIteration:
                        gm = None
            if gm is not None:
                for _ in gm:
                    pass
        sc.wait(SP, stores)

        all_keys = set()
        for e_ in ENGS:
            for (deps, fn, key, amt) in sc.ops[e_]:
                if key is not None:
                    all_keys.add(key)
        with contextlib.ExitStack() as stack:
            for k_ in sorted(all_keys):
                sems[k_] = stack.enter_context(nc.semaphore("s_" + k_))
            block = stack.enter_context(nc.Block())

            def run(eng_name, eng):
                waited = {}
                for (deps, fn, key, amt) in sc.ops[eng_name]:
                    for (k_, v_) in deps:
                        if waited.get(k_, 0) >= v_:
                            continue
                        eng.wait_ge(sems[k_], v_)
                        waited[k_] = v_
                    if fn is None:
                        continue
                    ins = fn(eng)
                    ins.then_inc(sems[key], amt)

            @block.tensor
            def _(e):
                run(PE, e)

            @block.scalar
            def _(e):
                run(ACT, e)

            @block.vector
            def _(e):
                run(DVE, e)

            @block.gpsimd
            def _(e):
                run(POOL, e)

            @block.sync
            def _(e):
                run(SP, e)
    return nc


def _host_inputs(inputs, NT=16):
    f = lambda a: np.ascontiguousarray(np.asarray(a, dtype=np.float32))
    S = NT * T
    x = f(inputs["x"])
    c = f(inputs["c"])
    pmaj = lambda w: np.ascontiguousarray(f(w).reshape(8, 128, -1).transpose(1, 0, 2).reshape(128, -1))
    col = lambda v, n: np.ascontiguousarray(f(v).reshape(n, 128).T)
    rc = np.zeros((4, 16), np.float32)
    for g in range(4):
        w = 2 << g
        for t in range(16):
            rc[g, t] = 1.0 / min(t + 1, w)
    shared = {
        "n1pb": np.ascontiguousarray(np.broadcast_to(f(inputs["norm1_post"])[None, :], (128, D))),
        "n2pb": np.ascontiguousarray(np.broadcast_to(f(inputs["norm2_post"])[None, :], (128, D))),
        "bspb": np.ascontiguousarray(np.broadcast_to(f(inputs["b_spatial"]).reshape(1, 512), (128, 512))),
        "mask": np.ascontiguousarray(np.tile(np.triu(np.ones((128, 128), np.float32)), (1, 4))),
        "wspT": np.ascontiguousarray(f(inputs["w_spatial"]).transpose(2, 0, 1).reshape(128, 512)),
        "ident": np.eye(128, dtype=np.float32),
        "wpool": np.ascontiguousarray(f(inputs["w_pool"]).transpose(1, 0, 2).reshape(128, 512)),
        "bada": f(inputs["b_ada"]).reshape(1, 6144),
        "wada": np.ascontiguousarray(f(inputs["w_ada"]).reshape(8, 128, 24, 256).transpose(1, 2, 0, 3).reshape(128, -1)),
        "w_in": pmaj(inputs["w_in"]),
        "w_out": pmaj(inputs["w_out"]),
        "w_fc1": pmaj(inputs["w_fc1"]),
        "w_fc2": f(inputs["w_fc2"]),
    }
    in_maps = []
    for b in range(x.shape[0]):
        cvec = np.zeros((128, NCV), np.float32)
        cvec[:, C_C:C_C + 8] = col(c[b], 8)
        cvec[:, C_N1:C_N1 + 8] = col(inputs["norm1_pre"], 8)
        cvec[:, C_N2:C_N2 + 8] = col(inputs["norm2_pre"], 8)
        cvec[:, C_LG:C_LG + 4] = col(inputs["ln_v_gain"], 4)
        cvec[:, C_LB:C_LB + 4] = col(inputs["ln_v_bias"], 4)
        cvec[:, C_BP:C_BP + 4] = col(np.asarray(inputs["b_pool"]).reshape(-1), 4)
        cvec[:, C_PS:C_PS + 4] = col(inputs["pool_scale"], 4)
        cvec[:, C_RC:C_RC + 64] = rc.reshape(1, 64)
        m = dict(shared)
        m["x"] = np.ascontiguousarray(x[b, :S])
        m["cvec"] = cvec
        in_maps.append(m)
    return in_maps


def kernel(**inputs):
    in_maps = _host_inputs(inputs, 16)
    nc = build(16)
    res = run_bass_kernel_spmd(nc, in_maps, core_ids=list(range(len(in_maps))))
    return np.stack([np.asarray(r["out"], dtype=np.float32) for r in res.results], axis=0)
```
